# Optimizing a Trainium2 kernel written in Bass

```python
import math
import jax
import jax.numpy as jnp
from jax import lax
import numpy as np


D_MODEL = 1024
BATCH = 4
SEQ = 8192
DEPTH = 4

HEAD_DIM = 64
A_HEADS = 4
A_WIDTH = A_HEADS * HEAD_DIM
DILATED_PATTERNS = ((128, 1), (512, 4), (2048, 16))
ALIBI_MAX_EXP = 8.0
B_HEADS = 6
B_Q_RANK = 256
B_KV_RANK = 128
B_NOPE = 64
B_ROPE = 32
B_V = 64
B_QK = B_NOPE + B_ROPE
B_WIDTH = B_HEADS * B_V
C_HEADS = 6
C_KV_HEADS = 2
C_WIDTH = C_HEADS * HEAD_DIM
C_KV_WIDTH = C_KV_HEADS * HEAD_DIM
MIX_WIDTH = A_WIDTH + B_WIDTH + C_WIDTH

GRID_W = 64
ROPE_THETA = 10000.0
Q_BLOCK = 128
NORM_EPS = 1e-6
MASK_VALUE = -1e30

IN_SPLITS = (A_WIDTH, A_WIDTH, A_WIDTH, A_WIDTH,
             B_Q_RANK, B_KV_RANK, B_ROPE, B_WIDTH,
             C_WIDTH, C_KV_WIDTH, C_KV_WIDTH, C_WIDTH)
IN_COLS = sum(IN_SPLITS)

kernel_name = 'hybrid_dilated_mla_axial_gqa_encoder'


def rms_norm(x, g):
    x32 = x.astype(jnp.float32)
    y = x32 * lax.rsqrt(jnp.mean(x32 * x32, axis=-1, keepdims=True) + NORM_EPS)
    return (y * g.astype(jnp.float32)).astype(x.dtype)


def rope(x, pos):
    dr = x.shape[-1]
    half = dr // 2
    freqs = ROPE_THETA ** (-2.0 * jnp.arange(half, dtype=jnp.float32) / dr)
    ang = pos.astype(jnp.float32)[:, None] * freqs[None, :]
    cos = jnp.cos(ang)[:, None, :]
    sin = jnp.sin(ang)[:, None, :]
    x1 = x[..., :half].astype(jnp.float32)
    x2 = x[..., half:].astype(jnp.float32)
    return jnp.concatenate([x1 * cos - x2 * sin, x2 * cos + x1 * sin], axis=-1).astype(x.dtype)


def split_cols(proj):
    parts = []
    off = 0
    for w in IN_SPLITS:
        parts.append(proj[..., off:off + w])
        off += w
    return parts


def dense_attention(q, k, v):
    B, S, Hq, dk = q.shape
    Hkv = k.shape[2]
    G = Hq // Hkv
    dv = v.shape[-1]
    nblk = S // Q_BLOCK
    scale = dk ** -0.5
    qb = q.reshape(B, nblk, Q_BLOCK, Hkv, G, dk).transpose(1, 0, 2, 3, 4, 5)

    def one_block(qblk):
        s = jnp.einsum('bqkgd,bskd->bkgqs', qblk, k, preferred_element_type=jnp.float32) * scale
        p = jax.nn.softmax(s, axis=-1).astype(v.dtype)
        return jnp.einsum('bkgqs,bskd->bqkgd', p, v)

    out = lax.map(one_block, qb)
    return out.transpose(1, 0, 2, 3, 4, 5).reshape(B, S, Hq * dv)


def dilated_attention(q, k, v, slopes):
    B, S, H, hd = q.shape
    scale = hd ** -0.5
    outs = []
    lses = []
    for window, dil in DILATED_PATTERNS:
        half = window // (2 * dil)
        L = S // dil
        nb = -(-L // half)
        pad = nb * half - L

        def by_residue(a):
            return a.reshape(B, L, dil, H, hd)

        qd = jnp.pad(by_residue(q), ((0, 0), (0, pad), (0, 0), (0, 0), (0, 0)))
        qd = qd.reshape(B, nb, half, dil, H, hd)
        kp = jnp.pad(by_residue(k), ((0, 0), (half, pad + half), (0, 0), (0, 0), (0, 0)))
        vp = jnp.pad(by_residue(v), ((0, 0), (half, pad + half), (0, 0), (0, 0), (0, 0)))
        kp = kp.reshape(B, nb + 2, half, dil, H, hd)
        vp = vp.reshape(B, nb + 2, half, dil, H, hd)
        kw = jnp.concatenate([kp[:, :-2], kp[:, 1:-1], kp[:, 2:]], axis=2)
        vw = jnp.concatenate([vp[:, :-2], vp[:, 1:-1], vp[:, 2:]], axis=2)

        s = jnp.einsum('bnidhe,bnjdhe->bndhij', qd, kw, preferred_element_type=jnp.float32) * scale
        qi = jnp.arange(half)
        kj = jnp.arange(3 * half) - half
        delta = kj[None, :] - qi[:, None]
        key_pos = jnp.arange(nb)[:, None] * half + kj[None, :]
        valid = (jnp.abs(delta) <= half)[None] & ((key_pos >= 0) & (key_pos < L))[:, None, :]
        dist = (jnp.abs(delta) * dil).astype(jnp.float32)
        bias = -slopes[:, None, None] * dist[None]
        s = s + bias[None, None, None]
        s = jnp.where(valid[None, :, None, None], s, MASK_VALUE)
        m = jnp.max(s, axis=-1, keepdims=True)
        p = jnp.exp(s - m)
        den = jnp.sum(p, axis=-1, keepdims=True)
        o = jnp.einsum('bndhij,bnjdhe->bnidhe', (p / den).astype(v.dtype), vw)
        lse = (m + jnp.log(den))[..., 0]
        o = o.reshape(B, nb * half, dil, H, hd)[:, :L].reshape(B, S, H, hd)
        lse = lse.transpose(0, 1, 4, 2, 3).reshape(B, nb * half, dil, H)[:, :L].reshape(B, S, H)
        outs.append(o.astype(jnp.float32))
        lses.append(lse)
    w = jax.nn.softmax(jnp.stack(lses, axis=0), axis=0)
    out = jnp.sum(w[..., None] * jnp.stack(outs, axis=0), axis=0)
    return out.astype(q.dtype)


def setup_inputs(seed: int = 0) -> dict:
    key = jax.random.key(seed)
    ks = jax.random.split(key, 16)

    def nrm(k, shape, scale):
        return jax.random.normal(k, shape, jnp.float32) * scale

    def gain(k, shape):
        return 1.0 + 0.02 * jax.random.normal(k, shape, jnp.float32)

    return {
        'x': jax.random.normal(ks[0], (BATCH, SEQ, D_MODEL), jnp.float32),
        'norm_g': gain(ks[1], (DEPTH, D_MODEL)),
        'w_in': nrm(ks[2], (DEPTH, D_MODEL, IN_COLS), D_MODEL ** -0.5),
        'a_q_norm_g': gain(ks[3], (DEPTH, HEAD_DIM)),
        'a_k_norm_g': gain(ks[4], (DEPTH, HEAD_DIM)),
        'b_q_lat_norm_g': gain(ks[5], (DEPTH, B_Q_RANK)),
        'b_kv_lat_norm_g': gain(ks[6], (DEPTH, B_KV_RANK)),
        'w_b_q_up': nrm(ks[7], (DEPTH, B_Q_RANK, B_HEADS * B_QK), B_Q_RANK ** -0.5),
        'w_b_kv_up': nrm(ks[8], (DEPTH, B_KV_RANK, B_HEADS * (B_NOPE + B_V)), B_KV_RANK ** -0.5),
        'b_q_norm_g': gain(ks[9], (DEPTH, B_QK)),
        'b_k_norm_g': gain(ks[10], (DEPTH, B_QK)),
        'c_q_norm_g': gain(ks[11], (DEPTH, HEAD_DIM)),
        'c_k_norm_g': gain(ks[12], (DEPTH, HEAD_DIM)),
        'w_out': nrm(ks[13], (DEPTH, MIX_WIDTH, D_MODEL), MIX_WIDTH ** -0.5),
    }


def reference(x, norm_g, w_in, a_q_norm_g, a_k_norm_g, b_q_lat_norm_g, b_kv_lat_norm_g,
              w_b_q_up, w_b_kv_up, b_q_norm_g, b_k_norm_g, c_q_norm_g, c_k_norm_g, w_out):
    B, S, _ = x.shape
    t = jnp.arange(S)
    n_rows = S // GRID_W
    row = jnp.repeat(jnp.arange(n_rows), GRID_W)
    col = jnp.tile(jnp.arange(GRID_W), n_rows)
    slopes = 2.0 ** (-ALIBI_MAX_EXP * jnp.arange(1, A_HEADS + 1, dtype=jnp.float32) / A_HEADS)
    half_c = HEAD_DIM // 2

    for l in range(DEPTH):
        h = rms_norm(x, norm_g[l])
        proj = h @ w_in[l]
        (aq, ak, av, ag, bq_lat, bkv_lat, bk_pe, bg, cq, ck, cv, cg) = split_cols(proj)

        qa = rms_norm(aq.reshape(B, S, A_HEADS, HEAD_DIM), a_q_norm_g[l])
        ka = rms_norm(ak.reshape(B, S, A_HEADS, HEAD_DIM), a_k_norm_g[l])
        va = av.reshape(B, S, A_HEADS, HEAD_DIM)
        ya = dilated_attention(qa, ka, va, slopes).reshape(B, S, A_WIDTH) * jax.nn.silu(ag)

        qb = (rms_norm(bq_lat, b_q_lat_norm_g[l]) @ w_b_q_up[l]).reshape(B, S, B_HEADS, B_QK)
        kvb = (rms_norm(bkv_lat, b_kv_lat_norm_g[l]) @ w_b_kv_up[l]).reshape(B, S, B_HEADS, B_NOPE + B_V)
        k_nope = kvb[..., :B_NOPE]
        vb = kvb[..., B_NOPE:]
        k_pe = jnp.broadcast_to(bk_pe[:, :, None, :], (B, S, B_HEADS, B_ROPE))
        qb = rms_norm(qb, b_q_norm_g[l])
        kb = rms_norm(jnp.concatenate([k_nope, k_pe], axis=-1), b_k_norm_g[l])
        qb = jnp.concatenate([qb[..., :B_NOPE], rope(qb[..., B_NOPE:], t)], axis=-1)
        kb = jnp.concatenate([kb[..., :B_NOPE], rope(kb[..., B_NOPE:], t)], axis=-1)
        yb = dense_attention(qb, kb, vb) * jax.nn.silu(bg)

        qc = rms_norm(cq.reshape(B, S, C_HEADS, HEAD_DIM), c_q_norm_g[l])
        kc = rms_norm(ck.reshape(B, S, C_KV_HEADS, HEAD_DIM), c_k_norm_g[l])
        vc = cv.reshape(B, S, C_KV_HEADS, HEAD_DIM)
        qc = jnp.concatenate([rope(qc[..., :half_c], row), rope(qc[..., half_c:], col)], axis=-1)
        kc = jnp.concatenate([rope(kc[..., :half_c], row), rope(kc[..., half_c:], col)], axis=-1)
        yc = dense_attention(qc, kc, vc) * jax.nn.silu(cg)

        y = jnp.concatenate([ya, yb, yc], axis=-1)
        x = x + y @ w_out[l]
    return x
```

```python
import numpy as np
from contextlib import ExitStack
import ml_dtypes
import concourse.bass as bass
import concourse.mybir as mybir
from concourse.bass_utils import run_bass_kernel_spmd

F32 = mybir.dt.float32
BF16 = mybir.dt.bfloat16
AF = mybir.ActivationFunctionType
ALU = mybir.AluOpType

S = 8192
D = 1024
HALF = 4096
TB = 512
NBLK = S // TB
NOWN = HALF // TB
DEPTH = 4
IN_COLS = 2848
OFF = dict(aq=0, ak=256, av=512, ag=768, bql=1024, bkv=1280, bkpe=1408, bg=1440,
           cq=1824, ck=2208, cv=2336, cg=2464)
EPS = 1e-6
GW = 2944
GOFF = 1408
NGV = 17


class Res:
    __slots__ = ("name", "w", "r", "dsem", "dcnt")

    def __init__(self, name):
        self.name = name
        self.w = None
        self.r = []
        self.dsem = None
        self.dcnt = 0


class Prog:
    COMPUTE = ("pe", "act", "dve", "pool")
    ALL = ("pe", "act", "dve", "pool", "sp")

    def __init__(self, nc, es):
        self.nc = nc
        self.es = es
        self.ops = {e: [] for e in self.ALL}
        self.tsem = {}
        self.tick = {}
        self.waited = {e: {} for e in self.ALL}
        self.nsem = 0
        self.epoch = 0
        self.allres = {}
        self.oldticks = []
        self.new_epoch()

    def _newsem(self, name):
        self.nsem += 1
        return self.es.enter_context(self.nc.semaphore(name))

    def new_epoch(self):
        self.epoch += 1
        for e in self.COMPUTE:
            if e in self.tsem and self.tick[e] > 0:
                self.oldticks.append((self.tsem[e], self.tick[e], e))
            self.tsem[e] = self._newsem(f"t_{e}_{self.epoch}")
            self.tick[e] = 0

    def res(self, name):
        if name not in self.allres:
            self.allres[name] = Res(name)
        return self.allres[name]

    def _need(self, eng, ev, waits):
        if ev is None:
            return
        sem, val, src = ev
        if src == "pe" and eng == "pe":
            return
        key = id(sem)
        if self.waited[eng].get(key, 0) >= val:
            return
        self.waited[eng][key] = val
        waits.append((sem, val))

    def _deps(self, eng, reads, writes):
        waits = []
        for r in reads:
            self._need(eng, r.w, waits)
        for w in writes:
            ev = w.w
            if ev is not None and ev[2] != eng and not (ev[2] == "dma" and eng == "sp"):
                self._need(eng, ev, waits)
            for rv in w.r:
                if rv[2] == eng:
                    continue
                self._need(eng, rv, waits)
        best = {}
        for sem, val in waits:
            k = id(sem)
            if k not in best or best[k][1] < val:
                best[k] = (sem, val)
        return list(best.values())

    def op(self, eng, meth, reads=(), writes=(), **kw):
        waits = self._deps(eng, reads, writes)
        self.tick[eng] += 1
        n = self.tick[eng]
        sem = self.tsem[eng]
        ev = (sem, n, eng)

        def emit(e, waits=waits, meth=meth, kw=kw, sem=sem):
            for s, v in waits:
                e.wait_ge(s, v)
            getattr(e, meth)(**kw).then_inc(sem, 1)
        self.ops[eng].append(emit)
        for r in reads:
            r.r.append(ev)
        for w in writes:
            w.w = ev
            w.r = []
        return ev

    def dma(self, q, out_ap, in_ap, reads=(), writes=()):
        waits = self._deps(q, reads, writes)
        owner = writes[0] if writes else reads[0]
        if owner.dsem is None:
            owner.dsem = self._newsem("d_" + owner.name)
        owner.dcnt += 16
        ev = (owner.dsem, owner.dcnt, "dma")

        def emit(e, waits=waits, sem=owner.dsem):
            for s, v in waits:
                e.wait_ge(s, v)
            e.dma_start(out=out_ap, in_=in_ap).then_inc(sem, 16)
        self.ops[q].append(emit)
        for r in reads:
            r.r.append(ev)
        for w in writes:
            w.w = ev
            w.r = []
        return ev

    def barrier(self):
        evs = list(self.oldticks)
        for e in self.COMPUTE:
            if self.tick[e] > 0:
                evs.append((self.tsem[e], self.tick[e], "x"))
        for r in self.allres.values():
            if r.dsem is not None and r.dcnt > 0:
                evs.append((r.dsem, r.dcnt, "dma"))
        for eng in self.ALL:
            waits = []
            for sem, val, _ in evs:
                self._need(eng, (sem, val, "x"), waits)

            def emit(e, waits=waits):
                for s, v in waits:
                    e.wait_ge(s, v)
            self.ops[eng].append(emit)
        for r in self.allres.values():
            r.w = None
            r.r = []

    def emit_all(self):
        nc = self.nc
        ops = self.ops
        with nc.Block() as block:
            @block.tensor
            def _(e):
                for f in ops["pe"]:
                    f(e)

            @block.scalar
            def _(e):
                for f in ops["act"]:
                    f(e)

            @block.vector
            def _(e):
                for f in ops["dve"]:
                    f(e)

            @block.gpsimd
            def _(e):
                for f in ops["pool"]:
                    f(e)

            @block.sync
            def _(e):
                for f in ops["sp"]:
                    f(e)


class Bufs:
    def __init__(self, nc, P, es, prefix):
        self.nc, self.P, self.es, self.prefix = nc, P, es, prefix
        self.b = {}
        self.i = {}

    def mk(self, name, shape, dtype, n=1, psum=False):
        lst = []
        for k in range(n):
            nm = f"{self.prefix}_{name}{k}"
            if psum:
                t = self.es.enter_context(self.nc.psum_tensor(nm, shape, dtype))
            else:
                t = self.es.enter_context(self.nc.sbuf_tensor(nm, shape, dtype))
            lst.append((t, self.P.res(nm)))
        self.b[name] = lst
        self.i[name] = 0
        return lst[0]

    def nxt(self, name):
        lst = self.b[name]
        k = self.i[name]
        self.i[name] = (k + 1) % len(lst)
        return lst[k]

    def get(self, name, k=0):
        return self.b[name][k]


def emit_layer(nc, P, lid, x_in, x_out, wts, cst, scr):
    w_in, w_out, wq_up, wkv_up, gvd = wts["w_in"], wts["w_out"], wts["wq_up"], wts["wkv_up"], wts["gv"]
    r_xin, r_xout = scr["r_xin"], scr["r_xout"]
    KT_A, KT_B, KT_C = scr["KT_A"], scr["KT_B"], scr["KT_C"]
    QT_A, QT_B, QT_C = scr["QT_A"], scr["QT_B"], scr["QT_C"]
    V_all, GATE, YT = scr["V"], scr["GATE"], scr["YT"]
    r_KT, r_QT, r_V, r_GATE, r_YT = (P.res("s_KT"), P.res("s_QT"), P.res("s_V"),
                                     P.res("s_GATE"), P.res("s_YT"))
    r_w = P.res("w_dram")

    def ACT(reads, writes, **kw):
        P.op("act", "activation", reads, writes, **kw)

    def MM(reads, writes, **kw):
        P.op("pe", "matmul", reads, writes, **kw)

    with ExitStack() as esL:
        BL = Bufs(nc, P, esL, "L")
        Wb, r_Wb = BL.mk("Wb", [128, 8, IN_COLS], BF16)
        Wqu, r_Wqu = BL.mk("Wqu", [128, 2, 576], BF16)
        Wkk, r_Wkk = BL.mk("Wkk", [128, 6, 64], BF16)
        Wkv, r_Wkv = BL.mk("Wkv", [128, 384], BF16)
        gv, r_gv = BL.mk("gv", [128, NGV], F32)
        cm, r_cm = BL.mk("cm", [128, 4, 128], BF16)
        ident, r_id = BL.mk("ident", [128, 128], F32)
        epsb, r_eps = BL.mk("epsb", [128, 1], F32)
        permB, permC, ones2, onesall = (cm[:, 0, :], cm[:, 1, :], cm[:, 2, :], cm[:, 3, :])

        P.dma("sp", gv[:], gvd, reads=[r_w], writes=[r_gv])
        P.dma("sp", cm[:], cst["cm"], reads=[r_w], writes=[r_cm])
        P.dma("sp", ident[:], cst["ident"], reads=[r_w], writes=[r_id])
        P.op("dve", "memset", [], [r_eps], ap=epsb[:], constant=EPS)

        with ExitStack() as esW:
            BW = Bufs(nc, P, esW, "W")
            BW.mk("wst", [128, IN_COLS], F32, 2)
            BW.mk("wq", [128, 2, 576], F32)
            BW.mk("wk", [128, 768], F32)
            win_v = w_in.rearrange("(c p) n -> p c n", p=128)
            for c in range(8):
                st, r_st = BW.nxt("wst")
                P.dma("sp", st[:], win_v[:, c, :], reads=[r_w], writes=[r_st])
                eng = ("pool", "dve", "act")[c % 3]
                if eng == "act":
                    ACT([r_st], [r_Wb], out=Wb[:, c, :], in_=st[:], func=AF.Copy)
                else:
                    P.op(eng, "tensor_copy", [r_st], [r_Wb], out=Wb[:, c, :], in_=st[:])
            wq, r_wq = BW.get("wq")
            P.dma("sp", wq[:], wq_up.rearrange("(c p) n -> p c n", p=128), reads=[r_w], writes=[r_wq])
            P.op("dve", "tensor_copy", [r_wq], [r_Wqu], out=Wqu[:], in_=wq[:])
            wk, r_wk = BW.get("wk")
            P.dma("sp", wk[:], wkv_up, reads=[r_w], writes=[r_wk])
            wk3 = wk[:].rearrange("p (h t) -> p h t", t=128)
            P.op("dve", "tensor_copy", [r_wk], [r_Wkk], out=Wkk[:], in_=wk3[:, :, 0:64])
            P.op("dve", "tensor_copy", [r_wk], [r_Wkv], out=Wkv[:].rearrange("p (h d) -> p h d", d=64),
                 in_=wk3[:, :, 64:128])
            P.barrier()

        with ExitStack() as esP:
            B = Bufs(nc, P, esP, "P")
            B.mk("bank", [128, 512], F32, 8, psum=True)
            B.mk("xt", [128, 4, D], F32, 2)
            B.mk("junk", [128, D], BF16)
            B.mk("ss", [128, 4], F32, 2)
            B.mk("l4", [128, 4], F32, 2)
            B.mk("r4", [128, 4], F32, 2)
            B.mk("hT", [128, 8, TB], BF16, 2)
            B.mk("tab", [128, 4, TB], F32, 2)
            B.mk("sq", [128, TB], BF16, 3)
            B.mk("ln", [128, TB], F32, 2)
            B.mk("rs", [128, TB], F32, 2)
            B.mk("qn", [128, TB], BF16, 3)
            B.mk("t1", [128, TB], F32, 2)
            B.mk("t2", [128, TB], F32, 2)
            B.mk("ob", [128, TB], BF16, 4)
            B.mk("qln", [128, 2, TB], BF16, 2)
            B.mk("kvn", [128, TB], BF16, 2)
            B.mk("ge", [128, TB], F32, 2)
            B.mk("gd", [128, TB], F32, 2)
            B.mk("gr", [128, TB], F32, 2)
            B.mk("go", [128, TB], F32, 3)
            B.mk("vst", [128, 4, 768], BF16, 2)

            def bank():
                return B.nxt("bank")

            def proj_group(hT, r_hT, col0, M, out_rows=0, bk=None):
                if bk is None:
                    bk = bank()
                bT, bR = bk
                for c in range(8):
                    MM([r_Wb, r_hT], [bR], out=bT[out_rows:out_rows + M, :], lhsT=Wb[:, c, col0:col0 + M],
                       rhs=hT[:, c, :], start=(c == 0), stop=(c == 7))
                return bk

            def square(bk, M):
                bT, bR = bk
                sq, r_sq = B.nxt("sq")
                ACT([bR], [r_sq], out=sq[0:M, :], in_=bT[0:M, :], func=AF.Square)
                return sq, r_sq

            def rstd_from(sqs, M, ones_ap, dk):
                pnT, pnR = bank()
                for i, (sq, r_sq) in enumerate(sqs):
                    MM([r_sq, r_cm], [pnR], out=pnT[0:M, :], lhsT=ones_ap, rhs=sq[0:M, :],
                       start=(i == 0), stop=(i == len(sqs) - 1))
                ln, r_ln = B.nxt("ln")
                ACT([pnR, r_eps], [r_ln], out=ln[0:M, :], in_=pnT[0:M, :], func=AF.Ln,
                    scale=1.0 / dk, bias=epsb[0:M, :])
                rs, r_rs = B.nxt("rs")
                ACT([r_ln], [r_rs], out=rs[0:M, :], in_=ln[0:M, :], func=AF.Exp, scale=-0.5)
                return rs, r_rs

            def headnorm(bk, M, dk, gcol, ones_ap, rope, tab, r_tab, stores, r_dst):
                bT, bR = bk
                rs, r_rs = rstd_from([square(bk, M)], M, ones_ap, dk)
                qn, r_qn = B.nxt("qn")
                P.op("dve", "scalar_tensor_tensor", [bR, r_rs, r_gv], [r_qn], out=qn[0:M, :], in0=bT[0:M, :],
                     scalar=gv[0:M, gcol:gcol + 1], in1=rs[0:M, :], op0=ALU.mult, op1=ALU.mult)
                if rope is None:
                    fin, r_fin = qn, r_qn
                else:
                    perm = permB[0:M, 0:M] if rope == "B" else permC
                    ci, si = (0, 1) if rope == "B" else (2, 3)
                    prT, prR = bank()
                    MM([r_qn, r_cm], [prR], out=prT[0:M, :], lhsT=perm, rhs=qn[0:M, :], start=True, stop=True)
                    t1, r_t1 = B.nxt("t1")
                    P.op("pool", "tensor_tensor", [r_qn, r_tab], [r_t1], out=t1[0:M, :], in0=qn[0:M, :],
                         in1=tab[0:M, ci, :], op=ALU.mult)
                    t2, r_t2 = B.nxt("t2")
                    P.op("dve", "tensor_tensor", [prR, r_tab], [r_t2], out=t2[0:M, :], in0=prT[0:M, :],
                         in1=tab[0:M, si, :], op=ALU.mult)
                    fin, r_fin = B.nxt("ob")
                    P.op("pool", "tensor_tensor", [r_t1, r_t2], [r_fin], out=fin[0:M, :], in0=t1[0:M, :],
                         in1=t2[0:M, :], op=ALU.add)
                for (dst, lo, hi) in stores:
                    P.dma("sp", dst, fin[lo:hi, :], reads=[r_fin], writes=[r_dst])

            x_v = x_in.rearrange("(n j p) d -> n p j d", j=4, p=128)
            V_v = V_all.rearrange("(n j p) c -> n p j c", j=4, p=128)

            def load_block(blk):
                xt, r_xt = B.nxt("xt")
                P.dma("sp", xt[:], x_v[blk], reads=[r_xin], writes=[r_xt])
                tab, r_tab = B.nxt("tab")
                t0 = blk * TB
                for i, nm in enumerate(("cosB", "sinB", "cosC", "sinC")):
                    P.dma("sp", tab[:, i, :], cst[nm][:, t0:t0 + TB], reads=[r_w], writes=[r_tab])
                return xt, r_xt, tab, r_tab

            nxt_loaded = load_block(0)
            for blk in range(NBLK):
                own = blk < NOWN
                t0 = blk * TB
                xt, r_xt, tab, r_tab = nxt_loaded
                if blk + 1 < NBLK:
                    nxt_loaded = load_block(blk + 1)
                junk, r_junk = B.get("junk")
                ss, r_ss = B.nxt("ss")
                for j in range(4):
                    ACT([r_xt], [r_junk, r_ss], out=junk[:], in_=xt[:, j, :], func=AF.Square, accum_out=ss[:, j:j + 1])
                l4, r_l4 = B.nxt("l4")
                ACT([r_ss, r_eps], [r_l4], out=l4[:], in_=ss[:], func=AF.Ln, scale=1.0 / D, bias=epsb[:])
                r4, r_r4 = B.nxt("r4")
                ACT([r_l4], [r_r4], out=r4[:], in_=l4[:], func=AF.Exp, scale=-0.5)
                for j in range(4):
                    eng = "dve" if j % 2 == 0 else "pool"
                    P.op(eng, "tensor_scalar", [r_r4, r_xt], [r_xt], out=xt[:, j, :], in0=xt[:, j, :],
                         scalar1=r4[:, j:j + 1], scalar2=None, op0=ALU.mult)
                hT, r_hT = B.nxt("hT")
                for c in range(8):
                    ptT, ptR = bank()
                    for j in range(4):
                        P.op("pe", "transpose", [r_xt, r_id], [ptR], out=ptT[:, j * 128:(j + 1) * 128],
                             in_=xt[:, j, c * 128:(c + 1) * 128], identity=ident[:])
                    P.op("dve", "tensor_scalar", [ptR, r_gv], [r_hT], out=hT[:, c, :], in0=ptT[:],
                         scalar1=gv[:, c:c + 1], scalar2=None, op0=ALU.mult)

                bk = proj_group(hT, r_hT, OFF["bkv"], 128)
                rs, r_rs = rstd_from([square(bk, 128)], 128, onesall, 128.0)
                kvn, r_kvn = B.nxt("kvn")
                P.op("dve", "scalar_tensor_tensor", [bk[1], r_rs, r_gv], [r_kvn], out=kvn[:], in0=bk[0][:],
                     scalar=gv[:, 12:13], in1=rs[:], op0=ALU.mult, op1=ALU.mult)
                vst, r_vst = B.nxt("vst")
                for j in range(4):
                    pvT, pvR = bank()
                    for (cols, n, o) in ((OFF["av"], 256, 0), (OFF["cv"], 128, 256)):
                        for c in range(8):
                            MM([r_hT, r_Wb], [pvR], out=pvT[:, o:o + n], lhsT=hT[:, c, j * 128:(j + 1) * 128],
                               rhs=Wb[:, c, cols:cols + n], start=(c == 0), stop=(c == 7))
                    ACT([pvR], [r_vst], out=vst[:, j, 0:384], in_=pvT[:, 0:384], func=AF.Copy)
                    pv2T, pv2R = bank()
                    MM([r_kvn, r_Wkv], [pv2R], out=pv2T[:, 0:384], lhsT=kvn[:, j * 128:(j + 1) * 128],
                       rhs=Wkv[:], start=True, stop=True)
                    P.op("dve", "tensor_copy", [pv2R], [r_vst], out=vst[:, j, 384:768], in_=pv2T[:, 0:384])
                P.dma("sp", V_v[blk], vst[:], reads=[r_vst], writes=[r_V])
                for h in range(6):
                    bkh = bank()
                    MM([r_Wkk, r_kvn], [bkh[1]], out=bkh[0][0:64, :], lhsT=Wkk[:, h, :], rhs=kvn[:], start=True, stop=True)
                    proj_group(hT, r_hT, OFF["bkpe"], 32, out_rows=64, bk=bkh)
                    headnorm(bkh, 96, 96.0, 14, onesall[0:96, 0:96], "B", tab, r_tab,
                             [(KT_B[h * 96:(h + 1) * 96, t0:t0 + TB], 0, 96)], r_KT)
                for g in range(2):
                    bk = proj_group(hT, r_hT, OFF["ak"] + 128 * g, 128)
                    headnorm(bk, 128, 64.0, 9, ones2, None, tab, r_tab,
                             [(KT_A[(2 * g) * 64:(2 * g + 1) * 64, t0:t0 + TB], 0, 64),
                              (KT_A[(2 * g + 1) * 64:(2 * g + 2) * 64, t0:t0 + TB], 64, 128)], r_KT)
                bk = proj_group(hT, r_hT, OFF["ck"], 128)
                headnorm(bk, 128, 64.0, 16, ones2, "C", tab, r_tab,
                         [(KT_C[0:64, t0:t0 + TB], 0, 64), (KT_C[64:128, t0:t0 + TB], 64, 128)], r_KT)
                if not own:
                    continue
                for g in range(2):
                    bk = proj_group(hT, r_hT, OFF["aq"] + 128 * g, 128)
                    headnorm(bk, 128, 64.0, 8, ones2, None, tab, r_tab,
                             [(QT_A[(2 * g) * 64:(2 * g + 1) * 64, t0:t0 + TB], 0, 64),
                              (QT_A[(2 * g + 1) * 64:(2 * g + 2) * 64, t0:t0 + TB], 64, 128)], r_QT)
                for g in range(3):
                    bk = proj_group(hT, r_hT, OFF["cq"] + 128 * g, 128)
                    headnorm(bk, 128, 64.0, 15, ones2, "C", tab, r_tab,
                             [(QT_C[(2 * g) * 64:(2 * g + 1) * 64, t0:t0 + TB], 0, 64),
                              (QT_C[(2 * g + 1) * 64:(2 * g + 2) * 64, t0:t0 + TB], 64, 128)], r_QT)
                bq = [proj_group(hT, r_hT, OFF["bql"] + 128 * c, 128) for c in range(2)]
                rs, r_rs = rstd_from([square(bq[0], 128), square(bq[1], 128)], 128, onesall, 256.0)
                qln, r_qln = B.nxt("qln")
                for c in range(2):
                    P.op("dve", "scalar_tensor_tensor", [bq[c][1], r_rs, r_gv], [r_qln], out=qln[:, c, :],
                         in0=bq[c][0][:], scalar=gv[:, 10 + c:11 + c], in1=rs[:], op0=ALU.mult, op1=ALU.mult)
                for h in range(6):
                    bkh = bank()
                    for c in range(2):
                        MM([r_Wqu, r_qln], [bkh[1]], out=bkh[0][0:96, :], lhsT=Wqu[:, c, h * 96:(h + 1) * 96],
                           rhs=qln[:, c, :], start=(c == 0), stop=(c == 1))
                    headnorm(bkh, 96, 96.0, 13, onesall[0:96, 0:96], "B", tab, r_tab,
                             [(QT_B[h * 96:(h + 1) * 96, t0:t0 + TB], 0, 96)], r_QT)
                for (gcol, yrow, ng) in ((OFF["ag"], 0, 2), (OFF["bg"], 256, 3), (OFF["cg"], 640, 3)):
                    for g in range(ng):
                        bk = proj_group(hT, r_hT, gcol + 128 * g, 128)
                        ge, r_ge = B.nxt("ge")
                        ACT([bk[1]], [r_ge], out=ge[:], in_=bk[0][:], func=AF.Exp, scale=-1.0)
                        gd, r_gd = B.nxt("gd")
                        P.op("pool", "tensor_scalar", [r_ge], [r_gd], out=gd[:], in0=ge[:], scalar1=1.0,
                             scalar2=None, op0=ALU.add)
                        gr, r_gr = B.nxt("gr")
                        P.op("dve", "reciprocal", [r_gd], [r_gr], out=gr[:], in_=gd[:])
                        go, r_go = B.nxt("go")
                        P.op("dve", "tensor_tensor", [bk[1], r_gr], [r_go], out=go[:], in0=bk[0][:], in1=gr[:],
                             op=ALU.mult)
                        y0 = yrow + 128 * g
                        P.dma("sp", GATE[y0:y0 + 128, t0:t0 + TB], go[:], reads=[r_go], writes=[r_GATE])
            P.barrier()

        with ExitStack() as esA:
            B = Bufs(nc, P, esA, "A")
            B.mk("S", [128, 1024], F32, 2, psum=True)
            B.mk("O", [128, 512], F32, 2, psum=True)
            B.mk("kT", [128, S], BF16, 2)
            B.mk("vA", [128, 64, 128], BF16, 2)
            B.mk("qT", [128, HALF], BF16, 2)
            B.mk("Gt", [128, GW], F32, 2)
            B.mk("gt", [64, 512], F32, 2)
            B.mk("pT", [128, 1024], BF16, 3)
            B.mk("sa", [128, 1024], F32, 2)
            B.mk("rd", [128, 512], F32, 2)
            B.mk("rd0", [64, 512], F32, 2)
            B.mk("yf", [64, 512], F32, 2)
            B.mk("yb", [64, 512], BF16, 2)
            for k in range(2):
                vt, vr = B.get("vA", k)
                P.op("pool", "memset", [], [vr], ap=vt[:, :, 64:128], constant=1.0)

            jobs = []
            for h in range(4):
                jobs.append(dict(kv=("A", h), kt=KT_A[h * 64:(h + 1) * 64, :], dk=64, vcol=h * 64,
                                 q=QT_A[h * 64:(h + 1) * 64, :], scale=0.125, yrow=h * 64, isA=True, gi=h))
            for h in range(6):
                jobs.append(dict(kv=("B", h), kt=KT_B[h * 96:(h + 1) * 96, :], dk=96, vcol=384 + 64 * h,
                                 q=QT_B[h * 96:(h + 1) * 96, :], scale=96.0 ** -0.5, yrow=256 + 64 * h, isA=False))
            for h in range(6):
                kvh = h // 3
                jobs.append(dict(kv=("C", kvh), kt=KT_C[kvh * 64:(kvh + 1) * 64, :], dk=64, vcol=256 + 64 * kvh,
                                 q=QT_C[h * 64:(h + 1) * 64, :], scale=0.125, yrow=640 + 64 * h, isA=False))

            V_cv = V_all.rearrange("(c p) d -> p c d", p=128)
            state = dict(kvkey=None, kvbuf=None)

            def load_job(job):
                if job["kv"] != state["kvkey"]:
                    kT, r_kT = B.nxt("kT")
                    vA, r_vA = B.nxt("vA")
                    dk = job["dk"]
                    for q4 in range(4):
                        P.dma("sp", kT[0:dk, q4 * 2048:(q4 + 1) * 2048], job["kt"][:, q4 * 2048:(q4 + 1) * 2048],
                              reads=[r_KT], writes=[r_kT])
                    vc = job["vcol"]
                    for q8 in range(8):
                        P.dma("sp", vA[:, q8 * 8:(q8 + 1) * 8, 0:64], V_cv[:, q8 * 8:(q8 + 1) * 8, vc:vc + 64],
                              reads=[r_V], writes=[r_vA])
                    state["kvkey"] = job["kv"]
                    state["kvbuf"] = (kT, r_kT, vA, r_vA)
                job["kvbuf"] = state["kvbuf"]
                qT, r_qT = B.nxt("qT")
                P.dma("sp", qT[0:job["dk"], :], job["q"], reads=[r_QT], writes=[r_qT])
                job["qbuf"] = (qT, r_qT)
                if job["isA"]:
                    Gt, r_Gt = B.nxt("Gt")
                    P.dma("sp", Gt[:], cst["gtab"][job["gi"]], reads=[r_w], writes=[r_Gt])
                    job["gbuf"] = (Gt, r_Gt)

            units = []
            for ji, job in enumerate(jobs):
                cnt = 0
                for qb in range(NOWN):
                    q0 = qb * TB
                    if job["isA"]:
                        lo = max(0, q0 - 1024) // 128
                        hi = min(S, q0 + TB + 1024) // 128
                    else:
                        lo, hi = 0, S // 128
                    ch = list(range(lo, hi))
                    assert len(ch) % 2 == 0
                    for i in range(0, len(ch), 2):
                        units.append(dict(ji=ji, qb=qb, ca=ch[i], cb=ch[i + 1], first=(i == 0),
                                          last=(i == len(ch) - 2), idx=cnt))
                        cnt += 1

            def stage1(u):
                job = jobs[u["ji"]]
                if u["idx"] == 0 and u["ji"] == 0:
                    load_job(jobs[0])
                if u["idx"] == 2 and u["ji"] + 1 < len(jobs):
                    load_job(jobs[u["ji"] + 1])
                q0 = u["qb"] * TB
                if u["first"]:
                    gt, r_gt = B.nxt("gt")
                    P.dma("sp", gt[:], GATE[job["yrow"]:job["yrow"] + 64, q0:q0 + TB], reads=[r_GATE], writes=[r_gt])
                    job["cur"] = ((gt, r_gt), B.nxt("O"))
                u["gt"], u["O"] = job["cur"]
                kT, r_kT, vA, r_vA = job["kvbuf"]
                qT, r_qT = job["qbuf"]
                dk = job["dk"]
                sT, r_S = B.nxt("S")
                u["S"] = (sT, r_S)
                for k, cc in enumerate((u["ca"], u["cb"])):
                    MM([r_kT, r_qT], [r_S], out=sT[:, k * 512:(k + 1) * 512], lhsT=kT[0:dk, cc * 128:(cc + 1) * 128],
                       rhs=qT[0:dk, q0:q0 + TB], start=True, stop=True)

            def stage2(u):
                job = jobs[u["ji"]]
                sT, r_S = u["S"]
                pT, r_pT = B.nxt("pT")
                u["pT"] = (pT, r_pT)
                if job["isA"]:
                    Gt, r_Gt = job["gbuf"]
                    sa, r_sa = B.nxt("sa")
                    q0 = u["qb"] * TB
                    for k, cc in enumerate((u["ca"], u["cb"])):
                        c = (cc * 128 - q0) // 128
                        u0 = GOFF - 128 * c
                        P.op("dve", "scalar_tensor_tensor", [r_S, r_Gt], [r_sa], out=sa[:, k * 512:(k + 1) * 512],
                             in0=sT[:, k * 512:(k + 1) * 512], scalar=job["scale"], in1=Gt[:, u0:u0 + 512],
                             op0=ALU.mult, op1=ALU.add)
                    ACT([r_sa], [r_pT], out=pT[:], in_=sa[:], func=AF.Exp)
                else:
                    ACT([r_S], [r_pT], out=pT[:], in_=sT[:], func=AF.Exp, scale=job["scale"])

            def stage3(u):
                job = jobs[u["ji"]]
                kT, r_kT, vA, r_vA = job["kvbuf"]
                pT, r_pT = u["pT"]
                oT, r_O = u["O"]
                for k, cc in enumerate((u["ca"], u["cb"])):
                    MM([r_vA, r_pT], [r_O], out=oT[:], lhsT=vA[:, cc, :], rhs=pT[:, k * 512:(k + 1) * 512],
                       start=(u["first"] and k == 0), stop=(u["last"] and k == 1))
                if u["last"]:
                    gt, r_gt = u["gt"]
                    rd, r_rd = B.nxt("rd")
                    P.op("dve", "reciprocal", [r_O], [r_rd], out=rd[64:128, :], in_=oT[64:128, :])
                    rd0, r_rd0 = B.nxt("rd0")
                    P.op("dve", "tensor_copy", [r_rd], [r_rd0], out=rd0[:], in_=rd[64:128, :])
                    yf, r_yf = B.nxt("yf")
                    P.op("dve", "tensor_tensor", [r_O, r_rd0], [r_yf], out=yf[:], in0=oT[0:64, :], in1=rd0[:], op=ALU.mult)
                    yb, r_yb = B.nxt("yb")
                    P.op("pool", "tensor_tensor", [r_yf, r_gt], [r_yb], out=yb[:], in0=yf[:], in1=gt[:], op=ALU.mult)
                    q0 = u["qb"] * TB
                    P.dma("sp", YT[job["yrow"]:job["yrow"] + 64, q0:q0 + TB], yb[:], reads=[r_yb], writes=[r_YT])

            n = len(units)
            for i in range(n + 2):
                if i < n:
                    stage1(units[i])
                if 0 <= i - 1 < n:
                    stage2(units[i - 1])
                if 0 <= i - 2 < n:
                    stage3(units[i - 2])
            P.barrier()

        with ExitStack() as esO:
            B = Bufs(nc, P, esO, "O")
            B.mk("bank", [128, 512], F32, 8, psum=True)
            Wo, r_Wo = B.mk("Wo", [128, 8, D], BF16)
            B.mk("wost", [128, D], F32, 2)
            B.mk("yT", [128, 8, TB], BF16, 2)
            B.mk("xo", [128, 4, D], F32, 2)
            wo_v = w_out.rearrange("(c p) n -> p c n", p=128)
            for c in range(8):
                st, r_st = B.nxt("wost")
                P.dma("sp", st[:], wo_v[:, c, :], reads=[r_w], writes=[r_st])
                eng = ("pool", "dve")[c % 2]
                P.op(eng, "tensor_copy", [r_st], [r_Wo], out=Wo[:, c, :], in_=st[:])
            YT_v = YT.rearrange("(c p) t -> p c t", p=128)
            x_v = x_in.rearrange("(n j p) d -> n p j d", j=4, p=128)
            xo_v = x_out.rearrange("(n j p) d -> n p j d", j=4, p=128)
            for blk in range(NOWN):
                q0 = blk * TB
                yT, r_yT = B.nxt("yT")
                P.dma("sp", yT[:], YT_v[:, :, q0:q0 + TB], reads=[r_YT], writes=[r_yT])
                xo, r_xo = B.nxt("xo")
                P.dma("sp", xo[:], x_v[blk], reads=[r_xin], writes=[r_xo])
                for j in range(4):
                    for hh in range(2):
                        bT, bR = B.nxt("bank")
                        for c in range(8):
                            MM([r_yT, r_Wo], [bR], out=bT[:], lhsT=yT[:, c, j * 128:(j + 1) * 128],
                               rhs=Wo[:, c, hh * 512:(hh + 1) * 512], start=(c == 0), stop=(c == 7))
                        P.op("dve", "tensor_tensor", [bR, r_xo], [r_xo], out=xo[:, j, hh * 512:(hh + 1) * 512],
                             in0=bT[:], in1=xo[:, j, hh * 512:(hh + 1) * 512], op=ALU.add)
                P.dma("sp", xo_v[blk], xo[:], reads=[r_xo], writes=[r_xout])
            P.barrier()


def build_program(dbg=False):
    nc = bass.Bass("TRN2", target_bir_lowering=False)
    x = nc.dram_tensor("x", [S, D], F32, kind="ExternalInput").ap()
    xo = nc.dram_tensor("xo", [HALF, D], F32, kind="ExternalOutput").ap()
    wts = dict(
        w_in=nc.dram_tensor("w_in", [D, IN_COLS], F32, kind="ExternalInput").ap(),
        w_out=nc.dram_tensor("w_out", [D, D], F32, kind="ExternalInput").ap(),
        wq_up=nc.dram_tensor("wq_up", [256, 576], F32, kind="ExternalInput").ap(),
        wkv_up=nc.dram_tensor("wkv_up", [128, 768], F32, kind="ExternalInput").ap(),
        gv=nc.dram_tensor("gv", [128, NGV], F32, kind="ExternalInput").ap(),
    )
    cst = dict(
        cm=nc.dram_tensor("cm", [128, 4, 128], BF16, kind="ExternalInput").ap(),
        ident=nc.dram_tensor("ident", [128, 128], F32, kind="ExternalInput").ap(),
        gtab=nc.dram_tensor("gtab", [4, 128, GW], F32, kind="ExternalInput").ap(),
    )
    for nm in ("cosB", "sinB", "cosC", "sinC"):
        cst[nm] = nc.dram_tensor(nm, [128, S], F32, kind="ExternalInput").ap()
    kind = "ExternalOutput" if dbg else "Internal"
    scr = dict(
        KT_A=nc.dram_tensor("KT_A", [256, S], BF16, kind=kind).ap(),
        KT_B=nc.dram_tensor("KT_B", [576, S], BF16, kind=kind).ap(),
        KT_C=nc.dram_tensor("KT_C", [128, S], BF16, kind=kind).ap(),
        QT_A=nc.dram_tensor("QT_A", [256, HALF], BF16, kind=kind).ap(),
        QT_B=nc.dram_tensor("QT_B", [576, HALF], BF16, kind=kind).ap(),
        QT_C=nc.dram_tensor("QT_C", [384, HALF], BF16, kind=kind).ap(),
        V=nc.dram_tensor("V_all", [S, 768], BF16, kind=kind).ap(),
        GATE=nc.dram_tensor("GATE", [D, HALF], F32, kind=kind).ap(),
        YT=nc.dram_tensor("YT", [D, HALF], BF16, kind=kind).ap(),
    )
    with ExitStack() as es:
        P = Prog(nc, es)
        scr["r_xin"] = P.res("x_in")
        scr["r_xout"] = P.res("x_out")
        emit_layer(nc, P, 0, x, xo, wts, cst, scr)
        P.emit_all()
        nsem = P.nsem
    return nc, nsem


def _rope_tables():
    f32 = np.float32
    freqs = np.power(f32(10000.0), (f32(-2.0) * np.arange(16, dtype=f32) / f32(32.0))).astype(f32)
    t = np.arange(S)
    row = (t // 64).astype(f32)
    col = (t % 64).astype(f32)
    tt = t.astype(f32)
    cosB = np.zeros((128, S), f32)
    sinB = np.zeros((128, S), f32)
    cosB[0:64] = 1.0
    for p in range(64, 96):
        sub = p - 64
        i = sub % 16
        ang = (tt * freqs[i]).astype(f32)
        cosB[p] = np.cos(ang)
        sinB[p] = (-np.sin(ang)) if sub < 16 else np.sin(ang)
    cosC = np.zeros((128, S), f32)
    sinC = np.zeros((128, S), f32)
    for p in range(128):
        dd = p % 64
        pos = row if dd < 32 else col
        sub = dd % 32
        i = sub % 16
        ang = (pos * freqs[i]).astype(f32)
        cosC[p] = np.cos(ang)
        sinC[p] = (-np.sin(ang)) if sub < 16 else np.sin(ang)
    return cosB, sinB, cosC, sinC


def _const_mats():
    permB = np.zeros((128, 128), np.float32)
    for p in range(64, 96):
        sub = p - 64
        partner = p + 16 if sub < 16 else p - 16
        permB[partner, p] = 1.0
    permC = np.zeros((128, 128), np.float32)
    for p in range(128):
        sub = (p % 64) % 32
        partner = p + 16 if sub < 16 else p - 16
        permC[partner, p] = 1.0
    ones2 = np.zeros((128, 128), np.float32)
    ones2[0:64, 0:64] = 1.0
    ones2[64:128, 64:128] = 1.0
    onesall = np.ones((128, 128), np.float32)
    cm = np.stack([permB, permC, ones2, onesall], axis=1)
    return np.ascontiguousarray(cm).astype(ml_dtypes.bfloat16)


def _gtab():
    delta = np.arange(-(GW + 128), GW + 128)
    mult = np.zeros(delta.shape, np.float64)
    for d in (1, 4, 16):
        mult += ((delta % d) == 0) & (np.abs(delta) // d <= 64)
    out = np.zeros((4, 128, GW), np.float32)
    kk = np.arange(128)[:, None]
    u = np.arange(GW)[None, :]
    dl = kk - u + GOFF
    idx = dl + (GW + 128)
    m = mult[idx]
    for h in range(4):
        slope = 2.0 ** (-8.0 * (h + 1) / 4.0)
        with np.errstate(divide="ignore"):
            tv = np.where(m > 0, np.log(np.maximum(m, 1.0)) - slope * np.abs(dl), -30000.0)
        out[h] = tv.astype(np.float32)
    return out


def _gains(l, p):
    gvec = np.zeros((128, NGV), np.float32)
    gvec[:, 0:8] = p["norm_g"][l].reshape(8, 128).T
    gvec[:, 8] = np.tile(p["a_q_norm_g"][l], 2)
    gvec[:, 9] = np.tile(p["a_k_norm_g"][l], 2)
    gvec[:, 10:12] = p["b_q_lat_norm_g"][l].reshape(2, 128).T
    gvec[:, 12] = p["b_kv_lat_norm_g"][l]
    gvec[0:96, 13] = p["b_q_norm_g"][l]
    gvec[0:96, 14] = p["b_k_norm_g"][l]
    gvec[:, 15] = np.tile(p["c_q_norm_g"][l], 2)
    gvec[:, 16] = np.tile(p["c_k_norm_g"][l], 2)
    return gvec


_CACHE = {}


def kernel(**inputs):
    p = {k: np.asarray(v) for k, v in inputs.items()}
    x = np.ascontiguousarray(p["x"], dtype=np.float32)
    if "nc" not in _CACHE:
        _CACHE["nc"] = build_program()[0]
        cosB, sinB, cosC, sinC = _rope_tables()
        tabs = {}
        for hf in range(2):
            sl = slice(None) if hf == 0 else slice(None, None, -1)
            tabs[hf] = dict(cosB=np.ascontiguousarray(cosB[:, sl]), sinB=np.ascontiguousarray(sinB[:, sl]),
                            cosC=np.ascontiguousarray(cosC[:, sl]), sinC=np.ascontiguousarray(sinC[:, sl]))
        _CACHE["tabs"] = tabs
        _CACHE["cm"] = _const_mats()
        _CACHE["ident"] = np.eye(128, dtype=np.float32)
        _CACHE["gtab"] = _gtab()
    nc = _CACHE["nc"]
    cur = x
    for l in range(DEPTH):
        gvec = _gains(l, p)
        in_maps = []
        for c in range(8):
            b, hf = c // 2, c % 2
            xl = cur[b] if hf == 0 else cur[b][::-1]
            m = dict(x=np.ascontiguousarray(xl),
                     w_in=np.ascontiguousarray(p["w_in"][l]), w_out=np.ascontiguousarray(p["w_out"][l]),
                     wq_up=np.ascontiguousarray(p["w_b_q_up"][l]), wkv_up=np.ascontiguousarray(p["w_b_kv_up"][l]),
                     gv=gvec, cm=_CACHE["cm"], ident=_CACHE["ident"], gtab=_CACHE["gtab"])
            m.update(_CACHE["tabs"][hf])
            in_maps.append(m)
        res = run_bass_kernel_spmd(nc, in_maps, core_ids=list(range(8)))
        new = np.empty_like(cur)
        for c in range(8):
            b, hf = c // 2, c % 2
            o = res.results[c]["xo"]
            if hf == 0:
                new[b, 0:HALF] = o
            else:
                new[b, HALF:S] = o[::-1]
        cur = new
    return cur
```

```python
import numpy as np
from contextlib import ExitStack
import ml_dtypes
import concourse.bass as bass
import concourse.mybir as mybir
from concourse.bass_utils import run_bass_kernel_spmd

F32 = mybir.dt.float32
BF16 = mybir.dt.bfloat16
AF = mybir.ActivationFunctionType
ALU = mybir.AluOpType

S = 8192
D = 1024
HALF = 4096
TB = 512
NBLK = S // TB
NOWN = HALF // TB
DEPTH = 4
IN_COLS = 2848
OFF = dict(aq=0, ak=256, av=512, ag=768, bql=1024, bkv=1280, bkpe=1408, bg=1440,
           cq=1824, ck=2208, cv=2336, cg=2464)
EPS = 1e-6
GW = 2944
GOFF = 1408
HW = 1408
NGV = 17


class Res:
    __slots__ = ("name", "w", "r", "dsem", "dcnt")

    def __init__(self, name):
        self.name = name
        self.w = None
        self.r = []
        self.dsem = None
        self.dcnt = 0


class Prog:
    COMPUTE = ("pe", "act", "dve", "pool")
    ALL = ("pe", "act", "dve", "pool", "sp")

    def __init__(self, nc, es):
        self.nc = nc
        self.es = es
        self.ops = {e: [] for e in self.ALL}
        self.tsem = {}
        self.tick = {}
        self.waited = {e: {} for e in self.ALL}
        self.nsem = 0
        self.epoch = 0
        self.allres = {}
        self.oldticks = []
        self.new_epoch()

    def _newsem(self, name):
        self.nsem += 1
        return self.es.enter_context(self.nc.semaphore(name))

    def new_epoch(self):
        self.epoch += 1
        for e in self.COMPUTE:
            if e in self.tsem and self.tick[e] > 0:
                self.oldticks.append((self.tsem[e], self.tick[e], e))
            self.tsem[e] = self._newsem(f"t_{e}_{self.epoch}")
            self.tick[e] = 0

    def res(self, name):
        if name not in self.allres:
            self.allres[name] = Res(name)
        return self.allres[name]

    def _need(self, eng, ev, waits):
        if ev is None:
            return
        sem, val, src = ev
        if src == "pe" and eng == "pe":
            return
        key = id(sem)
        if self.waited[eng].get(key, 0) >= val:
            return
        self.waited[eng][key] = val
        waits.append((sem, val))

    def _deps(self, eng, reads, writes):
        waits = []
        for r in reads:
            self._need(eng, r.w, waits)
        for w in writes:
            ev = w.w
            if ev is not None and ev[2] != eng and not (ev[2] == "dma" and eng == "sp"):
                self._need(eng, ev, waits)
            for rv in w.r:
                if rv[2] == eng:
                    continue
                self._need(eng, rv, waits)
        best = {}
        for sem, val in waits:
            k = id(sem)
            if k not in best or best[k][1] < val:
                best[k] = (sem, val)
        return list(best.values())

    def op(self, eng, meth, reads=(), writes=(), **kw):
        waits = self._deps(eng, reads, writes)
        self.tick[eng] += 1
        n = self.tick[eng]
        sem = self.tsem[eng]
        ev = (sem, n, eng)

        def emit(e, waits=waits, meth=meth, kw=kw, sem=sem):
            for s, v in waits:
                e.wait_ge(s, v)
            getattr(e, meth)(**kw).then_inc(sem, 1)
        self.ops[eng].append(emit)
        for r in reads:
            r.r.append(ev)
        for w in writes:
            w.w = ev
            w.r = []
        return ev

    def dma(self, q, out_ap, in_ap, reads=(), writes=()):
        waits = self._deps(q, reads, writes)
        owner = writes[0] if writes else reads[0]
        if owner.dsem is None:
            owner.dsem = self._newsem("d_" + owner.name)
        owner.dcnt += 16
        ev = (owner.dsem, owner.dcnt, "dma")

        def emit(e, waits=waits, sem=owner.dsem):
            for s, v in waits:
                e.wait_ge(s, v)
            e.dma_start(out=out_ap, in_=in_ap).then_inc(sem, 16)
        self.ops[q].append(emit)
        for r in reads:
            r.r.append(ev)
        for w in writes:
            w.w = ev
            w.r = []
        return ev

    def collective(self, kind, op, groups, in_ap, out_ap, reads, writes):
        q = "pool"
        waits = self._deps(q, reads, writes)
        owner = writes[0]
        if owner.dsem is None:
            owner.dsem = self._newsem("c_" + owner.name)
        owner.dcnt += 1
        ev = (owner.dsem, owner.dcnt, "dma")

        def emit(e, waits=waits, sem=owner.dsem):
            for s, v in waits:
                e.wait_ge(s, v)
            e.collective_compute(kind, op, replica_groups=groups, ins=[in_ap], outs=[out_ap]).then_inc(sem)
        self.ops[q].append(emit)
        for r in reads:
            r.r.append(ev)
        for w in writes:
            w.w = ev
            w.r = []
        return ev

    def barrier(self):
        evs = list(self.oldticks)
        for e in self.COMPUTE:
            if self.tick[e] > 0:
                evs.append((self.tsem[e], self.tick[e], "x"))
        for r in self.allres.values():
            if r.dsem is not None and r.dcnt > 0:
                evs.append((r.dsem, r.dcnt, "dma"))
        for eng in self.ALL:
            waits = []
            for sem, val, _ in evs:
                self._need(eng, (sem, val, "x"), waits)

            def emit(e, waits=waits):
                for s, v in waits:
                    e.wait_ge(s, v)
            self.ops[eng].append(emit)
        for r in self.allres.values():
            r.w = None
            r.r = []

    def emit_all(self):
        nc = self.nc
        ops = self.ops
        with nc.Block() as block:
            @block.tensor
            def _(e):
                for f in ops["pe"]:
                    f(e)

            @block.scalar
            def _(e):
                for f in ops["act"]:
                    f(e)

            @block.vector
            def _(e):
                for f in ops["dve"]:
                    f(e)

            @block.gpsimd
            def _(e):
                for f in ops["pool"]:
                    f(e)

            @block.sync
            def _(e):
                for f in ops["sp"]:
                    f(e)


class Bufs:
    def __init__(self, nc, P, es, prefix, uid=""):
        self.nc, self.P, self.es, self.prefix, self.uid = nc, P, es, prefix, uid
        self.b = {}
        self.i = {}

    def mk(self, name, shape, dtype, n=1, psum=False):
        lst = []
        for k in range(n):
            nm = f"{self.prefix}_{name}{k}"
            tn = f"{self.prefix}{self.uid}_{name}{k}"
            if psum:
                t = self.es.enter_context(self.nc.psum_tensor(tn, shape, dtype))
            else:
                t = self.es.enter_context(self.nc.sbuf_tensor(tn, shape, dtype))
            lst.append((t, self.P.res(nm)))
        self.b[name] = lst
        self.i[name] = 0
        return lst[0]

    def nxt(self, name):
        lst = self.b[name]
        k = self.i[name]
        self.i[name] = (k + 1) % len(lst)
        return lst[k]

    def get(self, name, k=0):
        return self.b[name][k]

    def acq(self, name):
        if not hasattr(self, "free"):
            self.free = {}
        fl = self.free.setdefault(name, list(self.b[name]))
        assert fl, f"buffer pool {name} exhausted"
        return fl.pop(0)

    def rel(self, name, item):
        self.free[name].append(item)


def run_gens(gens, width):
    gens = list(gens)
    active = []
    while gens or active:
        while gens and len(active) < width:
            active.append(gens.pop(0))
        nxt = []
        for g in active:
            try:
                next(g)
                nxt.append(g)
            except StopIteration:
                pass
        active = nxt


def emit_layer(nc, P, lid, src, dst, wts, cst, scr):
    w_in, w_out, wq_up, wkv_up, gvd = wts["w_in"], wts["w_out"], wts["wq_up"], wts["wkv_up"], wts["gv"]
    KT_A, KT_B, KT_C = scr["KT_A"], scr["KT_B"], scr["KT_C"]
    QT_A, QT_B, QT_C = scr["QT_A"], scr["QT_B"], scr["QT_C"]
    V_all, GATE, YT = scr["V"], scr["GATE"], scr["YT"]
    r_KT, r_QT, r_V, r_GATE, r_YT = (P.res("s_KT"), P.res("s_QT"), P.res("s_V"),
                                     P.res("s_GATE"), P.res("s_YT"))
    r_w = P.res("w_dram")

    def ACT(reads, writes, **kw):
        P.op("act", "activation", reads, writes, **kw)

    def MM(reads, writes, **kw):
        P.op("pe", "matmul", reads, writes, **kw)

    with ExitStack() as esL:
        BL = Bufs(nc, P, esL, "L", lid)
        Wb, r_Wb = BL.mk("Wb", [128, 8, IN_COLS], BF16)
        Wqu, r_Wqu = BL.mk("Wqu", [128, 2, 576], BF16)
        Wkk, r_Wkk = BL.mk("Wkk", [128, 6, 64], BF16)
        Wkv, r_Wkv = BL.mk("Wkv", [128, 384], BF16)
        gv, r_gv = BL.mk("gv", [128, NGV], F32)
        cm, r_cm = BL.mk("cm", [128, 4, 128], BF16)
        ident, r_id = BL.mk("ident", [128, 128], F32)
        epsb, r_eps = BL.mk("epsb", [128, 1], F32)
        msk, r_msk = BL.mk("msk", [128, 2], F32)
        permB, permC, ones2, onesall = (cm[:, 0, :], cm[:, 1, :], cm[:, 2, :], cm[:, 3, :])

        P.dma("sp", gv[:], gvd, reads=[r_w], writes=[r_gv])
        P.dma("sp", cm[:], cst["cm"], reads=[r_w], writes=[r_cm])
        P.dma("sp", ident[:], cst["ident"], reads=[r_w], writes=[r_id])
        P.dma("sp", msk[:], cst["msk"], reads=[r_w], writes=[r_msk])
        P.op("dve", "memset", [], [r_eps], ap=epsb[:], constant=EPS)

        with ExitStack() as esW:
            BW = Bufs(nc, P, esW, "W", lid)
            BW.mk("wst", [128, IN_COLS], F32, 2)
            BW.mk("wq", [128, 2, 576], F32)
            BW.mk("wk", [128, 768], F32)
            win_v = w_in.rearrange("(c p) n -> p c n", p=128)
            for c in range(8):
                st, r_st = BW.nxt("wst")
                P.dma("sp", st[:], win_v[:, c, :], reads=[r_w], writes=[r_st])
                eng = ("dve", "act")[c % 2]
                if eng == "act":
                    ACT([r_st], [r_Wb], out=Wb[:, c, :], in_=st[:], func=AF.Copy)
                else:
                    P.op(eng, "tensor_copy", [r_st], [r_Wb], out=Wb[:, c, :], in_=st[:])
            wq, r_wq = BW.get("wq")
            P.dma("sp", wq[:], wq_up.rearrange("(c p) n -> p c n", p=128), reads=[r_w], writes=[r_wq])
            P.op("dve", "tensor_copy", [r_wq], [r_Wqu], out=Wqu[:], in_=wq[:])
            wk, r_wk = BW.get("wk")
            P.dma("sp", wk[:], wkv_up, reads=[r_w], writes=[r_wk])
            wk3 = wk[:].rearrange("p (h t) -> p h t", t=128)
            P.op("dve", "tensor_copy", [r_wk], [r_Wkk], out=Wkk[:], in_=wk3[:, :, 0:64])
            P.op("dve", "tensor_copy", [r_wk], [r_Wkv], out=Wkv[:].rearrange("p (h d) -> p h d", d=64),
                 in_=wk3[:, :, 64:128])
            P.barrier()

        with ExitStack() as esP:
            B = Bufs(nc, P, esP, "P", lid)
            B.mk("bank", [128, 512], F32, 8, psum=True)
            B.mk("xt", [128, 4, D], F32, 2)
            if src["gath"] is not None:
                B.mk("xg", [128, 4, D], F32, 1)
            B.mk("junk", [128, D], BF16)
            B.mk("ss", [128, 4], F32, 2)
            B.mk("l4", [128, 4], F32, 2)
            B.mk("r4", [128, 4], F32, 2)
            B.mk("hT", [128, 8, TB], BF16, 2)
            B.mk("tab", [128, 4, TB], F32, 2)
            B.mk("sq", [128, TB], BF16, 5)
            B.mk("ln", [128, TB], F32, 4)
            B.mk("qn", [128, TB], BF16, 4)
            B.mk("t1", [128, TB], F32, 4)
            B.mk("t2", [128, TB], F32, 4)
            B.mk("ob", [128, TB], BF16, 6)
            B.mk("qln", [128, 2, TB], BF16, 2)
            B.mk("kvn", [128, TB], BF16, 2)
            B.mk("ge", [128, TB], F32, 4)
            B.mk("go", [128, TB], F32, 4)
            B.mk("vst", [128, 4, 768], BF16, 1)
            PW = 4

            def proj_group(hT, r_hT, col0, M, out_rows=0, bk=None):
                if bk is None:
                    bk = B.acq("bank")
                bT, bR = bk
                for c in range(8):
                    MM([r_Wb, r_hT], [bR], out=bT[out_rows:out_rows + M, :], lhsT=Wb[:, c, col0:col0 + M],
                       rhs=hT[:, c, :], start=(c == 0), stop=(c == 7))
                return bk

            def square(bk, M):
                bT, bR = bk
                sqp = B.acq("sq")
                sq, r_sq = sqp
                ACT([bR], [r_sq], out=sq[0:M, :], in_=bT[0:M, :], func=AF.Square)
                return sqp

            def rstd_gen(sqs, M, ones_ap, dk, out):
                pn = B.acq("bank")
                pnT, pnR = pn
                for i, (sq, r_sq) in enumerate(sqs):
                    MM([r_sq, r_cm], [pnR], out=pnT[0:M, :], lhsT=ones_ap, rhs=sq[0:M, :],
                       start=(i == 0), stop=(i == len(sqs) - 1))
                for sqp in sqs:
                    B.rel("sq", sqp)
                yield
                lnp = B.acq("ln")
                ln, r_ln = lnp
                ACT([pnR, r_eps], [r_ln], out=ln[0:M, :], in_=pnT[0:M, :], func=AF.Ln,
                    scale=1.0 / dk, bias=epsb[0:M, :])
                B.rel("bank", pn)
                yield
                ACT([r_ln], [r_ln], out=ln[0:M, :], in_=ln[0:M, :], func=AF.Exp, scale=-0.5)
                out.append(lnp)
                yield

            def headnorm_gen(mk_bank, M, dk, gcol, ones_ap, rope, tab, r_tab, stores, r_dst):
                bk = mk_bank()
                bT, bR = bk
                yield
                sqp = square(bk, M)
                yield
                res = []
                yield from rstd_gen([sqp], M, ones_ap, dk, res)
                rs, r_rs = res[0]
                qnp = B.acq("qn")
                qn, r_qn = qnp
                P.op("dve", "scalar_tensor_tensor", [bR, r_rs, r_gv], [r_qn], out=qn[0:M, :], in0=bT[0:M, :],
                     scalar=gv[0:M, gcol:gcol + 1], in1=rs[0:M, :], op0=ALU.mult, op1=ALU.mult)
                B.rel("ln", res[0])
                yield
                if rope is None:
                    B.rel("bank", bk)
                    for (dstap, lo, hi) in stores:
                        P.dma("sp", dstap, qn[lo:hi, :], reads=[r_qn], writes=[r_dst])
                    B.rel("qn", qnp)
                    yield
                    return
                perm = permB[0:M, 0:M] if rope == "B" else permC
                ci, si = (0, 1) if rope == "B" else (2, 3)
                MM([r_qn, r_cm], [bR], out=bT[0:M, :], lhsT=perm, rhs=qn[0:M, :], start=True, stop=True)
                t1p = B.acq("t1")
                t1, r_t1 = t1p
                P.op("pool", "tensor_tensor", [r_qn, r_tab], [r_t1], out=t1[0:M, :], in0=qn[0:M, :],
                     in1=tab[0:M, ci, :], op=ALU.mult)
                B.rel("qn", qnp)
                yield
                t2p = B.acq("t2")
                t2, r_t2 = t2p
                P.op("dve", "tensor_tensor", [bR, r_tab], [r_t2], out=t2[0:M, :], in0=bT[0:M, :],
                     in1=tab[0:M, si, :], op=ALU.mult)
                B.rel("bank", bk)
                yield
                obp = B.acq("ob")
                fin, r_fin = obp
                P.op("pool", "tensor_tensor", [r_t1, r_t2], [r_fin], out=fin[0:M, :], in0=t1[0:M, :],
                     in1=t2[0:M, :], op=ALU.add)
                B.rel("t1", t1p)
                B.rel("t2", t2p)
                yield
                for (dstap, lo, hi) in stores:
                    P.dma("sp", dstap, fin[lo:hi, :], reads=[r_fin], writes=[r_dst])
                B.rel("ob", obp)
                yield

            def gate_gen(hT, r_hT, col0, y0, t0):
                bk = proj_group(hT, r_hT, col0, 128)
                yield
                gep = B.acq("ge")
                ge, r_ge = gep
                ACT([bk[1]], [r_ge], out=ge[:], in_=bk[0][:], func=AF.Exp, scale=-1.0)
                yield
                P.op("dve", "tensor_scalar", [r_ge], [r_ge], out=ge[:], in0=ge[:], scalar1=1.0, scalar2=None, op0=ALU.add)
                yield
                P.op("dve", "reciprocal", [r_ge], [r_ge], out=ge[:], in_=ge[:])
                yield
                gop = B.acq("go")
                go, r_go = gop
                P.op("dve", "tensor_tensor", [bk[1], r_ge], [r_go], out=go[:], in0=bk[0][:], in1=ge[:], op=ALU.mult)
                B.rel("bank", bk)
                B.rel("ge", gep)
                yield
                P.dma("sp", GATE[y0:y0 + 128, t0:t0 + TB], go[:], reads=[r_go], writes=[r_GATE])
                B.rel("go", gop)
                yield

            V_v = V_all.rearrange("(n j p) c -> n p j c", j=4, p=128)

            def load_block(blk):
                xt, r_xt = B.nxt("xt")
                if blk < NOWN:
                    P.dma("sp", xt[:], src["own_blk"](blk), reads=[src["r_own"]], writes=[r_xt])
                elif src["gath"] is None:
                    P.dma("sp", xt[:], src["oth_blk"](blk - NOWN), reads=[src["r_oth"]], writes=[r_xt])
                else:
                    xg, r_xg = B.nxt("xg")
                    P.dma("sp", xt[:], src["ga_blk"](blk - NOWN), reads=[src["r_oth"]], writes=[r_xt])
                    P.dma("sp", xg[:], src["gb_blk"](blk - NOWN), reads=[src["r_oth"]], writes=[r_xg])
                    xt2 = xt[:].rearrange("p j d -> p (j d)")
                    xg2 = xg[:].rearrange("p j d -> p (j d)")
                    ACT([r_xt, r_msk], [r_xt], out=xt2, in_=xt2, func=AF.Copy, scale=msk[:, 0:1])
                    P.op("dve", "scalar_tensor_tensor", [r_xg, r_xt, r_msk], [r_xt], out=xt2, in0=xg2,
                         scalar=msk[:, 1:2], in1=xt2, op0=ALU.mult, op1=ALU.add)
                tab, r_tab = B.nxt("tab")
                t0 = blk * TB
                for i, nm in enumerate(("cosB", "sinB", "cosC", "sinC")):
                    P.dma("sp", tab[:, i, :], cst[nm][:, t0:t0 + TB], reads=[r_w], writes=[r_tab])
                return xt, r_xt, tab, r_tab

            nxt_loaded = load_block(0)
            for blk in range(NBLK):
                own = blk < NOWN
                t0 = blk * TB
                xt, r_xt, tab, r_tab = nxt_loaded
                if blk + 1 < NBLK:
                    nxt_loaded = load_block(blk + 1)
                junk, r_junk = B.get("junk")
                ss, r_ss = B.nxt("ss")
                for j in range(4):
                    ACT([r_xt], [r_junk, r_ss], out=junk[:], in_=xt[:, j, :], func=AF.Square, accum_out=ss[:, j:j + 1])
                l4, r_l4 = B.nxt("l4")
                ACT([r_ss, r_eps], [r_l4], out=l4[:], in_=ss[:], func=AF.Ln, scale=1.0 / D, bias=epsb[:])
                r4, r_r4 = B.nxt("r4")
                ACT([r_l4], [r_r4], out=r4[:], in_=l4[:], func=AF.Exp, scale=-0.5)
                for j in range(4):
                    if j % 2 == 0:
                        ACT([r_r4, r_xt], [r_xt], out=xt[:, j, :], in_=xt[:, j, :], func=AF.Copy, scale=r4[:, j:j + 1])
                    else:
                        P.op("dve", "tensor_scalar", [r_r4, r_xt], [r_xt], out=xt[:, j, :], in0=xt[:, j, :],
                             scalar1=r4[:, j:j + 1], scalar2=None, op0=ALU.mult)
                hT, r_hT = B.nxt("hT")
                for c in range(8):
                    ptp = B.acq("bank")
                    ptT, ptR = ptp
                    for j in range(4):
                        P.op("pe", "transpose", [r_xt, r_id], [ptR], out=ptT[:, j * 128:(j + 1) * 128],
                             in_=xt[:, j, c * 128:(c + 1) * 128], identity=ident[:])
                    if c % 2 == 0:
                        P.op("dve", "tensor_scalar", [ptR, r_gv], [r_hT], out=hT[:, c, :], in0=ptT[:],
                             scalar1=gv[:, c:c + 1], scalar2=None, op0=ALU.mult)
                    else:
                        ACT([ptR, r_gv], [r_hT], out=hT[:, c, :], in_=ptT[:], func=AF.Copy, scale=gv[:, c:c + 1])
                    B.rel("bank", ptp)

                kvn, r_kvn = B.nxt("kvn")
                qln, r_qln = B.nxt("qln")

                def kvlat_gen():
                    bk = proj_group(hT, r_hT, OFF["bkv"], 128)
                    yield
                    sqp = square(bk, 128)
                    yield
                    res = []
                    yield from rstd_gen([sqp], 128, onesall, 128.0, res)
                    rs, r_rs = res[0]
                    P.op("dve", "scalar_tensor_tensor", [bk[1], r_rs, r_gv], [r_kvn], out=kvn[:], in0=bk[0][:],
                         scalar=gv[:, 12:13], in1=rs[:], op0=ALU.mult, op1=ALU.mult)
                    B.rel("ln", res[0])
                    B.rel("bank", bk)
                    yield

                def qlat_gen():
                    bq = [proj_group(hT, r_hT, OFF["bql"] + 128 * c, 128) for c in range(2)]
                    yield
                    sqs = [square(bq[0], 128), square(bq[1], 128)]
                    yield
                    res = []
                    yield from rstd_gen(sqs, 128, onesall, 256.0, res)
                    rs, r_rs = res[0]
                    for c in range(2):
                        P.op("dve", "scalar_tensor_tensor", [bq[c][1], r_rs, r_gv], [r_qln], out=qln[:, c, :],
                             in0=bq[c][0][:], scalar=gv[:, 10 + c:11 + c], in1=rs[:], op0=ALU.mult, op1=ALU.mult)
                    B.rel("ln", res[0])
                    B.rel("bank", bq[0])
                    B.rel("bank", bq[1])
                    yield

                def pg(col0):
                    return lambda: proj_group(hT, r_hT, col0, 128)

                def two(dst, g):
                    return [(dst[(2 * g) * 64:(2 * g + 1) * 64, t0:t0 + TB], 0, 64),
                            (dst[(2 * g + 1) * 64:(2 * g + 2) * 64, t0:t0 + TB], 64, 128)]

                gens = [kvlat_gen()]
                if own:
                    gens.append(qlat_gen())
                run_gens(gens, 2)
                gens = []
                for g in range(2):
                    gens.append(headnorm_gen(pg(OFF["ak"] + 128 * g), 128, 64.0, 9, ones2, None, tab, r_tab, two(KT_A, g), r_KT))
                gens.append(headnorm_gen(pg(OFF["ck"]), 128, 64.0, 16, ones2, "C", tab, r_tab, two(KT_C, 0), r_KT))
                if own:
                    for g in range(2):
                        gens.append(headnorm_gen(pg(OFF["aq"] + 128 * g), 128, 64.0, 8, ones2, None, tab, r_tab, two(QT_A, g), r_QT))
                    for g in range(3):
                        gens.append(headnorm_gen(pg(OFF["cq"] + 128 * g), 128, 64.0, 15, ones2, "C", tab, r_tab, two(QT_C, g), r_QT))
                    for (gcol, yrow, ng) in ((OFF["ag"], 0, 2), (OFF["bg"], 256, 3), (OFF["cg"], 640, 3)):
                        for g in range(ng):
                            gens.append(gate_gen(hT, r_hT, gcol + 128 * g, yrow + 128 * g, t0))
                run_gens(gens, PW)

                vst, r_vst = B.nxt("vst")
                for j in range(4):
                    pvp = B.acq("bank")
                    pvT, pvR = pvp
                    for (cols, n, o) in ((OFF["av"], 256, 0), (OFF["cv"], 128, 256)):
                        for c in range(8):
                            MM([r_hT, r_Wb], [pvR], out=pvT[:, o:o + n], lhsT=hT[:, c, j * 128:(j + 1) * 128],
                               rhs=Wb[:, c, cols:cols + n], start=(c == 0), stop=(c == 7))
                    ACT([pvR], [r_vst], out=vst[:, j, 0:384], in_=pvT[:, 0:384], func=AF.Copy)
                    B.rel("bank", pvp)
                    pv2p = B.acq("bank")
                    pv2T, pv2R = pv2p
                    MM([r_kvn, r_Wkv], [pv2R], out=pv2T[:, 0:384], lhsT=kvn[:, j * 128:(j + 1) * 128],
                       rhs=Wkv[:], start=True, stop=True)
                    P.op("dve", "tensor_copy", [pv2R], [r_vst], out=vst[:, j, 384:768], in_=pv2T[:, 0:384])
                    B.rel("bank", pv2p)
                P.dma("sp", V_v[blk], vst[:], reads=[r_vst], writes=[r_V])

                def bk_bank(h):
                    def f():
                        bkh = B.acq("bank")
                        MM([r_Wkk, r_kvn], [bkh[1]], out=bkh[0][0:64, :], lhsT=Wkk[:, h, :], rhs=kvn[:], start=True, stop=True)
                        proj_group(hT, r_hT, OFF["bkpe"], 32, out_rows=64, bk=bkh)
                        return bkh
                    return f

                def bq_bank(h):
                    def f():
                        bkh = B.acq("bank")
                        for c in range(2):
                            MM([r_Wqu, r_qln], [bkh[1]], out=bkh[0][0:96, :], lhsT=Wqu[:, c, h * 96:(h + 1) * 96],
                               rhs=qln[:, c, :], start=(c == 0), stop=(c == 1))
                        return bkh
                    return f

                gens = []
                for h in range(6):
                    gens.append(headnorm_gen(bk_bank(h), 96, 96.0, 14, onesall[0:96, 0:96], "B", tab, r_tab,
                                             [(KT_B[h * 96:(h + 1) * 96, t0:t0 + TB], 0, 96)], r_KT))
                    if own:
                        gens.append(headnorm_gen(bq_bank(h), 96, 96.0, 13, onesall[0:96, 0:96], "B", tab, r_tab,
                                                 [(QT_B[h * 96:(h + 1) * 96, t0:t0 + TB], 0, 96)], r_QT))
                run_gens(gens, PW)
            P.barrier()

        with ExitStack() as esA:
            B = Bufs(nc, P, esA, "A", lid)
            B.mk("S", [128, 1024], F32, 2, psum=True)
            B.mk("O", [128, 512], F32, 2, psum=True)
            B.mk("kT", [128, S], BF16, 2)
            B.mk("vA", [128, 64, 128], BF16, 2)
            B.mk("qT", [128, HALF], BF16, 2)
            B.mk("Gt", [128, GW + HW], F32, 2)
            B.mk("gt", [64, 512], F32, 2)
            B.mk("pT", [128, 1024], BF16, 3)
            B.mk("rd", [128, 512], F32, 2)
            B.mk("rd0", [64, 512], F32, 2)
            B.mk("yf", [64, 512], F32, 2)
            B.mk("yb", [64, 512], BF16, 2)
            for k in range(2):
                vt, vr = B.get("vA", k)
                P.op("pool", "memset", [], [vr], ap=vt[:, :, 64:128], constant=1.0)

            jobs = []
            for h in range(4):
                jobs.append(dict(kv=("A", h), kt=KT_A[h * 64:(h + 1) * 64, :], dk=64, vcol=h * 64,
                                 q=QT_A[h * 64:(h + 1) * 64, :], scale=0.125, yrow=h * 64, isA=True, gi=h))
            for h in range(6):
                jobs.append(dict(kv=("B", h), kt=KT_B[h * 96:(h + 1) * 96, :], dk=96, vcol=384 + 64 * h,
                                 q=QT_B[h * 96:(h + 1) * 96, :], scale=96.0 ** -0.5, yrow=256 + 64 * h, isA=False))
            for h in range(6):
                kvh = h // 3
                jobs.append(dict(kv=("C", kvh), kt=KT_C[kvh * 64:(kvh + 1) * 64, :], dk=64, vcol=256 + 64 * kvh,
                                 q=QT_C[h * 64:(h + 1) * 64, :], scale=0.125, yrow=640 + 64 * h, isA=False))

            V_cv = V_all.rearrange("(c p) d -> p c d", p=128)
            state = dict(kvkey=None, kvbuf=None)

            def load_job(job):
                if job["kv"] != state["kvkey"]:
                    kT, r_kT = B.nxt("kT")
                    vA, r_vA = B.nxt("vA")
                    dk = job["dk"]
                    for q4 in range(4):
                        P.dma("sp", kT[0:dk, q4 * 2048:(q4 + 1) * 2048], job["kt"][:, q4 * 2048:(q4 + 1) * 2048],
                              reads=[r_KT], writes=[r_kT])
                    vc = job["vcol"]
                    for q8 in range(8):
                        P.dma("sp", vA[:, q8 * 8:(q8 + 1) * 8, 0:64], V_cv[:, q8 * 8:(q8 + 1) * 8, vc:vc + 64],
                              reads=[r_V], writes=[r_vA])
                    state["kvkey"] = job["kv"]
                    state["kvbuf"] = (kT, r_kT, vA, r_vA)
                job["kvbuf"] = state["kvbuf"]
                qT, r_qT = B.nxt("qT")
                P.dma("sp", qT[0:job["dk"], :], job["q"], reads=[r_QT], writes=[r_qT])
                job["qbuf"] = (qT, r_qT)
                if job["isA"]:
                    Gt, r_Gt = B.nxt("Gt")
                    P.dma("sp", Gt[:], cst["gtab"][job["gi"]], reads=[r_w], writes=[r_Gt])
                    job["gbuf"] = (Gt, r_Gt)

            units = []
            for ji, job in enumerate(jobs):
                cnt = 0
                for qb in range(NOWN):
                    q0 = qb * TB
                    if job["isA"]:
                        ch = []
                        for cc in range(max(0, q0 - 1024) // 128, min(HALF, q0 + TB + 1024) // 128):
                            c = (cc * 128 - q0) // 128
                            ch.append((cc, GOFF - 128 * c))
                        for k in range(8):
                            m = 639 + 128 * k
                            s0 = S - 1 - q0 - m
                            if 0 <= s0 < HALF:
                                assert s0 % 128 == 0
                                ch.append(((HALF + s0) // 128, GW + (1535 - m)))
                    else:
                        ch = [(cc, None) for cc in range(S // 128)]
                    assert len(ch) % 2 == 0
                    for i in range(0, len(ch), 2):
                        units.append(dict(ji=ji, qb=qb, ca=ch[i][0], cb=ch[i + 1][0], ua=ch[i][1], ub=ch[i + 1][1],
                                          first=(i == 0), last=(i == len(ch) - 2), idx=cnt))
                        cnt += 1

            def stage1(u):
                job = jobs[u["ji"]]
                if u["idx"] == 0 and u["ji"] == 0:
                    load_job(jobs[0])
                if u["idx"] == 2 and u["ji"] + 1 < len(jobs):
                    load_job(jobs[u["ji"] + 1])
                q0 = u["qb"] * TB
                if u["first"]:
                    gt, r_gt = B.nxt("gt")
                    P.dma("sp", gt[:], GATE[job["yrow"]:job["yrow"] + 64, q0:q0 + TB], reads=[r_GATE], writes=[r_gt])
                    job["cur"] = ((gt, r_gt), B.nxt("O"))
                u["gt"], u["O"] = job["cur"]
                kT, r_kT, vA, r_vA = job["kvbuf"]
                qT, r_qT = job["qbuf"]
                dk = job["dk"]
                sT, r_S = B.nxt("S")
                u["S"] = (sT, r_S)
                for k, cc in enumerate((u["ca"], u["cb"])):
                    MM([r_kT, r_qT], [r_S], out=sT[:, k * 512:(k + 1) * 512], lhsT=kT[0:dk, cc * 128:(cc + 1) * 128],
                       rhs=qT[0:dk, q0:q0 + TB], start=True, stop=True)

            def stage2(u):
                job = jobs[u["ji"]]
                sT, r_S = u["S"]
                pT, r_pT = B.nxt("pT")
                r_pTb = P.res(r_pT.name + "_b")
                u["pT"] = (pT, r_pT, r_pTb)
                ACT([r_S], [r_pT, r_pTb], out=pT[:], in_=sT[:], func=AF.Exp, scale=job["scale"])
                if job["isA"]:
                    Gt, r_Gt = job["gbuf"]
                    P.op("dve", "tensor_tensor", [r_pT, r_Gt], [r_pT], out=pT[:, 0:512], in0=pT[:, 0:512],
                         in1=Gt[:, u["ua"]:u["ua"] + 512], op=ALU.mult)
                    P.op("pool", "tensor_tensor", [r_pTb, r_Gt], [r_pTb], out=pT[:, 512:1024], in0=pT[:, 512:1024],
                         in1=Gt[:, u["ub"]:u["ub"] + 512], op=ALU.mult)

            def stage3(u):
                job = jobs[u["ji"]]
                kT, r_kT, vA, r_vA = job["kvbuf"]
                pT, r_pT, r_pTb = u["pT"]
                oT, r_O = u["O"]
                for k, cc in enumerate((u["ca"], u["cb"])):
                    MM([r_vA, (r_pT, r_pTb)[k]], [r_O], out=oT[:], lhsT=vA[:, cc, :], rhs=pT[:, k * 512:(k + 1) * 512],
                       start=(u["first"] and k == 0), stop=(u["last"] and k == 1))
                if u["last"]:
                    gt, r_gt = u["gt"]
                    rd, r_rd = B.nxt("rd")
                    P.op("dve", "reciprocal", [r_O], [r_rd], out=rd[64:128, :], in_=oT[64:128, :])
                    rd0, r_rd0 = B.nxt("rd0")
                    P.op("dve", "tensor_copy", [r_rd], [r_rd0], out=rd0[:], in_=rd[64:128, :])
                    yf, r_yf = B.nxt("yf")
                    P.op("dve", "tensor_tensor", [r_O, r_rd0], [r_yf], out=yf[:], in0=oT[0:64, :], in1=rd0[:], op=ALU.mult)
                    yb, r_yb = B.nxt("yb")
                    P.op("pool", "tensor_tensor", [r_yf, r_gt], [r_yb], out=yb[:], in0=yf[:], in1=gt[:], op=ALU.mult)
                    q0 = u["qb"] * TB
                    P.dma("sp", YT[job["yrow"]:job["yrow"] + 64, q0:q0 + TB], yb[:], reads=[r_yb], writes=[r_YT])

            n = len(units)
            for i in range(n + 2):
                if i < n:
                    stage1(units[i])
                if 0 <= i - 1 < n:
                    stage2(units[i - 1])
                if 0 <= i - 2 < n:
                    stage3(units[i - 2])
            P.barrier()

        with ExitStack() as esO:
            B = Bufs(nc, P, esO, "O", lid)
            B.mk("bank", [128, 512], F32, 8, psum=True)
            Wo, r_Wo = B.mk("Wo", [128, 8, D], BF16)
            B.mk("wost", [128, D], F32, 2)
            B.mk("yT", [128, 8, TB], BF16, 2)
            B.mk("xo", [128, 4, D], F32, 2)
            wo_v = w_out.rearrange("(c p) n -> p c n", p=128)
            for c in range(8):
                st, r_st = B.nxt("wost")
                P.dma("sp", st[:], wo_v[:, c, :], reads=[r_w], writes=[r_st])
                if c % 2 == 0:
                    P.op("dve", "tensor_copy", [r_st], [r_Wo], out=Wo[:, c, :], in_=st[:])
                else:
                    ACT([r_st], [r_Wo], out=Wo[:, c, :], in_=st[:], func=AF.Copy)
            YT_v = YT.rearrange("(c p) t -> p c t", p=128)
            for blk in range(NOWN):
                q0 = blk * TB
                yT, r_yT = B.nxt("yT")
                P.dma("sp", yT[:], YT_v[:, :, q0:q0 + TB], reads=[r_YT], writes=[r_yT])
                xo, r_xo = B.nxt("xo")
                P.dma("sp", xo[:], src["own_blk"](blk), reads=[src["r_own"]], writes=[r_xo])
                for j in range(4):
                    for hh in range(2):
                        bT, bR = B.nxt("bank")
                        for c in range(8):
                            MM([r_yT, r_Wo], [bR], out=bT[:], lhsT=yT[:, c, j * 128:(j + 1) * 128],
                               rhs=Wo[:, c, hh * 512:(hh + 1) * 512], start=(c == 0), stop=(c == 7))
                        P.op("dve", "tensor_tensor", [bR, r_xo], [r_xo], out=xo[:, j, hh * 512:(hh + 1) * 512],
                             in0=bT[:], in1=xo[:, j, hh * 512:(hh + 1) * 512], op=ALU.add)
                P.dma("sp", dst["blk"](blk), xo[:], reads=[r_xo], writes=[dst["res"]])
                if dst["after"] is not None:
                    dst["after"](blk)
            P.barrier()


PAIRS = [[0, 1], [2, 3], [4, 5], [6, 7]]


def build_program(nl=DEPTH, dbg=False):
    nc = bass.Bass("TRN2", target_bir_lowering=False)
    x_own = nc.dram_tensor("x_own", [HALF, D], F32, kind="ExternalInput").ap()
    x_oth = nc.dram_tensor("x_oth", [HALF, D], F32, kind="ExternalInput").ap()
    xo = nc.dram_tensor("xo", [HALF, D], F32, kind="ExternalOutput").ap()
    wts = []
    for l in range(nl):
        wts.append(dict(
            w_in=nc.dram_tensor(f"w_in{l}", [D, IN_COLS], F32, kind="ExternalInput").ap(),
            w_out=nc.dram_tensor(f"w_out{l}", [D, D], F32, kind="ExternalInput").ap(),
            wq_up=nc.dram_tensor(f"wq_up{l}", [256, 576], F32, kind="ExternalInput").ap(),
            wkv_up=nc.dram_tensor(f"wkv_up{l}", [128, 768], F32, kind="ExternalInput").ap(),
            gv=nc.dram_tensor(f"gv{l}", [128, NGV], F32, kind="ExternalInput").ap(),
        ))
    cst = dict(
        cm=nc.dram_tensor("cm", [128, 4, 128], BF16, kind="ExternalInput").ap(),
        ident=nc.dram_tensor("ident", [128, 128], F32, kind="ExternalInput").ap(),
        gtab=nc.dram_tensor("gtab", [4, 128, GW + HW], F32, kind="ExternalInput").ap(),
        msk=nc.dram_tensor("msk", [128, 2], F32, kind="ExternalInput").ap(),
    )
    for nm in ("cosB", "sinB", "cosC", "sinC"):
        cst[nm] = nc.dram_tensor(nm, [128, S], F32, kind="ExternalInput").ap()
    kind = "ExternalOutput" if dbg else "Internal"
    scr = dict(
        KT_A=nc.dram_tensor("KT_A", [256, S], BF16, kind=kind).ap(),
        KT_B=nc.dram_tensor("KT_B", [576, S], BF16, kind=kind).ap(),
        KT_C=nc.dram_tensor("KT_C", [128, S], BF16, kind=kind).ap(),
        QT_A=nc.dram_tensor("QT_A", [256, HALF], BF16, kind=kind).ap(),
        QT_B=nc.dram_tensor("QT_B", [576, HALF], BF16, kind=kind).ap(),
        QT_C=nc.dram_tensor("QT_C", [384, HALF], BF16, kind=kind).ap(),
        V=nc.dram_tensor("V_all", [S, 768], BF16, kind=kind).ap(),
        GATE=nc.dram_tensor("GATE", [D, HALF], F32, kind=kind).ap(),
        YT=nc.dram_tensor("YT", [D, HALF], BF16, kind=kind).ap(),
    )
    ownc = [[nc.dram_tensor(f"ownc{i}_{j}", [TB, D], F32) for j in range(NOWN)] for i in range(2)]
    gathc = [[nc.dram_tensor(f"gathc{i}_{j}", [2 * TB, D], F32) for j in range(NOWN)] for i in range(2)]

    def blkview(ap):
        return ap.rearrange("(j p) d -> p j d", p=128)

    x_own_v = x_own.rearrange("(n j p) d -> n p j d", j=4, p=128)
    x_oth_v = x_oth.rearrange("(n j p) d -> n p j d", j=4, p=128)
    xo_v = xo.rearrange("(n j p) d -> n p j d", j=4, p=128)
    with ExitStack() as es:
        P = Prog(nc, es)
        r_in = P.res("x_in")
        r_ownc = [P.res(f"ownc{i}") for i in range(2)]
        r_gathc = [P.res(f"gathc{i}") for i in range(2)]
        for l in range(nl):
            if l > 0:
                P.new_epoch()
            if l == 0:
                src = dict(own_blk=lambda b: x_own_v[b], r_own=r_in, oth_blk=lambda j: x_oth_v[j], r_oth=r_in, gath=None)
            else:
                k = (l - 1) % 2
                src = dict(own_blk=lambda b, k=k: blkview(ownc[k][b].ap()), r_own=r_ownc[k], gath=True,
                           ga_blk=lambda j, k=k: blkview(gathc[k][j].ap()[0:TB, :]),
                           gb_blk=lambda j, k=k: blkview(gathc[k][j].ap()[TB:2 * TB, :]), r_oth=r_gathc[k])
            last = (l == nl - 1)
            if last:
                dst = dict(blk=lambda b: xo_v[b], res=P.res("x_out"), after=None)
            else:
                k = l % 2

                def after(b, k=k):
                    P.collective("AllGather", ALU.bypass, PAIRS, ownc[k][b].ap().opt(), gathc[k][b].ap().opt(),
                                 reads=[r_ownc[k]], writes=[r_gathc[k]])
                dst = dict(blk=lambda b, k=k: blkview(ownc[k][b].ap()), res=r_ownc[k], after=after)
            emit_layer(nc, P, l, src, dst, wts[l], cst, scr)
        P.emit_all()
        nsem = P.nsem
    return nc, nsem


def _rope_tables():
    f32 = np.float32
    freqs = np.power(f32(10000.0), (f32(-2.0) * np.arange(16, dtype=f32) / f32(32.0))).astype(f32)
    t = np.arange(S)
    row = (t // 64).astype(f32)
    col = (t % 64).astype(f32)
    tt = t.astype(f32)
    cosB = np.zeros((128, S), f32)
    sinB = np.zeros((128, S), f32)
    cosB[0:64] = 1.0
    for p in range(64, 96):
        sub = p - 64
        i = sub % 16
        ang = (tt * freqs[i]).astype(f32)
        cosB[p] = np.cos(ang)
        sinB[p] = (-np.sin(ang)) if sub < 16 else np.sin(ang)
    cosC = np.zeros((128, S), f32)
    sinC = np.zeros((128, S), f32)
    for p in range(128):
        dd = p % 64
        pos = row if dd < 32 else col
        sub = dd % 32
        i = sub % 16
        ang = (pos * freqs[i]).astype(f32)
        cosC[p] = np.cos(ang)
        sinC[p] = (-np.sin(ang)) if sub < 16 else np.sin(ang)
    return cosB, sinB, cosC, sinC


def _const_mats():
    permB = np.zeros((128, 128), np.float32)
    for p in range(64, 96):
        sub = p - 64
        partner = p + 16 if sub < 16 else p - 16
        permB[partner, p] = 1.0
    permC = np.zeros((128, 128), np.float32)
    for p in range(128):
        sub = (p % 64) % 32
        partner = p + 16 if sub < 16 else p - 16
        permC[partner, p] = 1.0
    ones2 = np.zeros((128, 128), np.float32)
    ones2[0:64, 0:64] = 1.0
    ones2[64:128, 64:128] = 1.0
    onesall = np.ones((128, 128), np.float32)
    cm = np.stack([permB, permC, ones2, onesall], axis=1)
    return np.ascontiguousarray(cm).astype(ml_dtypes.bfloat16)


def _gtab():
    lim = 8192
    delta = np.arange(-lim, lim + 1)
    mult = np.zeros(delta.shape, np.float64)
    for d in (1, 4, 16):
        mult += ((delta % d) == 0) & (np.abs(delta) // d <= 64)
    out = np.zeros((4, 128, GW + HW), np.float32)
    kk = np.arange(128)[:, None]
    u = np.arange(GW)[None, :]
    dT = kk - u + GOFF
    uh = np.arange(HW)[None, :]
    dH = 1535 - kk - uh
    for h in range(4):
        slope = 2.0 ** (-8.0 * (h + 1) / 4.0)
        for (dl, c0, c1) in ((dT, 0, GW), (dH, GW, GW + HW)):
            m = mult[dl + lim]
            out[h][:, c0:c1] = (m * np.exp(-slope * np.abs(dl))).astype(np.float32)
    return out


def _gains(l, p):
    gvec = np.zeros((128, NGV), np.float32)
    gvec[:, 0:8] = p["norm_g"][l].reshape(8, 128).T
    gvec[:, 8] = np.tile(p["a_q_norm_g"][l], 2)
    gvec[:, 9] = np.tile(p["a_k_norm_g"][l], 2)
    gvec[:, 10:12] = p["b_q_lat_norm_g"][l].reshape(2, 128).T
    gvec[:, 12] = p["b_kv_lat_norm_g"][l]
    gvec[0:96, 13] = p["b_q_norm_g"][l]
    gvec[0:96, 14] = p["b_k_norm_g"][l]
    gvec[:, 15] = np.tile(p["c_q_norm_g"][l], 2)
    gvec[:, 16] = np.tile(p["c_k_norm_g"][l], 2)
    return gvec


def _core_consts():
    cosB, sinB, cosC, sinC = _rope_tables()
    out = {}
    asc = np.arange(HALF)
    desc = S - 1 - np.arange(HALF)
    for hf in range(2):
        tok = np.concatenate([asc, desc]) if hf == 0 else np.concatenate([desc, asc])
        msk = np.zeros((128, 2), np.float32)
        msk[:, 1 - hf] = 1.0
        out[hf] = dict(cosB=np.ascontiguousarray(cosB[:, tok]), sinB=np.ascontiguousarray(sinB[:, tok]),
                       cosC=np.ascontiguousarray(cosC[:, tok]), sinC=np.ascontiguousarray(sinC[:, tok]), msk=msk)
    return out


_CACHE = {}


def make_in_maps(p, nl=DEPTH, cores=range(8)):
    if "cc" not in _CACHE:
        _CACHE["cc"] = _core_consts()
        _CACHE["cm"] = _const_mats()
        _CACHE["ident"] = np.eye(128, dtype=np.float32)
        _CACHE["gtab"] = _gtab()
    x = np.ascontiguousarray(p["x"], dtype=np.float32)
    in_maps = []
    for c in cores:
        b, hf = c // 2, c % 2
        lo = np.ascontiguousarray(x[b, 0:HALF])
        hi = np.ascontiguousarray(x[b, HALF:S][::-1])
        m = dict(x_own=(lo if hf == 0 else hi), x_oth=(hi if hf == 0 else lo),
                 cm=_CACHE["cm"], ident=_CACHE["ident"], gtab=_CACHE["gtab"])
        m.update(_CACHE["cc"][hf])
        for l in range(nl):
            m[f"w_in{l}"] = np.ascontiguousarray(p["w_in"][l], dtype=np.float32)
            m[f"w_out{l}"] = np.ascontiguousarray(p["w_out"][l], dtype=np.float32)
            m[f"wq_up{l}"] = np.ascontiguousarray(p["w_b_q_up"][l], dtype=np.float32)
            m[f"wkv_up{l}"] = np.ascontiguousarray(p["w_b_kv_up"][l], dtype=np.float32)
            m[f"gv{l}"] = _gains(l, p)
        in_maps.append(m)
    return in_maps


def kernel(**inputs):
    p = {k: np.asarray(v) for k, v in inputs.items()}
    if "nc" not in _CACHE:
        _CACHE["nc"] = build_program(DEPTH)[0]
    in_maps = make_in_maps(p, DEPTH)
    res = run_bass_kernel_spmd(_CACHE["nc"], in_maps, core_ids=list(range(8)))
    out = np.empty((4, S, D), np.float32)
    for c in range(8):
        b, hf = c // 2, c % 2
        o = res.results[c]["xo"]
        if hf == 0:
            out[b, 0:HALF] = o
        else:
            out[b, HALF:S] = o[::-1]
    return out
```

```python
import numpy as np
from contextlib import ExitStack
import ml_dtypes
import concourse.bass as bass
import concourse.mybir as mybir
from concourse.bass_utils import run_bass_kernel_spmd

F32 = mybir.dt.float32
BF16 = mybir.dt.bfloat16
AF = mybir.ActivationFunctionType
ALU = mybir.AluOpType

S = 8192
D = 1024
HALF = 4096
TB = 512
NBLK = S // TB
NOWN = HALF // TB
DEPTH = 4
IN_COLS = 2848
OFF = dict(aq=0, ak=256, av=512, ag=768, bql=1024, bkv=1280, bkpe=1408, bg=1440,
           cq=1824, ck=2208, cv=2336, cg=2464)
EPS = 1e-6
GW = 2944
GOFF = 1408
HW = 1408
NGV = 17
UC = 3


class Res:
    __slots__ = ("name", "w", "r", "dsem", "dcnt")

    def __init__(self, name):
        self.name = name
        self.w = None
        self.r = []
        self.dsem = None
        self.dcnt = 0


class Prog:
    COMPUTE = ("pe", "act", "dve", "pool")
    ALL = ("pe", "act", "dve", "pool", "sp")

    def __init__(self, nc, es):
        self.nc = nc
        self.es = es
        self.ops = {e: [] for e in self.ALL}
        self.tsem = {}
        self.tick = {}
        self.waited = {e: {} for e in self.ALL}
        self.nsem = 0
        self.epoch = 0
        self.allres = {}
        self.oldticks = []
        self.new_epoch()

    def _newsem(self, name):
        self.nsem += 1
        return self.es.enter_context(self.nc.semaphore(name))

    def new_epoch(self):
        self.epoch += 1
        for e in self.COMPUTE:
            if e in self.tsem and self.tick[e] > 0:
                self.oldticks.append((self.tsem[e], self.tick[e], e))
            self.tsem[e] = self._newsem(f"t_{e}_{self.epoch}")
            self.tick[e] = 0

    def res(self, name):
        if name not in self.allres:
            self.allres[name] = Res(name)
        return self.allres[name]

    def _need(self, eng, ev, waits):
        if ev is None:
            return
        sem, val, src = ev
        if src == "pe" and eng == "pe":
            return
        key = id(sem)
        if self.waited[eng].get(key, 0) >= val:
            return
        self.waited[eng][key] = val
        waits.append((sem, val))

    def _deps(self, eng, reads, writes):
        waits = []
        for r in reads:
            self._need(eng, r.w, waits)
        for w in writes:
            ev = w.w
            if ev is not None and ev[2] != eng and not (ev[2] == "dma" and eng == "sp"):
                self._need(eng, ev, waits)
            for rv in w.r:
                if rv[2] == eng:
                    continue
                self._need(eng, rv, waits)
        best = {}
        for sem, val in waits:
            k = id(sem)
            if k not in best or best[k][1] < val:
                best[k] = (sem, val)
        return list(best.values())

    def op(self, eng, meth, reads=(), writes=(), **kw):
        waits = self._deps(eng, reads, writes)
        self.tick[eng] += 1
        n = self.tick[eng]
        sem = self.tsem[eng]
        ev = (sem, n, eng)

        def emit(e, waits=waits, meth=meth, kw=kw, sem=sem):
            for s, v in waits:
                e.wait_ge(s, v)
            getattr(e, meth)(**kw).then_inc(sem, 1)
        self.ops[eng].append(emit)
        for r in reads:
            r.r.append(ev)
        for w in writes:
            w.w = ev
            w.r = []
        return ev

    def dma(self, q, out_ap, in_ap, reads=(), writes=()):
        waits = self._deps(q, reads, writes)
        owner = writes[0] if writes else reads[0]
        if owner.dsem is None:
            owner.dsem = self._newsem("d_" + owner.name)
        owner.dcnt += 16
        ev = (owner.dsem, owner.dcnt, "dma")

        def emit(e, waits=waits, sem=owner.dsem):
            for s, v in waits:
                e.wait_ge(s, v)
            e.dma_start(out=out_ap, in_=in_ap).then_inc(sem, 16)
        self.ops[q].append(emit)
        for r in reads:
            r.r.append(ev)
        for w in writes:
            w.w = ev
            w.r = []
        return ev

    def collective(self, kind, op, groups, in_ap, out_ap, reads, writes):
        q = "pool"
        waits = self._deps(q, reads, writes)
        owner = writes[0]
        if owner.dsem is None:
            owner.dsem = self._newsem("c_" + owner.name)
        owner.dcnt += 1
        ev = (owner.dsem, owner.dcnt, "dma")

        def emit(e, waits=waits, sem=owner.dsem):
            for s, v in waits:
                e.wait_ge(s, v)
            e.collective_compute(kind, op, replica_groups=groups, ins=[in_ap], outs=[out_ap]).then_inc(sem)
        self.ops[q].append(emit)
        for r in reads:
            r.r.append(ev)
        for w in writes:
            w.w = ev
            w.r = []
        return ev

    def barrier(self):
        evs = list(self.oldticks)
        for e in self.COMPUTE:
            if self.tick[e] > 0:
                evs.append((self.tsem[e], self.tick[e], "x"))
        for r in self.allres.values():
            if r.dsem is not None and r.dcnt > 0:
                evs.append((r.dsem, r.dcnt, "dma"))
        for eng in self.ALL:
            waits = []
            for sem, val, _ in evs:
                self._need(eng, (sem, val, "x"), waits)

            def emit(e, waits=waits):
                for s, v in waits:
                    e.wait_ge(s, v)
            self.ops[eng].append(emit)
        for r in self.allres.values():
            r.w = None
            r.r = []

    def emit_all(self):
        nc = self.nc
        ops = self.ops
        with nc.Block() as block:
            @block.tensor
            def _(e):
                for f in ops["pe"]:
                    f(e)

            @block.scalar
            def _(e):
                for f in ops["act"]:
                    f(e)

            @block.vector
            def _(e):
                for f in ops["dve"]:
                    f(e)

            @block.gpsimd
            def _(e):
                for f in ops["pool"]:
                    f(e)

            @block.sync
            def _(e):
                for f in ops["sp"]:
                    f(e)


class Bufs:
    def __init__(self, nc, P, es, prefix, uid=""):
        self.nc, self.P, self.es, self.prefix, self.uid = nc, P, es, prefix, uid
        self.b = {}
        self.i = {}

    def mk(self, name, shape, dtype, n=1, psum=False):
        lst = []
        for k in range(n):
            nm = f"{self.prefix}_{name}{k}"
            tn = f"{self.prefix}{self.uid}_{name}{k}"
            if psum:
                t = self.es.enter_context(self.nc.psum_tensor(tn, shape, dtype))
            else:
                t = self.es.enter_context(self.nc.sbuf_tensor(tn, shape, dtype))
            lst.append((t, self.P.res(nm)))
        self.b[name] = lst
        self.i[name] = 0
        return lst[0]

    def nxt(self, name):
        lst = self.b[name]
        k = self.i[name]
        self.i[name] = (k + 1) % len(lst)
        return lst[k]

    def get(self, name, k=0):
        return self.b[name][k]

    def acq(self, name):
        if not hasattr(self, "free"):
            self.free = {}
        fl = self.free.setdefault(name, list(self.b[name]))
        assert fl, f"buffer pool {name} exhausted"
        return fl.pop(0)

    def rel(self, name, item):
        self.free[name].append(item)


def run_gens(gens, width):
    gens = list(gens)
    active = []
    while gens or active:
        while gens and len(active) < width:
            active.append(gens.pop(0))
        nxt = []
        for g in active:
            try:
                next(g)
                nxt.append(g)
            except StopIteration:
                pass
        active = nxt


def emit_layer(nc, P, lid, src, dst, wts, cst, scr):
    w_in, w_out, wq_up, wkv_up, gvd = wts["w_in"], wts["w_out"], wts["wq_up"], wts["wkv_up"], wts["gv"]
    KT_A, KT_B, KT_C = scr["KT_A"], scr["KT_B"], scr["KT_C"]
    QT_A, QT_B, QT_C = scr["QT_A"], scr["QT_B"], scr["QT_C"]
    V_all, GATE, YT = scr["V"], scr["GATE"], scr["YT"]
    r_KT, r_QT, r_V, r_GATE, r_YT = (P.res("s_KT"), P.res("s_QT"), P.res("s_V"),
                                     P.res("s_GATE"), P.res("s_YT"))
    r_w = P.res("w_dram")

    def ACT(reads, writes, **kw):
        P.op("act", "activation", reads, writes, **kw)

    def MM(reads, writes, **kw):
        P.op("pe", "matmul", reads, writes, **kw)

    with ExitStack() as esL:
        BL = Bufs(nc, P, esL, "L", lid)
        Wb, r_Wb = BL.mk("Wb", [128, 8, IN_COLS], BF16)
        Wqu, r_Wqu = BL.mk("Wqu", [128, 2, 576], BF16)
        Wkk, r_Wkk = BL.mk("Wkk", [128, 6, 64], BF16)
        Wkv, r_Wkv = BL.mk("Wkv", [128, 384], BF16)
        gv, r_gv = BL.mk("gv", [128, NGV], F32)
        cm, r_cm = BL.mk("cm", [128, 4, 128], BF16)
        ident, r_id = BL.mk("ident", [128, 128], F32)
        epsb, r_eps = BL.mk("epsb", [128, 1], F32)
        msk, r_msk = BL.mk("msk", [128, 2], F32)
        permB, permC, ones2, onesall = (cm[:, 0, :], cm[:, 1, :], cm[:, 2, :], cm[:, 3, :])

        P.dma("sp", gv[:], gvd, reads=[r_w], writes=[r_gv])
        P.dma("sp", cm[:], cst["cm"], reads=[r_w], writes=[r_cm])
        P.dma("sp", ident[:], cst["ident"], reads=[r_w], writes=[r_id])
        P.dma("sp", msk[:], cst["msk"], reads=[r_w], writes=[r_msk])
        P.op("dve", "memset", [], [r_eps], ap=epsb[:], constant=EPS)

        with ExitStack() as esW:
            BW = Bufs(nc, P, esW, "W", lid)
            BW.mk("wst", [128, IN_COLS], F32, 2)
            BW.mk("wq", [128, 2, 576], F32)
            BW.mk("wk", [128, 768], F32)
            win_v = w_in.rearrange("(c p) n -> p c n", p=128)
            for c in range(8):
                st, r_st = BW.nxt("wst")
                P.dma("sp", st[:], win_v[:, c, :], reads=[r_w], writes=[r_st])
                eng = ("dve", "act")[c % 2]
                if eng == "act":
                    ACT([r_st], [r_Wb], out=Wb[:, c, :], in_=st[:], func=AF.Copy)
                else:
                    P.op(eng, "tensor_copy", [r_st], [r_Wb], out=Wb[:, c, :], in_=st[:])
            wq, r_wq = BW.get("wq")
            P.dma("sp", wq[:], wq_up.rearrange("(c p) n -> p c n", p=128), reads=[r_w], writes=[r_wq])
            P.op("dve", "tensor_copy", [r_wq], [r_Wqu], out=Wqu[:], in_=wq[:])
            wk, r_wk = BW.get("wk")
            P.dma("sp", wk[:], wkv_up, reads=[r_w], writes=[r_wk])
            wk3 = wk[:].rearrange("p (h t) -> p h t", t=128)
            P.op("dve", "tensor_copy", [r_wk], [r_Wkk], out=Wkk[:], in_=wk3[:, :, 0:64])
            P.op("dve", "tensor_copy", [r_wk], [r_Wkv], out=Wkv[:].rearrange("p (h d) -> p h d", d=64),
                 in_=wk3[:, :, 64:128])
            P.barrier()

        with ExitStack() as esP:
            B = Bufs(nc, P, esP, "P", lid)
            B.mk("bank", [128, 512], F32, 8, psum=True)
            B.mk("xt", [128, 4, D], F32, 2)
            if src["gath"] is not None:
                B.mk("xg", [128, 4, D], F32, 1)
            B.mk("junk", [128, D], BF16)
            B.mk("ss", [128, 4], F32, 2)
            B.mk("l4", [128, 4], F32, 2)
            B.mk("r4", [128, 4], F32, 2)
            B.mk("hT", [128, 8, TB], BF16, 2)
            B.mk("tab", [128, 4, TB], F32, 2)
            B.mk("sq", [128, TB], BF16, 5)
            B.mk("ln", [128, TB], F32, 4)
            B.mk("qn", [128, TB], BF16, 4)
            B.mk("t1", [128, TB], F32, 4)
            B.mk("t2", [128, TB], F32, 4)
            B.mk("ob", [128, TB], BF16, 6)
            B.mk("qln", [128, 2, TB], BF16, 2)
            B.mk("kvn", [128, TB], BF16, 2)
            B.mk("ge", [128, TB], F32, 4)
            B.mk("go", [128, TB], F32, 4)
            B.mk("vst", [128, 4, 768], BF16, 1)
            PW = 4

            def proj_group(hT, r_hT, col0, M, out_rows=0, bk=None):
                if bk is None:
                    bk = B.acq("bank")
                bT, bR = bk
                for c in range(8):
                    MM([r_Wb, r_hT], [bR], out=bT[out_rows:out_rows + M, :], lhsT=Wb[:, c, col0:col0 + M],
                       rhs=hT[:, c, :], start=(c == 0), stop=(c == 7))
                return bk

            def square(bk, M):
                bT, bR = bk
                sqp = B.acq("sq")
                sq, r_sq = sqp
                ACT([bR], [r_sq], out=sq[0:M, :], in_=bT[0:M, :], func=AF.Square)
                return sqp

            def rstd_gen(sqs, M, ones_ap, dk, out):
                pn = B.acq("bank")
                pnT, pnR = pn
                for i, (sq, r_sq) in enumerate(sqs):
                    MM([r_sq, r_cm], [pnR], out=pnT[0:M, :], lhsT=ones_ap, rhs=sq[0:M, :],
                       start=(i == 0), stop=(i == len(sqs) - 1))
                for sqp in sqs:
                    B.rel("sq", sqp)
                yield
                lnp = B.acq("ln")
                ln, r_ln = lnp
                ACT([pnR, r_eps], [r_ln], out=ln[0:M, :], in_=pnT[0:M, :], func=AF.Ln,
                    scale=1.0 / dk, bias=epsb[0:M, :])
                B.rel("bank", pn)
                yield
                ACT([r_ln], [r_ln], out=ln[0:M, :], in_=ln[0:M, :], func=AF.Exp, scale=-0.5)
                out.append(lnp)
                yield

            def headnorm_gen(mk_bank, M, dk, gcol, ones_ap, rope, tab, r_tab, stores, r_dst):
                bk = mk_bank()
                bT, bR = bk
                yield
                sqp = square(bk, M)
                yield
                res = []
                yield from rstd_gen([sqp], M, ones_ap, dk, res)
                rs, r_rs = res[0]
                qnp = B.acq("qn")
                qn, r_qn = qnp
                P.op("dve", "scalar_tensor_tensor", [bR, r_rs, r_gv], [r_qn], out=qn[0:M, :], in0=bT[0:M, :],
                     scalar=gv[0:M, gcol:gcol + 1], in1=rs[0:M, :], op0=ALU.mult, op1=ALU.mult)
                B.rel("ln", res[0])
                yield
                if rope is None:
                    B.rel("bank", bk)
                    for (dstap, lo, hi) in stores:
                        P.dma("sp", dstap, qn[lo:hi, :], reads=[r_qn], writes=[r_dst])
                    B.rel("qn", qnp)
                    yield
                    return
                perm = permB[0:M, 0:M] if rope == "B" else permC
                ci, si = (0, 1) if rope == "B" else (2, 3)
                MM([r_qn, r_cm], [bR], out=bT[0:M, :], lhsT=perm, rhs=qn[0:M, :], start=True, stop=True)
                t1p = B.acq("t1")
                t1, r_t1 = t1p
                P.op("dve", "tensor_tensor", [r_qn, r_tab], [r_t1], out=t1[0:M, :], in0=qn[0:M, :],
                     in1=tab[0:M, ci, :], op=ALU.mult)
                B.rel("qn", qnp)
                yield
                t2p = B.acq("t2")
                t2, r_t2 = t2p
                P.op("dve", "tensor_tensor", [bR, r_tab], [r_t2], out=t2[0:M, :], in0=bT[0:M, :],
                     in1=tab[0:M, si, :], op=ALU.mult)
                B.rel("bank", bk)
                yield
                obp = B.acq("ob")
                fin, r_fin = obp
                P.op("dve", "tensor_tensor", [r_t1, r_t2], [r_fin], out=fin[0:M, :], in0=t1[0:M, :],
                     in1=t2[0:M, :], op=ALU.add)
                B.rel("t1", t1p)
                B.rel("t2", t2p)
                yield
                for (dstap, lo, hi) in stores:
                    P.dma("sp", dstap, fin[lo:hi, :], reads=[r_fin], writes=[r_dst])
                B.rel("ob", obp)
                yield

            def gate_gen(hT, r_hT, col0, y0, t0):
                bk = proj_group(hT, r_hT, col0, 128)
                yield
                gep = B.acq("ge")
                ge, r_ge = gep
                ACT([bk[1]], [r_ge], out=ge[:], in_=bk[0][:], func=AF.Exp, scale=-1.0)
                yield
                P.op("dve", "tensor_scalar", [r_ge], [r_ge], out=ge[:], in0=ge[:], scalar1=1.0, scalar2=None, op0=ALU.add)
                yield
                P.op("dve", "reciprocal", [r_ge], [r_ge], out=ge[:], in_=ge[:])
                yield
                gop = B.acq("go")
                go, r_go = gop
                P.op("dve", "tensor_tensor", [bk[1], r_ge], [r_go], out=go[:], in0=bk[0][:], in1=ge[:], op=ALU.mult)
                B.rel("bank", bk)
                B.rel("ge", gep)
                yield
                P.dma("sp", GATE[y0:y0 + 128, t0:t0 + TB], go[:], reads=[r_go], writes=[r_GATE])
                B.rel("go", gop)
                yield

            V_v = V_all.rearrange("(n j p) c -> n p j c", j=4, p=128)

            def load_x(blk):
                xt, r_xt = B.nxt("xt")
                ent = dict(xt=xt, r_xt=r_xt, blend=None)
                if blk < NOWN:
                    P.dma("sp", xt[:], src["own_blk"](blk), reads=[src["r_own"]], writes=[r_xt])
                elif src["gath"] is None:
                    P.dma("sp", xt[:], src["oth_blk"](blk - NOWN), reads=[src["r_oth"]], writes=[r_xt])
                else:
                    xg, r_xg = B.nxt("xg")
                    P.dma("sp", xt[:], src["ga_blk"](blk - NOWN), reads=[src["r_oth"]], writes=[r_xt])
                    P.dma("sp", xg[:], src["gb_blk"](blk - NOWN), reads=[src["r_oth"]], writes=[r_xg])
                    ent["blend"] = (xg, r_xg)
                return ent

            def load_tab(blk):
                tab, r_tab = B.nxt("tab")
                t0 = blk * TB
                for i, nm in enumerate(("cosB", "sinB", "cosC", "sinC")):
                    P.dma("sp", tab[:, i, :], cst[nm][:, t0:t0 + TB], reads=[r_w], writes=[r_tab])
                return tab, r_tab

            def xprep_gen(ent):
                xt, r_xt = ent["xt"], ent["r_xt"]
                if ent["blend"] is not None:
                    xg, r_xg = ent["blend"]
                    xt2 = xt[:].rearrange("p j d -> p (j d)")
                    xg2 = xg[:].rearrange("p j d -> p (j d)")
                    ACT([r_xt, r_msk], [r_xt], out=xt2, in_=xt2, func=AF.Copy, scale=msk[:, 0:1])
                    yield
                    P.op("dve", "scalar_tensor_tensor", [r_xg, r_xt, r_msk], [r_xt], out=xt2, in0=xg2,
                         scalar=msk[:, 1:2], in1=xt2, op0=ALU.mult, op1=ALU.add)
                    yield
                junk, r_junk = B.get("junk")
                ss, r_ss = B.nxt("ss")
                for j in range(4):
                    ACT([r_xt], [r_junk, r_ss], out=junk[:], in_=xt[:, j, :], func=AF.Square, accum_out=ss[:, j:j + 1])
                yield
                l4, r_l4 = B.nxt("l4")
                ACT([r_ss, r_eps], [r_l4], out=l4[:], in_=ss[:], func=AF.Ln, scale=1.0 / D, bias=epsb[:])
                yield
                r4, r_r4 = B.nxt("r4")
                ACT([r_l4], [r_r4], out=r4[:], in_=l4[:], func=AF.Exp, scale=-0.5)
                yield
                for j in range(4):
                    if j % 2 == 0:
                        ACT([r_r4, r_xt], [r_xt], out=xt[:, j, :], in_=xt[:, j, :], func=AF.Copy, scale=r4[:, j:j + 1])
                    else:
                        P.op("dve", "tensor_scalar", [r_r4, r_xt], [r_xt], out=xt[:, j, :], in0=xt[:, j, :],
                             scalar1=r4[:, j:j + 1], scalar2=None, op0=ALU.mult)
                yield
                hT, r_hT = B.nxt("hT")
                ent["hT"] = (hT, r_hT)
                for c in range(8):
                    ptp = B.acq("bank")
                    ptT, ptR = ptp
                    for j in range(4):
                        P.op("pe", "transpose", [r_xt, r_id], [ptR], out=ptT[:, j * 128:(j + 1) * 128],
                             in_=xt[:, j, c * 128:(c + 1) * 128], identity=ident[:])
                    yield
                    if c % 2 == 0:
                        P.op("dve", "tensor_scalar", [ptR, r_gv], [r_hT], out=hT[:, c, :], in0=ptT[:],
                             scalar1=gv[:, c:c + 1], scalar2=None, op0=ALU.mult)
                    else:
                        ACT([ptR, r_gv], [r_hT], out=hT[:, c, :], in_=ptT[:], func=AF.Copy, scale=gv[:, c:c + 1])
                    B.rel("bank", ptp)
                    yield

            ents = {0: load_x(0)}
            if NBLK > 1:
                ents[1] = load_x(1)
            tabs = {0: load_tab(0)}
            run_gens([xprep_gen(ents[0])], 1)
            for blk in range(NBLK):
                own = blk < NOWN
                t0 = blk * TB
                hT, r_hT = ents[blk]["hT"]
                tab, r_tab = tabs[blk]
                if blk + 1 < NBLK:
                    tabs[blk + 1] = load_tab(blk + 1)

                kvn, r_kvn = B.nxt("kvn")
                qln, r_qln = B.nxt("qln")

                def kvlat_gen():
                    bk = proj_group(hT, r_hT, OFF["bkv"], 128)
                    yield
                    sqp = square(bk, 128)
                    yield
                    res = []
                    yield from rstd_gen([sqp], 128, onesall, 128.0, res)
                    rs, r_rs = res[0]
                    P.op("dve", "scalar_tensor_tensor", [bk[1], r_rs, r_gv], [r_kvn], out=kvn[:], in0=bk[0][:],
                         scalar=gv[:, 12:13], in1=rs[:], op0=ALU.mult, op1=ALU.mult)
                    B.rel("ln", res[0])
                    B.rel("bank", bk)
                    yield

                def qlat_gen():
                    bq = [proj_group(hT, r_hT, OFF["bql"] + 128 * c, 128) for c in range(2)]
                    yield
                    sqs = [square(bq[0], 128), square(bq[1], 128)]
                    yield
                    res = []
                    yield from rstd_gen(sqs, 128, onesall, 256.0, res)
                    rs, r_rs = res[0]
                    for c in range(2):
                        P.op("dve", "scalar_tensor_tensor", [bq[c][1], r_rs, r_gv], [r_qln], out=qln[:, c, :],
                             in0=bq[c][0][:], scalar=gv[:, 10 + c:11 + c], in1=rs[:], op0=ALU.mult, op1=ALU.mult)
                    B.rel("ln", res[0])
                    B.rel("bank", bq[0])
                    B.rel("bank", bq[1])
                    yield

                def pg(col0):
                    return lambda: proj_group(hT, r_hT, col0, 128)

                def two(dst, g):
                    return [(dst[(2 * g) * 64:(2 * g + 1) * 64, t0:t0 + TB], 0, 64),
                            (dst[(2 * g + 1) * 64:(2 * g + 2) * 64, t0:t0 + TB], 64, 128)]

                gens = [kvlat_gen()]
                if own:
                    gens.append(qlat_gen())
                run_gens(gens, 2)
                gens = []
                if blk + 1 < NBLK:
                    gens.append(xprep_gen(ents[blk + 1]))
                for g in range(2):
                    gens.append(headnorm_gen(pg(OFF["ak"] + 128 * g), 128, 64.0, 9, ones2, None, tab, r_tab, two(KT_A, g), r_KT))
                gens.append(headnorm_gen(pg(OFF["ck"]), 128, 64.0, 16, ones2, "C", tab, r_tab, two(KT_C, 0), r_KT))
                if own:
                    for g in range(2):
                        gens.append(headnorm_gen(pg(OFF["aq"] + 128 * g), 128, 64.0, 8, ones2, None, tab, r_tab, two(QT_A, g), r_QT))
                    for g in range(3):
                        gens.append(headnorm_gen(pg(OFF["cq"] + 128 * g), 128, 64.0, 15, ones2, "C", tab, r_tab, two(QT_C, g), r_QT))
                    for (gcol, yrow, ng) in ((OFF["ag"], 0, 2), (OFF["bg"], 256, 3), (OFF["cg"], 640, 3)):
                        for g in range(ng):
                            gens.append(gate_gen(hT, r_hT, gcol + 128 * g, yrow + 128 * g, t0))

                def v_gen():
                    vst, r_vst = B.nxt("vst")
                    for j in range(4):
                        pvp = B.acq("bank")
                        pvT, pvR = pvp
                        for (cols, n, o) in ((OFF["av"], 256, 0), (OFF["cv"], 128, 256)):
                            for c in range(8):
                                MM([r_hT, r_Wb], [pvR], out=pvT[:, o:o + n], lhsT=hT[:, c, j * 128:(j + 1) * 128],
                                   rhs=Wb[:, c, cols:cols + n], start=(c == 0), stop=(c == 7))
                        yield
                        ACT([pvR], [r_vst], out=vst[:, j, 0:384], in_=pvT[:, 0:384], func=AF.Copy)
                        B.rel("bank", pvp)
                        pv2p = B.acq("bank")
                        pv2T, pv2R = pv2p
                        MM([r_kvn, r_Wkv], [pv2R], out=pv2T[:, 0:384], lhsT=kvn[:, j * 128:(j + 1) * 128],
                           rhs=Wkv[:], start=True, stop=True)
                        yield
                        P.op("dve", "tensor_copy", [pv2R], [r_vst], out=vst[:, j, 384:768], in_=pv2T[:, 0:384])
                        B.rel("bank", pv2p)
                        yield
                    P.dma("sp", V_v[blk], vst[:], reads=[r_vst], writes=[r_V])
                    yield
                gens.insert(1, v_gen())

                def bk_bank(h):
                    def f():
                        bkh = B.acq("bank")
                        MM([r_Wkk, r_kvn], [bkh[1]], out=bkh[0][0:64, :], lhsT=Wkk[:, h, :], rhs=kvn[:], start=True, stop=True)
                        proj_group(hT, r_hT, OFF["bkpe"], 32, out_rows=64, bk=bkh)
                        return bkh
                    return f

                def bq_bank(h):
                    def f():
                        bkh = B.acq("bank")
                        for c in range(2):
                            MM([r_Wqu, r_qln], [bkh[1]], out=bkh[0][0:96, :], lhsT=Wqu[:, c, h * 96:(h + 1) * 96],
                               rhs=qln[:, c, :], start=(c == 0), stop=(c == 1))
                        return bkh
                    return f

                for h in range(6):
                    gens.append(headnorm_gen(bk_bank(h), 96, 96.0, 14, onesall[0:96, 0:96], "B", tab, r_tab,
                                             [(KT_B[h * 96:(h + 1) * 96, t0:t0 + TB], 0, 96)], r_KT))
                    if own:
                        gens.append(headnorm_gen(bq_bank(h), 96, 96.0, 13, onesall[0:96, 0:96], "B", tab, r_tab,
                                                 [(QT_B[h * 96:(h + 1) * 96, t0:t0 + TB], 0, 96)], r_QT))
                run_gens(gens, PW)
                if blk + 2 < NBLK:
                    ents[blk + 2] = load_x(blk + 2)
            P.barrier()

        with ExitStack() as esA:
            B = Bufs(nc, P, esA, "A", lid)
            B.mk("S", [128, 512 * UC], F32, 2, psum=True)
            B.mk("O", [128, 512], F32, 2, psum=True)
            B.mk("kT", [128, S], BF16, 2)
            B.mk("vA", [128, 64, 128], BF16, 2)
            B.mk("qT", [128, HALF], BF16, 2)
            B.mk("Gt", [128, GW + HW], BF16, 2)
            B.mk("gt", [64, 512], F32, 2)
            B.mk("pT", [128, 512 * UC], BF16, 3)
            B.mk("rd", [128, 512], F32, 2)
            B.mk("rd0", [64, 512], F32, 2)
            B.mk("yf", [64, 512], F32, 2)
            B.mk("yb", [64, 512], BF16, 2)
            for k in range(2):
                vt, vr = B.get("vA", k)
                P.op("pool", "memset", [], [vr], ap=vt[:, :, 64:128], constant=1.0)

            jobs = []
            for h in range(4):
                jobs.append(dict(kv=("A", h), kt=KT_A[h * 64:(h + 1) * 64, :], dk=64, vcol=h * 64,
                                 q=QT_A[h * 64:(h + 1) * 64, :], scale=0.125, yrow=h * 64, isA=True, gi=h))
            for h in range(6):
                jobs.append(dict(kv=("B", h), kt=KT_B[h * 96:(h + 1) * 96, :], dk=96, vcol=384 + 64 * h,
                                 q=QT_B[h * 96:(h + 1) * 96, :], scale=96.0 ** -0.5, yrow=256 + 64 * h, isA=False))
            for h in range(6):
                kvh = h // 3
                jobs.append(dict(kv=("C", kvh), kt=KT_C[kvh * 64:(kvh + 1) * 64, :], dk=64, vcol=256 + 64 * kvh,
                                 q=QT_C[h * 64:(h + 1) * 64, :], scale=0.125, yrow=640 + 64 * h, isA=False))

            V_cv = V_all.rearrange("(c p) d -> p c d", p=128)
            state = dict(kvkey=None, kvbuf=None)

            def load_job(job):
                if job["kv"] != state["kvkey"]:
                    kT, r_kT = B.nxt("kT")
                    vA, r_vA = B.nxt("vA")
                    dk = job["dk"]
                    for q4 in range(4):
                        P.dma("sp", kT[0:dk, q4 * 2048:(q4 + 1) * 2048], job["kt"][:, q4 * 2048:(q4 + 1) * 2048],
                              reads=[r_KT], writes=[r_kT])
                    vc = job["vcol"]
                    for q8 in range(8):
                        P.dma("sp", vA[:, q8 * 8:(q8 + 1) * 8, 0:64], V_cv[:, q8 * 8:(q8 + 1) * 8, vc:vc + 64],
                              reads=[r_V], writes=[r_vA])
                    state["kvkey"] = job["kv"]
                    state["kvbuf"] = (kT, r_kT, vA, r_vA)
                job["kvbuf"] = state["kvbuf"]
                qT, r_qT = B.nxt("qT")
                P.dma("sp", qT[0:job["dk"], :], job["q"], reads=[r_QT], writes=[r_qT])
                job["qbuf"] = (qT, r_qT)
                if job["isA"]:
                    Gt, r_Gt = B.nxt("Gt")
                    P.dma("sp", Gt[:], cst["gtab"][job["gi"]], reads=[r_w], writes=[r_Gt])
                    job["gbuf"] = (Gt, r_Gt)

            units = []
            for ji, job in enumerate(jobs):
                cnt = 0
                for qb in range(NOWN):
                    q0 = qb * TB
                    if job["isA"]:
                        ch = []
                        for cc in range(max(0, q0 - 1024) // 128, min(HALF, q0 + TB + 1024) // 128):
                            c = (cc * 128 - q0) // 128
                            ch.append((cc, GOFF - 128 * c))
                        for k in range(8):
                            m = 639 + 128 * k
                            s0 = S - 1 - q0 - m
                            if 0 <= s0 < HALF:
                                assert s0 % 128 == 0
                                ch.append(((HALF + s0) // 128, GW + (1535 - m)))
                    else:
                        ch = [(cc, None) for cc in range(S // 128)]
                    for i in range(0, len(ch), UC):
                        units.append(dict(ji=ji, qb=qb, ch=ch[i:i + UC], first=(i == 0),
                                          last=(i + UC >= len(ch)), idx=cnt))
                        cnt += 1

            def stage1(u):
                job = jobs[u["ji"]]
                if u["idx"] == 0 and u["ji"] == 0:
                    load_job(jobs[0])
                if u["idx"] == 2 and u["ji"] + 1 < len(jobs):
                    load_job(jobs[u["ji"] + 1])
                q0 = u["qb"] * TB
                if u["first"]:
                    gt, r_gt = B.nxt("gt")
                    P.dma("sp", gt[:], GATE[job["yrow"]:job["yrow"] + 64, q0:q0 + TB], reads=[r_GATE], writes=[r_gt])
                    job["cur"] = ((gt, r_gt), B.nxt("O"))
                u["gt"], u["O"] = job["cur"]
                kT, r_kT, vA, r_vA = job["kvbuf"]
                qT, r_qT = job["qbuf"]
                dk = job["dk"]
                sT, r_S = B.nxt("S")
                u["S"] = (sT, r_S)
                for k, (cc, _) in enumerate(u["ch"]):
                    MM([r_kT, r_qT], [r_S], out=sT[:, k * 512:(k + 1) * 512], lhsT=kT[0:dk, cc * 128:(cc + 1) * 128],
                       rhs=qT[0:dk, q0:q0 + TB], start=True, stop=True)

            def stage2(u):
                job = jobs[u["ji"]]
                sT, r_S = u["S"]
                pT, r_pT = B.nxt("pT")
                r_pTc = P.res(r_pT.name + "_c")
                u["pT"] = (pT, r_pT, r_pTc)
                w = 512 * len(u["ch"])
                ACT([r_S], [r_pT, r_pTc], out=pT[:, 0:w], in_=sT[:, 0:w], func=AF.Exp, scale=job["scale"])
                if job["isA"]:
                    Gt, r_Gt = job["gbuf"]
                    for k, (cc, u0) in enumerate(u["ch"]):
                        eng, rr = ("dve", r_pTc) if k == 2 else ("dve", r_pT)
                        P.op(eng, "tensor_tensor", [rr, r_Gt], [rr], out=pT[:, k * 512:(k + 1) * 512],
                             in0=pT[:, k * 512:(k + 1) * 512], in1=Gt[:, u0:u0 + 512], op=ALU.mult)

            def stage3(u):
                job = jobs[u["ji"]]
                kT, r_kT, vA, r_vA = job["kvbuf"]
                pT, r_pT, r_pTc = u["pT"]
                oT, r_O = u["O"]
                nch = len(u["ch"])
                for k, (cc, _) in enumerate(u["ch"]):
                    MM([r_vA, (r_pTc if k == 2 else r_pT)], [r_O], out=oT[:], lhsT=vA[:, cc, :], rhs=pT[:, k * 512:(k + 1) * 512],
                       start=(u["first"] and k == 0), stop=(u["last"] and k == nch - 1))
                if u["last"]:
                    gt, r_gt = u["gt"]
                    rd, r_rd = B.nxt("rd")
                    P.op("dve", "reciprocal", [r_O], [r_rd], out=rd[64:128, :], in_=oT[64:128, :])
                    rd0, r_rd0 = B.nxt("rd0")
                    P.op("dve", "tensor_copy", [r_rd], [r_rd0], out=rd0[:], in_=rd[64:128, :])
                    yf, r_yf = B.nxt("yf")
                    P.op("dve", "tensor_tensor", [r_O, r_rd0], [r_yf], out=yf[:], in0=oT[0:64, :], in1=rd0[:], op=ALU.mult)
                    yb, r_yb = B.nxt("yb")
                    P.op("dve", "tensor_tensor", [r_yf, r_gt], [r_yb], out=yb[:], in0=yf[:], in1=gt[:], op=ALU.mult)
                    q0 = u["qb"] * TB
                    P.dma("sp", YT[job["yrow"]:job["yrow"] + 64, q0:q0 + TB], yb[:], reads=[r_yb], writes=[r_YT])

            n = len(units)
            for i in range(n + 2):
                if i < n:
                    stage1(units[i])
                if 0 <= i - 1 < n:
                    stage2(units[i - 1])
                if 0 <= i - 2 < n:
                    stage3(units[i - 2])
            P.barrier()

        with ExitStack() as esO:
            B = Bufs(nc, P, esO, "O", lid)
            B.mk("bank", [128, 512], F32, 8, psum=True)
            Wo, r_Wo = B.mk("Wo", [128, 8, D], BF16)
            B.mk("wost", [128, D], F32, 2)
            B.mk("yT", [128, 8, TB], BF16, 2)
            B.mk("xo", [128, 4, D], F32, 2)
            wo_v = w_out.rearrange("(c p) n -> p c n", p=128)
            for c in range(8):
                st, r_st = B.nxt("wost")
                P.dma("sp", st[:], wo_v[:, c, :], reads=[r_w], writes=[r_st])
                if c % 2 == 0:
                    P.op("dve", "tensor_copy", [r_st], [r_Wo], out=Wo[:, c, :], in_=st[:])
                else:
                    ACT([r_st], [r_Wo], out=Wo[:, c, :], in_=st[:], func=AF.Copy)
            YT_v = YT.rearrange("(c p) t -> p c t", p=128)
            for blk in range(NOWN):
                q0 = blk * TB
                yT, r_yT = B.nxt("yT")
                P.dma("sp", yT[:], YT_v[:, :, q0:q0 + TB], reads=[r_YT], writes=[r_yT])
                xo, r_xo = B.nxt("xo")
                P.dma("sp", xo[:], src["own_blk"](blk), reads=[src["r_own"]], writes=[r_xo])
                for j in range(4):
                    for hh in range(2):
                        bT, bR = B.nxt("bank")
                        for c in range(8):
                            MM([r_yT, r_Wo], [bR], out=bT[:], lhsT=yT[:, c, j * 128:(j + 1) * 128],
                               rhs=Wo[:, c, hh * 512:(hh + 1) * 512], start=(c == 0), stop=(c == 7))
                        P.op("dve", "tensor_tensor", [bR, r_xo], [r_xo], out=xo[:, j, hh * 512:(hh + 1) * 512],
                             in0=bT[:], in1=xo[:, j, hh * 512:(hh + 1) * 512], op=ALU.add)
                P.dma("sp", dst["blk"](blk), xo[:], reads=[r_xo], writes=[dst["res"]])
                if dst["after"] is not None:
                    dst["after"](blk)
            P.barrier()


PAIRS = [[0, 1], [2, 3], [4, 5], [6, 7]]


def build_program(nl=DEPTH, dbg=False):
    nc = bass.Bass("TRN2", target_bir_lowering=False)
    x_own = nc.dram_tensor("x_own", [HALF, D], F32, kind="ExternalInput").ap()
    x_oth = nc.dram_tensor("x_oth", [HALF, D], F32, kind="ExternalInput").ap()
    xo = nc.dram_tensor("xo", [HALF, D], F32, kind="ExternalOutput").ap()
    wts = []
    for l in range(nl):
        wts.append(dict(
            w_in=nc.dram_tensor(f"w_in{l}", [D, IN_COLS], F32, kind="ExternalInput").ap(),
            w_out=nc.dram_tensor(f"w_out{l}", [D, D], F32, kind="ExternalInput").ap(),
            wq_up=nc.dram_tensor(f"wq_up{l}", [256, 576], F32, kind="ExternalInput").ap(),
            wkv_up=nc.dram_tensor(f"wkv_up{l}", [128, 768], F32, kind="ExternalInput").ap(),
            gv=nc.dram_tensor(f"gv{l}", [128, NGV], F32, kind="ExternalInput").ap(),
        ))
    cst = dict(
        cm=nc.dram_tensor("cm", [128, 4, 128], BF16, kind="ExternalInput").ap(),
        ident=nc.dram_tensor("ident", [128, 128], F32, kind="ExternalInput").ap(),
        gtab=nc.dram_tensor("gtab", [4, 128, GW + HW], BF16, kind="ExternalInput").ap(),
        msk=nc.dram_tensor("msk", [128, 2], F32, kind="ExternalInput").ap(),
    )
    for nm in ("cosB", "sinB", "cosC", "sinC"):
        cst[nm] = nc.dram_tensor(nm, [128, S], F32, kind="ExternalInput").ap()
    kind = "ExternalOutput" if dbg else "Internal"
    scr = dict(
        KT_A=nc.dram_tensor("KT_A", [256, S], BF16, kind=kind).ap(),
        KT_B=nc.dram_tensor("KT_B", [576, S], BF16, kind=kind).ap(),
        KT_C=nc.dram_tensor("KT_C", [128, S], BF16, kind=kind).ap(),
        QT_A=nc.dram_tensor("QT_A", [256, HALF], BF16, kind=kind).ap(),
        QT_B=nc.dram_tensor("QT_B", [576, HALF], BF16, kind=kind).ap(),
        QT_C=nc.dram_tensor("QT_C", [384, HALF], BF16, kind=kind).ap(),
        V=nc.dram_tensor("V_all", [S, 768], BF16, kind=kind).ap(),
        GATE=nc.dram_tensor("GATE", [D, HALF], F32, kind=kind).ap(),
        YT=nc.dram_tensor("YT", [D, HALF], BF16, kind=kind).ap(),
    )
    ownc = [[nc.dram_tensor(f"ownc{i}_{j}", [TB, D], F32) for j in range(NOWN)] for i in range(2)]
    gathc = [[nc.dram_tensor(f"gathc{i}_{j}", [2 * TB, D], F32) for j in range(NOWN)] for i in range(2)]

    def blkview(ap):
        return ap.rearrange("(j p) d -> p j d", p=128)

    x_own_v = x_own.rearrange("(n j p) d -> n p j d", j=4, p=128)
    x_oth_v = x_oth.rearrange("(n j p) d -> n p j d", j=4, p=128)
    xo_v = xo.rearrange("(n j p) d -> n p j d", j=4, p=128)
    with ExitStack() as es:
        P = Prog(nc, es)
        r_in = P.res("x_in")
        r_ownc = [P.res(f"ownc{i}") for i in range(2)]
        r_gathc = [P.res(f"gathc{i}") for i in range(2)]
        for l in range(nl):
            if l > 0:
                P.new_epoch()
            if l == 0:
                src = dict(own_blk=lambda b: x_own_v[b], r_own=r_in, oth_blk=lambda j: x_oth_v[j], r_oth=r_in, gath=None)
            else:
                k = (l - 1) % 2
                src = dict(own_blk=lambda b, k=k: blkview(ownc[k][b].ap()), r_own=r_ownc[k], gath=True,
                           ga_blk=lambda j, k=k: blkview(gathc[k][j].ap()[0:TB, :]),
                           gb_blk=lambda j, k=k: blkview(gathc[k][j].ap()[TB:2 * TB, :]), r_oth=r_gathc[k])
            last = (l == nl - 1)
            if last:
                dst = dict(blk=lambda b: xo_v[b], res=P.res("x_out"), after=None)
            else:
                k = l % 2

                def after(b, k=k):
                    P.collective("AllGather", ALU.bypass, PAIRS, ownc[k][b].ap().opt(), gathc[k][b].ap().opt(),
                                 reads=[r_ownc[k]], writes=[r_gathc[k]])
                dst = dict(blk=lambda b, k=k: blkview(ownc[k][b].ap()), res=r_ownc[k], after=after)
            emit_layer(nc, P, l, src, dst, wts[l], cst, scr)
        P.emit_all()
        nsem = P.nsem
    return nc, nsem


def _rope_tables():
    f32 = np.float32
    freqs = np.power(f32(10000.0), (f32(-2.0) * np.arange(16, dtype=f32) / f32(32.0))).astype(f32)
    t = np.arange(S)
    row = (t // 64).astype(f32)
    col = (t % 64).astype(f32)
    tt = t.astype(f32)
    cosB = np.zeros((128, S), f32)
    sinB = np.zeros((128, S), f32)
    cosB[0:64] = 1.0
    for p in range(64, 96):
        sub = p - 64
        i = sub % 16
        ang = (tt * freqs[i]).astype(f32)
        cosB[p] = np.cos(ang)
        sinB[p] = (-np.sin(ang)) if sub < 16 else np.sin(ang)
    cosC = np.zeros((128, S), f32)
    sinC = np.zeros((128, S), f32)
    for p in range(128):
        dd = p % 64
        pos = row if dd < 32 else col
        sub = dd % 32
        i = sub % 16
        ang = (pos * freqs[i]).astype(f32)
        cosC[p] = np.cos(ang)
        sinC[p] = (-np.sin(ang)) if sub < 16 else np.sin(ang)
    return cosB, sinB, cosC, sinC


def _const_mats():
    permB = np.zeros((128, 128), np.float32)
    for p in range(64, 96):
        sub = p - 64
        partner = p + 16 if sub < 16 else p - 16
        permB[partner, p] = 1.0
    permC = np.zeros((128, 128), np.float32)
    for p in range(128):
        sub = (p % 64) % 32
        partner = p + 16 if sub < 16 else p - 16
        permC[partner, p] = 1.0
    ones2 = np.zeros((128, 128), np.float32)
    ones2[0:64, 0:64] = 1.0
    ones2[64:128, 64:128] = 1.0
    onesall = np.ones((128, 128), np.float32)
    cm = np.stack([permB, permC, ones2, onesall], axis=1)
    return np.ascontiguousarray(cm).astype(ml_dtypes.bfloat16)


def _gtab():
    lim = 8192
    delta = np.arange(-lim, lim + 1)
    mult = np.zeros(delta.shape, np.float64)
    for d in (1, 4, 16):
        mult += ((delta % d) == 0) & (np.abs(delta) // d <= 64)
    out = np.zeros((4, 128, GW + HW), np.float32)
    kk = np.arange(128)[:, None]
    u = np.arange(GW)[None, :]
    dT = kk - u + GOFF
    uh = np.arange(HW)[None, :]
    dH = 1535 - kk - uh
    for h in range(4):
        slope = 2.0 ** (-8.0 * (h + 1) / 4.0)
        for (dl, c0, c1) in ((dT, 0, GW), (dH, GW, GW + HW)):
            m = mult[dl + lim]
            out[h][:, c0:c1] = (m * np.exp(-slope * np.abs(dl))).astype(np.float32)
    return out.astype(ml_dtypes.bfloat16)


def _gains(l, p):
    gvec = np.zeros((128, NGV), np.float32)
    gvec[:, 0:8] = p["norm_g"][l].reshape(8, 128).T
    gvec[:, 8] = np.tile(p["a_q_norm_g"][l], 2)
    gvec[:, 9] = np.tile(p["a_k_norm_g"][l], 2)
    gvec[:, 10:12] = p["b_q_lat_norm_g"][l].reshape(2, 128).T
    gvec[:, 12] = p["b_kv_lat_norm_g"][l]
    gvec[0:96, 13] = p["b_q_norm_g"][l]
    gvec[0:96, 14] = p["b_k_norm_g"][l]
    gvec[:, 15] = np.tile(p["c_q_norm_g"][l], 2)
    gvec[:, 16] = np.tile(p["c_k_norm_g"][l], 2)
    return gvec


def _core_consts():
    cosB, sinB, cosC, sinC = _rope_tables()
    out = {}
    asc = np.arange(HALF)
    desc = S - 1 - np.arange(HALF)
    for hf in range(2):
        tok = np.concatenate([asc, desc]) if hf == 0 else np.concatenate([desc, asc])
        msk = np.zeros((128, 2), np.float32)
        msk[:, 1 - hf] = 1.0
        out[hf] = dict(cosB=np.ascontiguousarray(cosB[:, tok]), sinB=np.ascontiguousarray(sinB[:, tok]),
                       cosC=np.ascontiguousarray(cosC[:, tok]), sinC=np.ascontiguousarray(sinC[:, tok]), msk=msk)
    return out


_CACHE = {}


def make_in_maps(p, nl=DEPTH, cores=range(8)):
    if "cc" not in _CACHE:
        _CACHE["cc"] = _core_consts()
        _CACHE["cm"] = _const_mats()
        _CACHE["ident"] = np.eye(128, dtype=np.float32)
        _CACHE["gtab"] = _gtab()
    x = np.ascontiguousarray(p["x"], dtype=np.float32)
    in_maps = []
    for c in cores:
        b, hf = c // 2, c % 2
        lo = np.ascontiguousarray(x[b, 0:HALF])
        hi = np.ascontiguousarray(x[b, HALF:S][::-1])
        m = dict(x_own=(lo if hf == 0 else hi), x_oth=(hi if hf == 0 else lo),
                 cm=_CACHE["cm"], ident=_CACHE["ident"], gtab=_CACHE["gtab"])
        m.update(_CACHE["cc"][hf])
        for l in range(nl):
            m[f"w_in{l}"] = np.ascontiguousarray(p["w_in"][l], dtype=np.float32)
            m[f"w_out{l}"] = np.ascontiguousarray(p["w_out"][l], dtype=np.float32)
            m[f"wq_up{l}"] = np.ascontiguousarray(p["w_b_q_up"][l], dtype=np.float32)
            m[f"wkv_up{l}"] = np.ascontiguousarray(p["w_b_kv_up"][l], dtype=np.float32)
            m[f"gv{l}"] = _gains(l, p)
        in_maps.append(m)
    return in_maps


def kernel(**inputs):
    p = {k: np.asarray(v) for k, v in inputs.items()}
    if "nc" not in _CACHE:
        _CACHE["nc"] = build_program(DEPTH)[0]
    in_maps = make_in_maps(p, DEPTH)
    res = run_bass_kernel_spmd(_CACHE["nc"], in_maps, core_ids=list(range(8)))
    out = np.empty((4, S, D), np.float32)
    for c in range(8):
        b, hf = c // 2, c % 2
        o = res.results[c]["xo"]
        if hf == 0:
            out[b, 0:HALF] = o
        else:
            out[b, HALF:S] = o[::-1]
    return out
```

```python
import numpy as np
from contextlib import ExitStack
import ml_dtypes
import concourse.bass as bass
import concourse.mybir as mybir
from concourse.bass_utils import run_bass_kernel_spmd

F32 = mybir.dt.float32
BF16 = mybir.dt.bfloat16
AF = mybir.ActivationFunctionType
ALU = mybir.AluOpType

S = 8192
D = 1024
HALF = 4096
TB = 512
NBLK = S // TB
NOWN = HALF // TB
DEPTH = 4
IN_COLS = 2848
OFF = dict(aq=0, ak=256, av=512, ag=768, bql=1024, bkv=1280, bkpe=1408, bg=1440,
           cq=1824, ck=2208, cv=2336, cg=2464)
EPS = 1e-6
GW = 2944
GOFF = 1408
HW = 1408
NGV = 17
UC = 3


class Res:
    __slots__ = ("name", "w", "r", "dsem", "dcnt")

    def __init__(self, name):
        self.name = name
        self.w = None
        self.r = []
        self.dsem = None
        self.dcnt = 0


class Prog:
    COMPUTE = ("pe", "act", "dve", "pool")
    ALL = ("pe", "act", "dve", "pool", "sp")

    def __init__(self, nc, es):
        self.nc = nc
        self.es = es
        self.ops = {e: [] for e in self.ALL}
        self.tsem = {}
        self.tick = {}
        self.waited = {e: {} for e in self.ALL}
        self.nsem = 0
        self.epoch = 0
        self.allres = {}
        self.oldticks = []
        self.new_epoch()

    def _newsem(self, name):
        self.nsem += 1
        return self.es.enter_context(self.nc.semaphore(name))

    def new_epoch(self):
        self.epoch += 1
        for e in self.COMPUTE:
            if e in self.tsem and self.tick[e] > 0:
                self.oldticks.append((self.tsem[e], self.tick[e], e))
            self.tsem[e] = self._newsem(f"t_{e}_{self.epoch}")
            self.tick[e] = 0

    def res(self, name):
        if name not in self.allres:
            self.allres[name] = Res(name)
        return self.allres[name]

    def _need(self, eng, ev, waits):
        if ev is None:
            return
        sem, val, src = ev
        if src == "pe" and eng == "pe":
            return
        key = id(sem)
        if self.waited[eng].get(key, 0) >= val:
            return
        self.waited[eng][key] = val
        waits.append((sem, val))

    def _deps(self, eng, reads, writes):
        waits = []
        for r in reads:
            self._need(eng, r.w, waits)
        for w in writes:
            ev = w.w
            if ev is not None and ev[2] != eng and not (ev[2] == "dma" and eng == "sp"):
                self._need(eng, ev, waits)
            for rv in w.r:
                if rv[2] == eng:
                    continue
                self._need(eng, rv, waits)
        best = {}
        for sem, val in waits:
            k = id(sem)
            if k not in best or best[k][1] < val:
                best[k] = (sem, val)
        return list(best.values())

    def op(self, eng, meth, reads=(), writes=(), **kw):
        waits = self._deps(eng, reads, writes)
        self.tick[eng] += 1
        n = self.tick[eng]
        sem = self.tsem[eng]
        ev = (sem, n, eng)

        def emit(e, waits=waits, meth=meth, kw=kw, sem=sem):
            for s, v in waits:
                e.wait_ge(s, v)
            getattr(e, meth)(**kw).then_inc(sem, 1)
        self.ops[eng].append(emit)
        for r in reads:
            r.r.append(ev)
        for w in writes:
            w.w = ev
            w.r = []
        return ev

    def dma(self, q, out_ap, in_ap, reads=(), writes=()):
        waits = self._deps(q, reads, writes)
        owner = writes[0] if writes else reads[0]
        if owner.dsem is None:
            owner.dsem = self._newsem("d_" + owner.name)
        owner.dcnt += 16
        ev = (owner.dsem, owner.dcnt, "dma")

        def emit(e, waits=waits, sem=owner.dsem):
            for s, v in waits:
                e.wait_ge(s, v)
            e.dma_start(out=out_ap, in_=in_ap).then_inc(sem, 16)
        self.ops[q].append(emit)
        for r in reads:
            r.r.append(ev)
        for w in writes:
            w.w = ev
            w.r = []
        return ev

    def collective(self, kind, op, groups, in_ap, out_ap, reads, writes):
        q = "pool"
        waits = self._deps(q, reads, writes)
        owner = writes[0]
        if owner.dsem is None:
            owner.dsem = self._newsem("c_" + owner.name)
        owner.dcnt += 1
        ev = (owner.dsem, owner.dcnt, "dma")

        def emit(e, waits=waits, sem=owner.dsem):
            for s, v in waits:
                e.wait_ge(s, v)
            e.collective_compute(kind, op, replica_groups=groups, ins=[in_ap], outs=[out_ap]).then_inc(sem)
        self.ops[q].append(emit)
        for r in reads:
            r.r.append(ev)
        for w in writes:
            w.w = ev
            w.r = []
        return ev

    def barrier(self):
        evs = list(self.oldticks)
        for e in self.COMPUTE:
            if self.tick[e] > 0:
                evs.append((self.tsem[e], self.tick[e], "x"))
        for r in self.allres.values():
            if r.dsem is not None and r.dcnt > 0:
                evs.append((r.dsem, r.dcnt, "dma"))
        for eng in self.ALL:
            waits = []
            for sem, val, _ in evs:
                self._need(eng, (sem, val, "x"), waits)

            def emit(e, waits=waits):
                for s, v in waits:
                    e.wait_ge(s, v)
            self.ops[eng].append(emit)
        for r in self.allres.values():
            r.w = None
            r.r = []

    def emit_all(self):
        nc = self.nc
        ops = self.ops
        with nc.Block() as block:
            @block.tensor
            def _(e):
                for f in ops["pe"]:
                    f(e)

            @block.scalar
            def _(e):
                for f in ops["act"]:
                    f(e)

            @block.vector
            def _(e):
                for f in ops["dve"]:
                    f(e)

            @block.gpsimd
            def _(e):
                for f in ops["pool"]:
                    f(e)

            @block.sync
            def _(e):
                for f in ops["sp"]:
                    f(e)


class Bufs:
    def __init__(self, nc, P, es, prefix, uid=""):
        self.nc, self.P, self.es, self.prefix, self.uid = nc, P, es, prefix, uid
        self.b = {}
        self.i = {}

    def mk(self, name, shape, dtype, n=1, psum=False):
        lst = []
        for k in range(n):
            nm = f"{self.prefix}_{name}{k}"
            tn = f"{self.prefix}{self.uid}_{name}{k}"
            if psum:
                t = self.es.enter_context(self.nc.psum_tensor(tn, shape, dtype))
            else:
                t = self.es.enter_context(self.nc.sbuf_tensor(tn, shape, dtype))
            lst.append((t, self.P.res(nm)))
        self.b[name] = lst
        self.i[name] = 0
        return lst[0]

    def nxt(self, name):
        lst = self.b[name]
        k = self.i[name]
        self.i[name] = (k + 1) % len(lst)
        return lst[k]

    def get(self, name, k=0):
        return self.b[name][k]

    def acq(self, name):
        if not hasattr(self, "free"):
            self.free = {}
        fl = self.free.setdefault(name, list(self.b[name]))
        assert fl, f"buffer pool {name} exhausted"
        return fl.pop(0)

    def rel(self, name, item):
        self.free[name].append(item)


def run_gens(gens, width, caps=None):
    queue = [(g if isinstance(g, tuple) else ({}, g)) for g in gens]
    caps = caps or {}
    active = []
    used = {}
    while queue or active:
        k = 0
        while k < len(queue) and len(active) < width:
            need, g = queue[k]
            if all(used.get(r, 0) + v <= caps.get(r, 1 << 30) for r, v in need.items()):
                for r, v in need.items():
                    used[r] = used.get(r, 0) + v
                active.append(queue.pop(k))
            else:
                k += 1
        assert active
        nxt = []
        for need, g in active:
            try:
                next(g)
                nxt.append((need, g))
            except StopIteration:
                for r, v in need.items():
                    used[r] -= v
        active = nxt


def alloc_persist(nc, P, es, cst):
    BL = Bufs(nc, P, es, "L", "")
    pw = dict(
        Wb=BL.mk("Wb", [128, 8, IN_COLS], BF16), Wqu=BL.mk("Wqu", [128, 2, 576], BF16),
        Wkk=BL.mk("Wkk", [128, 6, 64], BF16), Wkv=BL.mk("Wkv", [128, 384], BF16),
        gv=BL.mk("gv", [128, NGV], F32), cm=BL.mk("cm", [128, 4, 128], BF16),
        ident=BL.mk("ident", [128, 128], F32), epsb=BL.mk("epsb", [128, 1], F32),
        msk=BL.mk("msk", [128, 2], F32))
    r_w = P.res("w_dram")
    P.dma("sp", pw["cm"][0][:], cst["cm"], reads=[r_w], writes=[pw["cm"][1]])
    P.dma("sp", pw["ident"][0][:], cst["ident"], reads=[r_w], writes=[pw["ident"][1]])
    P.dma("sp", pw["msk"][0][:], cst["msk"], reads=[r_w], writes=[pw["msk"][1]])
    P.op("dve", "memset", [], [pw["epsb"][1]], ap=pw["epsb"][0][:], constant=EPS)
    return pw


def emit_W(nc, P, lid, wts, pw, es):
    r_w = P.res("w_dram")
    Wb, r_Wb = pw["Wb"]
    Wqu, r_Wqu = pw["Wqu"]
    Wkk, r_Wkk = pw["Wkk"]
    Wkv, r_Wkv = pw["Wkv"]
    gv, r_gv = pw["gv"]
    P.dma("sp", gv[:], wts["gv"], reads=[r_w], writes=[r_gv])
    BW = Bufs(nc, P, es, "W", lid)
    BW.mk("wst", [128, IN_COLS], F32, 2)
    BW.mk("wq", [128, 2, 576], F32)
    BW.mk("wk", [128, 768], F32)
    win_v = wts["w_in"].rearrange("(c p) n -> p c n", p=128)
    for c in range(8):
        st, r_st = BW.nxt("wst")
        P.dma("sp", st[:], win_v[:, c, :], reads=[r_w], writes=[r_st])
        if c % 2 == 1:
            P.op("act", "activation", [r_st], [r_Wb], out=Wb[:, c, :], in_=st[:], func=AF.Copy)
        else:
            P.op("dve", "tensor_copy", [r_st], [r_Wb], out=Wb[:, c, :], in_=st[:])
        yield
    wq, r_wq = BW.get("wq")
    P.dma("sp", wq[:], wts["wq_up"].rearrange("(c p) n -> p c n", p=128), reads=[r_w], writes=[r_wq])
    P.op("dve", "tensor_copy", [r_wq], [r_Wqu], out=Wqu[:], in_=wq[:])
    wk, r_wk = BW.get("wk")
    P.dma("sp", wk[:], wts["wkv_up"], reads=[r_w], writes=[r_wk])
    wk3 = wk[:].rearrange("p (h t) -> p h t", t=128)
    P.op("dve", "tensor_copy", [r_wk], [r_Wkk], out=Wkk[:], in_=wk3[:, :, 0:64])
    P.op("dve", "tensor_copy", [r_wk], [r_Wkv], out=Wkv[:].rearrange("p (h d) -> p h d", d=64),
         in_=wk3[:, :, 64:128])


def emit_layer(nc, P, lid, src, dst, wts, cst, scr, pw, do_w, next_w):
    w_in, w_out, wq_up, wkv_up, gvd = wts["w_in"], wts["w_out"], wts["wq_up"], wts["wkv_up"], wts["gv"]
    KT_A, KT_B, KT_C = scr["KT_A"], scr["KT_B"], scr["KT_C"]
    QT_A, QT_B, QT_C = scr["QT_A"], scr["QT_B"], scr["QT_C"]
    V_all, GATE, YT = scr["V"], scr["GATE"], scr["YT"]
    r_KT, r_QT, r_V, r_GATE, r_YT = (P.res("s_KT"), P.res("s_QT"), P.res("s_V"),
                                     P.res("s_GATE"), P.res("s_YT"))
    r_w = P.res("w_dram")

    def ACT(reads, writes, **kw):
        P.op("act", "activation", reads, writes, **kw)

    def MM(reads, writes, **kw):
        P.op("pe", "matmul", reads, writes, **kw)

    with ExitStack() as esL:
        Wb, r_Wb = pw["Wb"]
        Wqu, r_Wqu = pw["Wqu"]
        Wkk, r_Wkk = pw["Wkk"]
        Wkv, r_Wkv = pw["Wkv"]
        gv, r_gv = pw["gv"]
        cm, r_cm = pw["cm"]
        ident, r_id = pw["ident"]
        epsb, r_eps = pw["epsb"]
        msk, r_msk = pw["msk"]
        permB, permC, ones2, onesall = (cm[:, 0, :], cm[:, 1, :], cm[:, 2, :], cm[:, 3, :])

        if do_w:
            with ExitStack() as esW:
                for _ in emit_W(nc, P, lid, wts, pw, esW):
                    pass
                P.barrier()

        with ExitStack() as esP:
            B = Bufs(nc, P, esP, "P", lid)
            B.mk("bank", [128, 512], F32, 8, psum=True)
            B.mk("xt", [128, 4, D], F32, 2)
            if src["gath"] is not None:
                B.mk("xg", [128, 4, D], F32, 1)
            B.mk("junk", [128, D], BF16)
            B.mk("ss", [128, 4], F32, 2)
            B.mk("l4", [128, 4], F32, 2)
            B.mk("r4", [128, 4], F32, 2)
            B.mk("hT", [128, 8, TB], BF16, 2)
            B.mk("tab", [128, 4, TB], F32, 2)
            B.mk("sq", [128, TB], BF16, 5)
            B.mk("ln", [128, TB], F32, 4)
            B.mk("qn", [128, TB], BF16, 4)
            B.mk("t1", [128, TB], F32, 4)
            B.mk("t2", [128, TB], F32, 4)
            B.mk("ob", [128, TB], BF16, 6)
            B.mk("qln", [128, 2, TB], BF16, 2)
            B.mk("kvn", [128, TB], BF16, 2)
            B.mk("ge", [128, TB], F32, 4)
            B.mk("go", [128, TB], F32, 4)
            B.mk("vst", [128, 4, 768], BF16, 1)
            PW = 4

            def proj_group(hT, r_hT, col0, M, out_rows=0, bk=None):
                if bk is None:
                    bk = B.acq("bank")
                bT, bR = bk
                for c in range(8):
                    MM([r_Wb, r_hT], [bR], out=bT[out_rows:out_rows + M, :], lhsT=Wb[:, c, col0:col0 + M],
                       rhs=hT[:, c, :], start=(c == 0), stop=(c == 7))
                return bk

            def square(bk, M):
                bT, bR = bk
                sqp = B.acq("sq")
                sq, r_sq = sqp
                ACT([bR], [r_sq], out=sq[0:M, :], in_=bT[0:M, :], func=AF.Square)
                return sqp

            def rstd_gen(sqs, M, ones_ap, dk, out):
                pn = B.acq("bank")
                pnT, pnR = pn
                for i, (sq, r_sq) in enumerate(sqs):
                    MM([r_sq, r_cm], [pnR], out=pnT[0:M, :], lhsT=ones_ap, rhs=sq[0:M, :],
                       start=(i == 0), stop=(i == len(sqs) - 1))
                for sqp in sqs:
                    B.rel("sq", sqp)
                yield
                lnp = B.acq("ln")
                ln, r_ln = lnp
                ACT([pnR, r_eps], [r_ln], out=ln[0:M, :], in_=pnT[0:M, :], func=AF.Ln,
                    scale=1.0 / dk, bias=epsb[0:M, :])
                B.rel("bank", pn)
                yield
                ACT([r_ln], [r_ln], out=ln[0:M, :], in_=ln[0:M, :], func=AF.Exp, scale=-0.5)
                out.append(lnp)
                yield

            def headnorm_gen(mk_bank, M, dk, gcol, ones_ap, rope, tab, r_tab, stores, r_dst):
                bk = mk_bank()
                bT, bR = bk
                yield
                sqp = square(bk, M)
                yield
                res = []
                yield from rstd_gen([sqp], M, ones_ap, dk, res)
                rs, r_rs = res[0]
                qnp = B.acq("qn")
                qn, r_qn = qnp
                P.op("dve", "scalar_tensor_tensor", [bR, r_rs, r_gv], [r_qn], out=qn[0:M, :], in0=bT[0:M, :],
                     scalar=gv[0:M, gcol:gcol + 1], in1=rs[0:M, :], op0=ALU.mult, op1=ALU.mult)
                B.rel("ln", res[0])
                yield
                if rope is None:
                    B.rel("bank", bk)
                    for (dstap, lo, hi) in stores:
                        P.dma("sp", dstap, qn[lo:hi, :], reads=[r_qn], writes=[r_dst])
                    B.rel("qn", qnp)
                    yield
                    return
                perm = permB[0:M, 0:M] if rope == "B" else permC
                ci, si = (0, 1) if rope == "B" else (2, 3)
                MM([r_qn, r_cm], [bR], out=bT[0:M, :], lhsT=perm, rhs=qn[0:M, :], start=True, stop=True)
                t1p = B.acq("t1")
                t1, r_t1 = t1p
                P.op("dve", "tensor_tensor", [r_qn, r_tab], [r_t1], out=t1[0:M, :], in0=qn[0:M, :],
                     in1=tab[0:M, ci, :], op=ALU.mult)
                B.rel("qn", qnp)
                yield
                t2p = B.acq("t2")
                t2, r_t2 = t2p
                P.op("dve", "tensor_tensor", [bR, r_tab], [r_t2], out=t2[0:M, :], in0=bT[0:M, :],
                     in1=tab[0:M, si, :], op=ALU.mult)
                B.rel("bank", bk)
                yield
                obp = B.acq("ob")
                fin, r_fin = obp
                P.op("dve", "tensor_tensor", [r_t1, r_t2], [r_fin], out=fin[0:M, :], in0=t1[0:M, :],
                     in1=t2[0:M, :], op=ALU.add)
                B.rel("t1", t1p)
                B.rel("t2", t2p)
                yield
                for (dstap, lo, hi) in stores:
                    P.dma("sp", dstap, fin[lo:hi, :], reads=[r_fin], writes=[r_dst])
                B.rel("ob", obp)
                yield

            def gate_gen(hT, r_hT, col0, y0, t0):
                bk = proj_group(hT, r_hT, col0, 128)
                yield
                gep = B.acq("ge")
                ge, r_ge = gep
                ACT([bk[1]], [r_ge], out=ge[:], in_=bk[0][:], func=AF.Exp, scale=-1.0)
                yield
                P.op("dve", "tensor_scalar", [r_ge], [r_ge], out=ge[:], in0=ge[:], scalar1=1.0, scalar2=None, op0=ALU.add)
                yield
                P.op("dve", "reciprocal", [r_ge], [r_ge], out=ge[:], in_=ge[:])
                yield
                gop = B.acq("go")
                go, r_go = gop
                P.op("dve", "tensor_tensor", [bk[1], r_ge], [r_go], out=go[:], in0=bk[0][:], in1=ge[:], op=ALU.mult)
                B.rel("bank", bk)
                B.rel("ge", gep)
                yield
                P.dma("sp", GATE[y0:y0 + 128, t0:t0 + TB], go[:], reads=[r_go], writes=[r_GATE])
                B.rel("go", gop)
                yield

            V_v = V_all.rearrange("(n j p) c -> n p j c", j=4, p=128)

            def load_x(blk):
                xt, r_xt = B.nxt("xt")
                ent = dict(xt=xt, r_xt=r_xt, blend=None)
                if blk < NOWN:
                    P.dma("sp", xt[:], src["own_blk"](blk), reads=[src["r_own"]], writes=[r_xt])
                elif src["gath"] is None:
                    P.dma("sp", xt[:], src["oth_blk"](blk - NOWN), reads=[src["r_oth"]], writes=[r_xt])
                else:
                    xg, r_xg = B.nxt("xg")
                    P.dma("sp", xt[:], src["ga_blk"](blk - NOWN), reads=[src["r_oth"]], writes=[r_xt])
                    P.dma("sp", xg[:], src["gb_blk"](blk - NOWN), reads=[src["r_oth"]], writes=[r_xg])
                    ent["blend"] = (xg, r_xg)
                return ent

            def load_tab(blk):
                tab, r_tab = B.nxt("tab")
                t0 = blk * TB
                for i, nm in enumerate(("cosB", "sinB", "cosC", "sinC")):
                    P.dma("sp", tab[:, i, :], cst[nm][:, t0:t0 + TB], reads=[r_w], writes=[r_tab])
                return tab, r_tab

            def xprep_gen(ent):
                xt, r_xt = ent["xt"], ent["r_xt"]
                if ent["blend"] is not None:
                    xg, r_xg = ent["blend"]
                    xt2 = xt[:].rearrange("p j d -> p (j d)")
                    xg2 = xg[:].rearrange("p j d -> p (j d)")
                    ACT([r_xt, r_msk], [r_xt], out=xt2, in_=xt2, func=AF.Copy, scale=msk[:, 0:1])
                    yield
                    P.op("dve", "scalar_tensor_tensor", [r_xg, r_xt, r_msk], [r_xt], out=xt2, in0=xg2,
                         scalar=msk[:, 1:2], in1=xt2, op0=ALU.mult, op1=ALU.add)
                    yield
                junk, r_junk = B.get("junk")
                ss, r_ss = B.nxt("ss")
                for j in range(4):
                    ACT([r_xt], [r_junk, r_ss], out=junk[:], in_=xt[:, j, :], func=AF.Square, accum_out=ss[:, j:j + 1])
                yield
                l4, r_l4 = B.nxt("l4")
                ACT([r_ss, r_eps], [r_l4], out=l4[:], in_=ss[:], func=AF.Ln, scale=1.0 / D, bias=epsb[:])
                yield
                r4, r_r4 = B.nxt("r4")
                ACT([r_l4], [r_r4], out=r4[:], in_=l4[:], func=AF.Exp, scale=-0.5)
                yield
                for j in range(4):
                    if j % 2 == 0:
                        ACT([r_r4, r_xt], [r_xt], out=xt[:, j, :], in_=xt[:, j, :], func=AF.Copy, scale=r4[:, j:j + 1])
                    else:
                        P.op("dve", "tensor_scalar", [r_r4, r_xt], [r_xt], out=xt[:, j, :], in0=xt[:, j, :],
                             scalar1=r4[:, j:j + 1], scalar2=None, op0=ALU.mult)
                yield
                hT, r_hT = B.nxt("hT")
                ent["hT"] = (hT, r_hT)
                for c in range(8):
                    ptp = B.acq("bank")
                    ptT, ptR = ptp
                    for j in range(4):
                        P.op("pe", "transpose", [r_xt, r_id], [ptR], out=ptT[:, j * 128:(j + 1) * 128],
                             in_=xt[:, j, c * 128:(c + 1) * 128], identity=ident[:])
                    yield
                    if c % 2 == 0:
                        P.op("dve", "tensor_scalar", [ptR, r_gv], [r_hT], out=hT[:, c, :], in0=ptT[:],
                             scalar1=gv[:, c:c + 1], scalar2=None, op0=ALU.mult)
                    else:
                        ACT([ptR, r_gv], [r_hT], out=hT[:, c, :], in_=ptT[:], func=AF.Copy, scale=gv[:, c:c + 1])
                    B.rel("bank", ptp)
                    yield

            ents = {0: load_x(0)}
            if NBLK > 1:
                ents[1] = load_x(1)
            tabs = {0: load_tab(0)}
            run_gens([xprep_gen(ents[0])], 1)
            for blk in range(NBLK):
                own = blk < NOWN
                t0 = blk * TB
                hT, r_hT = ents[blk]["hT"]
                tab, r_tab = tabs[blk]
                if blk + 1 < NBLK:
                    tabs[blk + 1] = load_tab(blk + 1)

                kvn, r_kvn = B.nxt("kvn")
                qln, r_qln = B.nxt("qln")

                def kvlat_gen():
                    bk = proj_group(hT, r_hT, OFF["bkv"], 128)
                    yield
                    sqp = square(bk, 128)
                    yield
                    res = []
                    yield from rstd_gen([sqp], 128, onesall, 128.0, res)
                    rs, r_rs = res[0]
                    P.op("dve", "scalar_tensor_tensor", [bk[1], r_rs, r_gv], [r_kvn], out=kvn[:], in0=bk[0][:],
                         scalar=gv[:, 12:13], in1=rs[:], op0=ALU.mult, op1=ALU.mult)
                    B.rel("ln", res[0])
                    B.rel("bank", bk)
                    yield

                def qlat_gen():
                    bq = [proj_group(hT, r_hT, OFF["bql"] + 128 * c, 128) for c in range(2)]
                    yield
                    sqs = [square(bq[0], 128), square(bq[1], 128)]
                    yield
                    res = []
                    yield from rstd_gen(sqs, 128, onesall, 256.0, res)
                    rs, r_rs = res[0]
                    for c in range(2):
                        P.op("dve", "scalar_tensor_tensor", [bq[c][1], r_rs, r_gv], [r_qln], out=qln[:, c, :],
                             in0=bq[c][0][:], scalar=gv[:, 10 + c:11 + c], in1=rs[:], op0=ALU.mult, op1=ALU.mult)
                    B.rel("ln", res[0])
                    B.rel("bank", bq[0])
                    B.rel("bank", bq[1])
                    yield

                def HN(g):
                    return (dict(bank=2, h=1), g)

                def pg(col0):
                    return lambda: proj_group(hT, r_hT, col0, 128)

                def two(dst, g):
                    return [(dst[(2 * g) * 64:(2 * g + 1) * 64, t0:t0 + TB], 0, 64),
                            (dst[(2 * g + 1) * 64:(2 * g + 2) * 64, t0:t0 + TB], 64, 128)]

                gens = [kvlat_gen()]
                if own:
                    gens.append(qlat_gen())
                run_gens(gens, 2)
                gens = []
                if blk + 1 < NBLK:
                    gens.append((dict(bank=1), xprep_gen(ents[blk + 1])))
                for g in range(2):
                    gens.append(HN(headnorm_gen(pg(OFF["ak"] + 128 * g), 128, 64.0, 9, ones2, None, tab, r_tab, two(KT_A, g), r_KT)))
                gens.append(HN(headnorm_gen(pg(OFF["ck"]), 128, 64.0, 16, ones2, "C", tab, r_tab, two(KT_C, 0), r_KT)))
                if own:
                    for g in range(2):
                        gens.append(HN(headnorm_gen(pg(OFF["aq"] + 128 * g), 128, 64.0, 8, ones2, None, tab, r_tab, two(QT_A, g), r_QT)))
                    for g in range(3):
                        gens.append(HN(headnorm_gen(pg(OFF["cq"] + 128 * g), 128, 64.0, 15, ones2, "C", tab, r_tab, two(QT_C, g), r_QT)))
                    for (gcol, yrow, ng) in ((OFF["ag"], 0, 2), (OFF["bg"], 256, 3), (OFF["cg"], 640, 3)):
                        for g in range(ng):
                            gens.append((dict(bank=1, g=1), gate_gen(hT, r_hT, gcol + 128 * g, yrow + 128 * g, t0)))

                def v_gen():
                    vst, r_vst = B.nxt("vst")
                    for j in range(4):
                        pvp = B.acq("bank")
                        pvT, pvR = pvp
                        for (cols, n, o) in ((OFF["av"], 256, 0), (OFF["cv"], 128, 256)):
                            for c in range(8):
                                MM([r_hT, r_Wb], [pvR], out=pvT[:, o:o + n], lhsT=hT[:, c, j * 128:(j + 1) * 128],
                                   rhs=Wb[:, c, cols:cols + n], start=(c == 0), stop=(c == 7))
                        yield
                        ACT([pvR], [r_vst], out=vst[:, j, 0:384], in_=pvT[:, 0:384], func=AF.Copy)
                        B.rel("bank", pvp)
                        pv2p = B.acq("bank")
                        pv2T, pv2R = pv2p
                        MM([r_kvn, r_Wkv], [pv2R], out=pv2T[:, 0:384], lhsT=kvn[:, j * 128:(j + 1) * 128],
                           rhs=Wkv[:], start=True, stop=True)
                        yield
                        P.op("dve", "tensor_copy", [pv2R], [r_vst], out=vst[:, j, 384:768], in_=pv2T[:, 0:384])
                        B.rel("bank", pv2p)
                        yield
                    P.dma("sp", V_v[blk], vst[:], reads=[r_vst], writes=[r_V])
                    yield
                gens.insert(1, (dict(bank=1), v_gen()))

                def bk_bank(h):
                    def f():
                        bkh = B.acq("bank")
                        MM([r_Wkk, r_kvn], [bkh[1]], out=bkh[0][0:64, :], lhsT=Wkk[:, h, :], rhs=kvn[:], start=True, stop=True)
                        proj_group(hT, r_hT, OFF["bkpe"], 32, out_rows=64, bk=bkh)
                        return bkh
                    return f

                def bq_bank(h):
                    def f():
                        bkh = B.acq("bank")
                        for c in range(2):
                            MM([r_Wqu, r_qln], [bkh[1]], out=bkh[0][0:96, :], lhsT=Wqu[:, c, h * 96:(h + 1) * 96],
                               rhs=qln[:, c, :], start=(c == 0), stop=(c == 1))
                        return bkh
                    return f

                for h in range(6):
                    gens.append(HN(headnorm_gen(bk_bank(h), 96, 96.0, 14, onesall[0:96, 0:96], "B", tab, r_tab,
                                             [(KT_B[h * 96:(h + 1) * 96, t0:t0 + TB], 0, 96)], r_KT)))
                    if own:
                        gens.append(HN(headnorm_gen(bq_bank(h), 96, 96.0, 13, onesall[0:96, 0:96], "B", tab, r_tab,
                                                 [(QT_B[h * 96:(h + 1) * 96, t0:t0 + TB], 0, 96)], r_QT)))
                if own:
                    gates = [g for g in gens if "g" in g[0]]
                    rest = [g for g in gens if "g" not in g[0]]
                    gens = []
                    while rest or gates:
                        gens.extend(rest[:2])
                        rest = rest[2:]
                        gens.extend(gates[:1])
                        gates = gates[1:]
                run_gens(gens, 6, dict(bank=8, h=4, g=4))
                if blk + 2 < NBLK:
                    ents[blk + 2] = load_x(blk + 2)
            P.barrier()

        with ExitStack() as esA:
            B = Bufs(nc, P, esA, "A", lid)
            B.mk("S", [128, 512 * UC], F32, 2, psum=True)
            B.mk("O", [128, 512], F32, 2, psum=True)
            B.mk("kT", [128, S], BF16, 2)
            B.mk("vA", [128, 64, 128], BF16, 2)
            B.mk("qT", [128, HALF], BF16, 2)
            B.mk("Gt", [128, GW + HW], BF16, 2)
            B.mk("gt", [64, 512], F32, 2)
            B.mk("pT", [128, 512 * UC], BF16, 4)
            B.mk("rd", [128, 512], F32, 2)
            B.mk("rd0", [64, 512], F32, 2)
            B.mk("yf", [64, 512], F32, 2)
            B.mk("yb", [64, 512], BF16, 2)
            for k in range(2):
                vt, vr = B.get("vA", k)
                P.op("pool", "memset", [], [vr], ap=vt[:, :, 64:128], constant=1.0)

            jobs = []
            for h in range(4):
                jobs.append(dict(kv=("A", h), kt=KT_A[h * 64:(h + 1) * 64, :], dk=64, vcol=h * 64,
                                 q=QT_A[h * 64:(h + 1) * 64, :], scale=0.125, yrow=h * 64, isA=True, gi=h))
            for h in range(6):
                jobs.append(dict(kv=("B", h), kt=KT_B[h * 96:(h + 1) * 96, :], dk=96, vcol=384 + 64 * h,
                                 q=QT_B[h * 96:(h + 1) * 96, :], scale=96.0 ** -0.5, yrow=256 + 64 * h, isA=False))
            for h in range(6):
                kvh = h // 3
                jobs.append(dict(kv=("C", kvh), kt=KT_C[kvh * 64:(kvh + 1) * 64, :], dk=64, vcol=256 + 64 * kvh,
                                 q=QT_C[h * 64:(h + 1) * 64, :], scale=0.125, yrow=640 + 64 * h, isA=False))

            V_cv = V_all.rearrange("(c p) d -> p c d", p=128)
            state = dict(kvkey=None, kvbuf=None)

            def load_job(job):
                if job["kv"] != state["kvkey"]:
                    kT, r_kT = B.nxt("kT")
                    vA, r_vA = B.nxt("vA")
                    dk = job["dk"]
                    for q4 in range(4):
                        P.dma("sp", kT[0:dk, q4 * 2048:(q4 + 1) * 2048], job["kt"][:, q4 * 2048:(q4 + 1) * 2048],
                              reads=[r_KT], writes=[r_kT])
                    vc = job["vcol"]
                    for q8 in range(8):
                        P.dma("sp", vA[:, q8 * 8:(q8 + 1) * 8, 0:64], V_cv[:, q8 * 8:(q8 + 1) * 8, vc:vc + 64],
                              reads=[r_V], writes=[r_vA])
                    state["kvkey"] = job["kv"]
                    state["kvbuf"] = (kT, r_kT, vA, r_vA)
                job["kvbuf"] = state["kvbuf"]
                qT, r_qT = B.nxt("qT")
                P.dma("sp", qT[0:job["dk"], :], job["q"], reads=[r_QT], writes=[r_qT])
                job["qbuf"] = (qT, r_qT)
                if job["isA"]:
                    Gt, r_Gt = B.nxt("Gt")
                    P.dma("sp", Gt[:], cst["gtab"][job["gi"]], reads=[r_w], writes=[r_Gt])
                    job["gbuf"] = (Gt, r_Gt)

            units = []
            for ji, job in enumerate(jobs):
                cnt = 0
                for qb in range(NOWN):
                    q0 = qb * TB
                    if job["isA"]:
                        ch = []
                        for cc in range(max(0, q0 - 1024) // 128, min(HALF, q0 + TB + 1024) // 128):
                            c = (cc * 128 - q0) // 128
                            ch.append((cc, GOFF - 128 * c))
                        for k in range(8):
                            m = 639 + 128 * k
                            s0 = S - 1 - q0 - m
                            if 0 <= s0 < HALF:
                                assert s0 % 128 == 0
                                ch.append(((HALF + s0) // 128, GW + (1535 - m)))
                    else:
                        ch = [(cc, None) for cc in range(S // 128)]
                    for i in range(0, len(ch), UC):
                        units.append(dict(ji=ji, qb=qb, ch=ch[i:i + UC], first=(i == 0),
                                          last=(i + UC >= len(ch)), idx=cnt))
                        cnt += 1

            def stage1(u):
                job = jobs[u["ji"]]
                if u["idx"] == 0 and u["ji"] == 0:
                    load_job(jobs[0])
                if u["idx"] == 3 and u["ji"] + 1 < len(jobs):
                    load_job(jobs[u["ji"] + 1])
                q0 = u["qb"] * TB
                if u["first"]:
                    gt, r_gt = B.nxt("gt")
                    P.dma("sp", gt[:], GATE[job["yrow"]:job["yrow"] + 64, q0:q0 + TB], reads=[r_GATE], writes=[r_gt])
                    job["cur"] = ((gt, r_gt), B.nxt("O"))
                u["gt"], u["O"] = job["cur"]
                kT, r_kT, vA, r_vA = job["kvbuf"]
                qT, r_qT = job["qbuf"]
                dk = job["dk"]
                sT, r_S = B.nxt("S")
                u["S"] = (sT, r_S)
                for k, (cc, _) in enumerate(u["ch"]):
                    MM([r_kT, r_qT], [r_S], out=sT[:, k * 512:(k + 1) * 512], lhsT=kT[0:dk, cc * 128:(cc + 1) * 128],
                       rhs=qT[0:dk, q0:q0 + TB], start=True, stop=True)

            def stage2(u):
                job = jobs[u["ji"]]
                sT, r_S = u["S"]
                pT, r_pT = B.nxt("pT")
                r_pTc = P.res(r_pT.name + "_c")
                u["pT"] = (pT, r_pT, r_pTc)
                w = 512 * len(u["ch"])
                ACT([r_S], [r_pT, r_pTc], out=pT[:, 0:w], in_=sT[:, 0:w], func=AF.Exp, scale=job["scale"])
                if job["isA"]:
                    Gt, r_Gt = job["gbuf"]
                    for k, (cc, u0) in enumerate(u["ch"]):
                        eng, rr = ("dve", r_pTc) if k == 2 else ("dve", r_pT)
                        P.op(eng, "tensor_tensor", [rr, r_Gt], [rr], out=pT[:, k * 512:(k + 1) * 512],
                             in0=pT[:, k * 512:(k + 1) * 512], in1=Gt[:, u0:u0 + 512], op=ALU.mult)

            def stage3(u):
                job = jobs[u["ji"]]
                kT, r_kT, vA, r_vA = job["kvbuf"]
                pT, r_pT, r_pTc = u["pT"]
                oT, r_O = u["O"]
                nch = len(u["ch"])
                for k, (cc, _) in enumerate(u["ch"]):
                    MM([r_vA, (r_pTc if k == 2 else r_pT)], [r_O], out=oT[:], lhsT=vA[:, cc, :], rhs=pT[:, k * 512:(k + 1) * 512],
                       start=(u["first"] and k == 0), stop=(u["last"] and k == nch - 1))
                if u["last"]:
                    gt, r_gt = u["gt"]
                    rd, r_rd = B.nxt("rd")
                    P.op("dve", "reciprocal", [r_O], [r_rd], out=rd[64:128, :], in_=oT[64:128, :])
                    rd0, r_rd0 = B.nxt("rd0")
                    P.op("dve", "tensor_copy", [r_rd], [r_rd0], out=rd0[:], in_=rd[64:128, :])
                    yf, r_yf = B.nxt("yf")
                    P.op("dve", "tensor_tensor", [r_O, r_rd0], [r_yf], out=yf[:], in0=oT[0:64, :], in1=rd0[:], op=ALU.mult)
                    yb, r_yb = B.nxt("yb")
                    P.op("dve", "tensor_tensor", [r_yf, r_gt], [r_yb], out=yb[:], in0=yf[:], in1=gt[:], op=ALU.mult)
                    q0 = u["qb"] * TB
                    P.dma("sp", YT[job["yrow"]:job["yrow"] + 64, q0:q0 + TB], yb[:], reads=[r_yb], writes=[r_YT])

            n = len(units)
            for i in range(n + 3):
                if i < n:
                    stage1(units[i])
                if 0 <= i - 1 < n:
                    stage2(units[i - 1])
                if 0 <= i - 3 < n:
                    stage3(units[i - 3])
            P.barrier()

        with ExitStack() as esO:
            B = Bufs(nc, P, esO, "O", lid)
            B.mk("bank", [128, 512], F32, 8, psum=True)
            Wo, r_Wo = B.mk("Wo", [128, 8, D], BF16)
            B.mk("wost", [128, D], F32, 2)
            B.mk("yT", [128, 8, TB], BF16, 2)
            B.mk("xo", [128, 4, D], F32, 2)
            wo_v = w_out.rearrange("(c p) n -> p c n", p=128)
            for c in range(8):
                st, r_st = B.nxt("wost")
                P.dma("sp", st[:], wo_v[:, c, :], reads=[r_w], writes=[r_st])
                if c % 2 == 0:
                    P.op("dve", "tensor_copy", [r_st], [r_Wo], out=Wo[:, c, :], in_=st[:])
                else:
                    ACT([r_st], [r_Wo], out=Wo[:, c, :], in_=st[:], func=AF.Copy)
            wgen = next_w(esO) if next_w is not None else iter(())
            YT_v = YT.rearrange("(c p) t -> p c t", p=128)
            for blk in range(NOWN):
                q0 = blk * TB
                yT, r_yT = B.nxt("yT")
                P.dma("sp", yT[:], YT_v[:, :, q0:q0 + TB], reads=[r_YT], writes=[r_yT])
                xo, r_xo = B.nxt("xo")
                P.dma("sp", xo[:], src["own_blk"](blk), reads=[src["r_own"]], writes=[r_xo])
                for j in range(4):
                    for hh in range(2):
                        bT, bR = B.nxt("bank")
                        for c in range(8):
                            MM([r_yT, r_Wo], [bR], out=bT[:], lhsT=yT[:, c, j * 128:(j + 1) * 128],
                               rhs=Wo[:, c, hh * 512:(hh + 1) * 512], start=(c == 0), stop=(c == 7))
                        P.op("dve", "tensor_tensor", [bR, r_xo], [r_xo], out=xo[:, j, hh * 512:(hh + 1) * 512],
                             in0=bT[:], in1=xo[:, j, hh * 512:(hh + 1) * 512], op=ALU.add)
                P.dma("sp", dst["blk"](blk), xo[:], reads=[r_xo], writes=[dst["res"]])
                if dst["after"] is not None:
                    dst["after"](blk)
                next(wgen, None)
            for _ in wgen:
                pass
            P.barrier()


PAIRS = [[0, 1], [2, 3], [4, 5], [6, 7]]


def build_program(nl=DEPTH, dbg=False):
    nc = bass.Bass("TRN2", target_bir_lowering=False)
    x_own = nc.dram_tensor("x_own", [HALF, D], F32, kind="ExternalInput").ap()
    x_oth = nc.dram_tensor("x_oth", [HALF, D], F32, kind="ExternalInput").ap()
    xo = nc.dram_tensor("xo", [HALF, D], F32, kind="ExternalOutput").ap()
    wts = []
    for l in range(nl):
        wts.append(dict(
            w_in=nc.dram_tensor(f"w_in{l}", [D, IN_COLS], F32, kind="ExternalInput").ap(),
            w_out=nc.dram_tensor(f"w_out{l}", [D, D], F32, kind="ExternalInput").ap(),
            wq_up=nc.dram_tensor(f"wq_up{l}", [256, 576], F32, kind="ExternalInput").ap(),
            wkv_up=nc.dram_tensor(f"wkv_up{l}", [128, 768], F32, kind="ExternalInput").ap(),
            gv=nc.dram_tensor(f"gv{l}", [128, NGV], F32, kind="ExternalInput").ap(),
        ))
    cst = dict(
        cm=nc.dram_tensor("cm", [128, 4, 128], BF16, kind="ExternalInput").ap(),
        ident=nc.dram_tensor("ident", [128, 128], F32, kind="ExternalInput").ap(),
        gtab=nc.dram_tensor("gtab", [4, 128, GW + HW], BF16, kind="ExternalInput").ap(),
        msk=nc.dram_tensor("msk", [128, 2], F32, kind="ExternalInput").ap(),
    )
    for nm in ("cosB", "sinB", "cosC", "sinC"):
        cst[nm] = nc.dram_tensor(nm, [128, S], F32, kind="ExternalInput").ap()
    kind = "ExternalOutput" if dbg else "Internal"
    scr = dict(
        KT_A=nc.dram_tensor("KT_A", [256, S], BF16, kind=kind).ap(),
        KT_B=nc.dram_tensor("KT_B", [576, S], BF16, kind=kind).ap(),
        KT_C=nc.dram_tensor("KT_C", [128, S], BF16, kind=kind).ap(),
        QT_A=nc.dram_tensor("QT_A", [256, HALF], BF16, kind=kind).ap(),
        QT_B=nc.dram_tensor("QT_B", [576, HALF], BF16, kind=kind).ap(),
        QT_C=nc.dram_tensor("QT_C", [384, HALF], BF16, kind=kind).ap(),
        V=nc.dram_tensor("V_all", [S, 768], BF16, kind=kind).ap(),
        GATE=nc.dram_tensor("GATE", [D, HALF], F32, kind=kind).ap(),
        YT=nc.dram_tensor("YT", [D, HALF], BF16, kind=kind).ap(),
    )
    ownc = [[nc.dram_tensor(f"ownc{i}_{j}", [TB, D], F32) for j in range(NOWN)] for i in range(2)]
    gathc = [[nc.dram_tensor(f"gathc{i}_{j}", [2 * TB, D], F32) for j in range(NOWN)] for i in range(2)]

    def blkview(ap):
        return ap.rearrange("(j p) d -> p j d", p=128)

    x_own_v = x_own.rearrange("(n j p) d -> n p j d", j=4, p=128)
    x_oth_v = x_oth.rearrange("(n j p) d -> n p j d", j=4, p=128)
    xo_v = xo.rearrange("(n j p) d -> n p j d", j=4, p=128)
    with ExitStack() as es:
        P = Prog(nc, es)
        pw = alloc_persist(nc, P, es, cst)
        r_in = P.res("x_in")
        r_ownc = [P.res(f"ownc{i}") for i in range(2)]
        r_gathc = [P.res(f"gathc{i}") for i in range(2)]
        for l in range(nl):
            if l > 0:
                P.new_epoch()
            if l == 0:
                src = dict(own_blk=lambda b: x_own_v[b], r_own=r_in, oth_blk=lambda j: x_oth_v[j], r_oth=r_in, gath=None)
            else:
                k = (l - 1) % 2
                src = dict(own_blk=lambda b, k=k: blkview(ownc[k][b].ap()), r_own=r_ownc[k], gath=True,
                           ga_blk=lambda j, k=k: blkview(gathc[k][j].ap()[0:TB, :]),
                           gb_blk=lambda j, k=k: blkview(gathc[k][j].ap()[TB:2 * TB, :]), r_oth=r_gathc[k])
            last = (l == nl - 1)
            if last:
                dst = dict(blk=lambda b: xo_v[b], res=P.res("x_out"), after=None)
            else:
                k = l % 2

                def after(b, k=k):
                    P.collective("AllGather", ALU.bypass, PAIRS, ownc[k][b].ap().opt(), gathc[k][b].ap().opt(),
                                 reads=[r_ownc[k]], writes=[r_gathc[k]])
                dst = dict(blk=lambda b, k=k: blkview(ownc[k][b].ap()), res=r_ownc[k], after=after)
            next_w = None
            if not last:
                next_w = (lambda es_, l=l: emit_W(nc, P, l + 1, wts[l + 1], pw, es_))
            emit_layer(nc, P, l, src, dst, wts[l], cst, scr, pw, (l == 0), next_w)
        P.emit_all()
        nsem = P.nsem
    return nc, nsem


def _rope_tables():
    f32 = np.float32
    freqs = np.power(f32(10000.0), (f32(-2.0) * np.arange(16, dtype=f32) / f32(32.0))).astype(f32)
    t = np.arange(S)
    row = (t // 64).astype(f32)
    col = (t % 64).astype(f32)
    tt = t.astype(f32)
    cosB = np.zeros((128, S), f32)
    sinB = np.zeros((128, S), f32)
    cosB[0:64] = 1.0
    for p in range(64, 96):
        sub = p - 64
        i = sub % 16
        ang = (tt * freqs[i]).astype(f32)
        cosB[p] = np.cos(ang)
        sinB[p] = (-np.sin(ang)) if sub < 16 else np.sin(ang)
    cosC = np.zeros((128, S), f32)
    sinC = np.zeros((128, S), f32)
    for p in range(128):
        dd = p % 64
        pos = row if dd < 32 else col
        sub = dd % 32
        i = sub % 16
        ang = (pos * freqs[i]).astype(f32)
        cosC[p] = np.cos(ang)
        sinC[p] = (-np.sin(ang)) if sub < 16 else np.sin(ang)
    return cosB, sinB, cosC, sinC


def _const_mats():
    permB = np.zeros((128, 128), np.float32)
    for p in range(64, 96):
        sub = p - 64
        partner = p + 16 if sub < 16 else p - 16
        permB[partner, p] = 1.0
    permC = np.zeros((128, 128), np.float32)
    for p in range(128):
        sub = (p % 64) % 32
        partner = p + 16 if sub < 16 else p - 16
        permC[partner, p] = 1.0
    ones2 = np.zeros((128, 128), np.float32)
    ones2[0:64, 0:64] = 1.0
    ones2[64:128, 64:128] = 1.0
    onesall = np.ones((128, 128), np.float32)
    cm = np.stack([permB, permC, ones2, onesall], axis=1)
    return np.ascontiguousarray(cm).astype(ml_dtypes.bfloat16)


def _gtab():
    lim = 8192
    delta = np.arange(-lim, lim + 1)
    mult = np.zeros(delta.shape, np.float64)
    for d in (1, 4, 16):
        mult += ((delta % d) == 0) & (np.abs(delta) // d <= 64)
    out = np.zeros((4, 128, GW + HW), np.float32)
    kk = np.arange(128)[:, None]
    u = np.arange(GW)[None, :]
    dT = kk - u + GOFF
    uh = np.arange(HW)[None, :]
    dH = 1535 - kk - uh
    for h in range(4):
        slope = 2.0 ** (-8.0 * (h + 1) / 4.0)
        for (dl, c0, c1) in ((dT, 0, GW), (dH, GW, GW + HW)):
            m = mult[dl + lim]
            out[h][:, c0:c1] = (m * np.exp(-slope * np.abs(dl))).astype(np.float32)
    return out.astype(ml_dtypes.bfloat16)


def _gains(l, p):
    gvec = np.zeros((128, NGV), np.float32)
    gvec[:, 0:8] = p["norm_g"][l].reshape(8, 128).T
    gvec[:, 8] = np.tile(p["a_q_norm_g"][l], 2)
    gvec[:, 9] = np.tile(p["a_k_norm_g"][l], 2)
    gvec[:, 10:12] = p["b_q_lat_norm_g"][l].reshape(2, 128).T
    gvec[:, 12] = p["b_kv_lat_norm_g"][l]
    gvec[0:96, 13] = p["b_q_norm_g"][l]
    gvec[0:96, 14] = p["b_k_norm_g"][l]
    gvec[:, 15] = np.tile(p["c_q_norm_g"][l], 2)
    gvec[:, 16] = np.tile(p["c_k_norm_g"][l], 2)
    return gvec


def _core_consts():
    cosB, sinB, cosC, sinC = _rope_tables()
    out = {}
    asc = np.arange(HALF)
    desc = S - 1 - np.arange(HALF)
    for hf in range(2):
        tok = np.concatenate([asc, desc]) if hf == 0 else np.concatenate([desc, asc])
        msk = np.zeros((128, 2), np.float32)
        msk[:, 1 - hf] = 1.0
        out[hf] = dict(cosB=np.ascontiguousarray(cosB[:, tok]), sinB=np.ascontiguousarray(sinB[:, tok]),
                       cosC=np.ascontiguousarray(cosC[:, tok]), sinC=np.ascontiguousarray(sinC[:, tok]), msk=msk)
    return out


_CACHE = {}


def make_in_maps(p, nl=DEPTH, cores=range(8)):
    if "cc" not in _CACHE:
        _CACHE["cc"] = _core_consts()
        _CACHE["cm"] = _const_mats()
        _CACHE["ident"] = np.eye(128, dtype=np.float32)
        _CACHE["gtab"] = _gtab()
    x = np.ascontiguousarray(p["x"], dtype=np.float32)
    in_maps = []
    for c in cores:
        b, hf = c // 2, c % 2
        lo = np.ascontiguousarray(x[b, 0:HALF])
        hi = np.ascontiguousarray(x[b, HALF:S][::-1])
        m = dict(x_own=(lo if hf == 0 else hi), x_oth=(hi if hf == 0 else lo),
                 cm=_CACHE["cm"], ident=_CACHE["ident"], gtab=_CACHE["gtab"])
        m.update(_CACHE["cc"][hf])
        for l in range(nl):
            m[f"w_in{l}"] = np.ascontiguousarray(p["w_in"][l], dtype=np.float32)
            m[f"w_out{l}"] = np.ascontiguousarray(p["w_out"][l], dtype=np.float32)
            m[f"wq_up{l}"] = np.ascontiguousarray(p["w_b_q_up"][l], dtype=np.float32)
            m[f"wkv_up{l}"] = np.ascontiguousarray(p["w_b_kv_up"][l], dtype=np.float32)
            m[f"gv{l}"] = _gains(l, p)
        in_maps.append(m)
    return in_maps


def kernel(**inputs):
    p = {k: np.asarray(v) for k, v in inputs.items()}
    if "nc" not in _CACHE:
        _CACHE["nc"] = build_program(DEPTH)[0]
    in_maps = make_in_maps(p, DEPTH)
    res = run_bass_kernel_spmd(_CACHE["nc"], in_maps, core_ids=list(range(8)))
    out = np.empty((4, S, D), np.float32)
    for c in range(8):
        b, hf = c // 2, c % 2
        o = res.results[c]["xo"]
        if hf == 0:
            out[b, 0:HALF] = o
        else:
            out[b, HALF:S] = o[::-1]
    return out
```

```python
import numpy as np
from contextlib import ExitStack
import ml_dtypes
import concourse.bass as bass
import concourse.mybir as mybir
from concourse.bass_utils import run_bass_kernel_spmd

F32 = mybir.dt.float32
BF16 = mybir.dt.bfloat16
AF = mybir.ActivationFunctionType
ALU = mybir.AluOpType

S = 8192
D = 1024
HALF = 4096
TB = 512
NBLK = S // TB
NOWN = HALF // TB
DEPTH = 4
IN_COLS = 2848
OFF = dict(aq=0, ak=256, av=512, ag=768, bql=1024, bkv=1280, bkpe=1408, bg=1440,
           cq=1824, ck=2208, cv=2336, cg=2464)
EPS = 1e-6
GW = 2944
GOFF = 1408
HW = 1408
NGV = 17
UC = 3


class Res:
    __slots__ = ("name", "w", "r", "dsem", "dcnt")

    def __init__(self, name):
        self.name = name
        self.w = None
        self.r = []
        self.dsem = None
        self.dcnt = 0


class Prog:
    COMPUTE = ("pe", "act", "dve", "pool")
    ALL = ("pe", "act", "dve", "pool", "sp")

    def __init__(self, nc, es):
        self.nc = nc
        self.es = es
        self.ops = {e: [] for e in self.ALL}
        self.tsem = {}
        self.tick = {}
        self.waited = {e: {} for e in self.ALL}
        self.nsem = 0
        self.epoch = 0
        self.allres = {}
        self.oldticks = []
        self.new_epoch()

    def _newsem(self, name):
        self.nsem += 1
        return self.es.enter_context(self.nc.semaphore(name))

    def new_epoch(self):
        self.epoch += 1
        for e in self.COMPUTE:
            if e in self.tsem and self.tick[e] > 0:
                self.oldticks.append((self.tsem[e], self.tick[e], e))
            self.tsem[e] = self._newsem(f"t_{e}_{self.epoch}")
            self.tick[e] = 0

    def res(self, name):
        if name not in self.allres:
            self.allres[name] = Res(name)
        return self.allres[name]

    def _need(self, eng, ev, waits):
        if ev is None:
            return
        sem, val, src = ev
        if src == "pe" and eng == "pe":
            return
        key = id(sem)
        if self.waited[eng].get(key, 0) >= val:
            return
        self.waited[eng][key] = val
        waits.append((sem, val))

    def _deps(self, eng, reads, writes):
        waits = []
        for r in reads:
            self._need(eng, r.w, waits)
        for w in writes:
            ev = w.w
            if ev is not None and ev[2] != eng and not (ev[2] == "dma" and eng == "sp"):
                self._need(eng, ev, waits)
            for rv in w.r:
                if rv[2] == eng:
                    continue
                self._need(eng, rv, waits)
        best = {}
        for sem, val in waits:
            k = id(sem)
            if k not in best or best[k][1] < val:
                best[k] = (sem, val)
        return list(best.values())

    def op(self, eng, meth, reads=(), writes=(), **kw):
        waits = self._deps(eng, reads, writes)
        self.tick[eng] += 1
        n = self.tick[eng]
        sem = self.tsem[eng]
        ev = (sem, n, eng)

        def emit(e, waits=waits, meth=meth, kw=kw, sem=sem):
            for s, v in waits:
                e.wait_ge(s, v)
            getattr(e, meth)(**kw).then_inc(sem, 1)
        self.ops[eng].append(emit)
        for r in reads:
            r.r.append(ev)
        for w in writes:
            w.w = ev
            w.r = []
        return ev

    def dma(self, q, out_ap, in_ap, reads=(), writes=()):
        waits = self._deps(q, reads, writes)
        owner = writes[0] if writes else reads[0]
        if owner.dsem is None:
            owner.dsem = self._newsem("d_" + owner.name)
        owner.dcnt += 16
        ev = (owner.dsem, owner.dcnt, "dma")

        def emit(e, waits=waits, sem=owner.dsem):
            for s, v in waits:
                e.wait_ge(s, v)
            e.dma_start(out=out_ap, in_=in_ap).then_inc(sem, 16)
        self.ops[q].append(emit)
        for r in reads:
            r.r.append(ev)
        for w in writes:
            w.w = ev
            w.r = []
        return ev

    def collective(self, kind, op, groups, in_ap, out_ap, reads, writes):
        q = "pool"
        waits = self._deps(q, reads, writes)
        owner = writes[0]
        if owner.dsem is None:
            owner.dsem = self._newsem("c_" + owner.name)
        owner.dcnt += 1
        ev = (owner.dsem, owner.dcnt, "dma")

        def emit(e, waits=waits, sem=owner.dsem):
            for s, v in waits:
                e.wait_ge(s, v)
            e.collective_compute(kind, op, replica_groups=groups, ins=[in_ap], outs=[out_ap]).then_inc(sem)
        self.ops[q].append(emit)
        for r in reads:
            r.r.append(ev)
        for w in writes:
            w.w = ev
            w.r = []
        return ev

    def barrier(self):
        evs = list(self.oldticks)
        for e in self.COMPUTE:
            if self.tick[e] > 0:
                evs.append((self.tsem[e], self.tick[e], "x"))
        for r in self.allres.values():
            if r.dsem is not None and r.dcnt > 0:
                evs.append((r.dsem, r.dcnt, "dma"))
        for eng in self.ALL:
            waits = []
            for sem, val, _ in evs:
                self._need(eng, (sem, val, "x"), waits)

            def emit(e, waits=waits):
                for s, v in waits:
                    e.wait_ge(s, v)
            self.ops[eng].append(emit)
        for r in self.allres.values():
            r.w = None
            r.r = []

    def emit_all(self):
        nc = self.nc
        ops = self.ops
        with nc.Block() as block:
            @block.tensor
            def _(e):
                for f in ops["pe"]:
                    f(e)

            @block.scalar
            def _(e):
                for f in ops["act"]:
                    f(e)

            @block.vector
            def _(e):
                for f in ops["dve"]:
                    f(e)

            @block.gpsimd
            def _(e):
                for f in ops["pool"]:
                    f(e)

            @block.sync
            def _(e):
                for f in ops["sp"]:
                    f(e)


class Bufs:
    def __init__(self, nc, P, es, prefix, uid=""):
        self.nc, self.P, self.es, self.prefix, self.uid = nc, P, es, prefix, uid
        self.b = {}
        self.i = {}

    def mk(self, name, shape, dtype, n=1, psum=False):
        lst = []
        for k in range(n):
            nm = f"{self.prefix}_{name}{k}"
            tn = f"{self.prefix}{self.uid}_{name}{k}"
            if psum:
                t = self.es.enter_context(self.nc.psum_tensor(tn, shape, dtype))
            else:
                t = self.es.enter_context(self.nc.sbuf_tensor(tn, shape, dtype))
            lst.append((t, self.P.res(nm)))
        self.b[name] = lst
        self.i[name] = 0
        return lst[0]

    def nxt(self, name):
        lst = self.b[name]
        k = self.i[name]
        self.i[name] = (k + 1) % len(lst)
        return lst[k]

    def get(self, name, k=0):
        return self.b[name][k]

    def acq(self, name):
        if not hasattr(self, "free"):
            self.free = {}
        fl = self.free.setdefault(name, list(self.b[name]))
        assert fl, f"buffer pool {name} exhausted"
        return fl.pop(0)

    def rel(self, name, item):
        self.free[name].append(item)


def run_gens(gens, width, caps=None):
    queue = [(g if isinstance(g, tuple) else ({}, g)) for g in gens]
    caps = caps or {}
    active = []
    used = {}
    while queue or active:
        k = 0
        while k < len(queue) and len(active) < width:
            need, g = queue[k]
            if all(used.get(r, 0) + v <= caps.get(r, 1 << 30) for r, v in need.items()):
                for r, v in need.items():
                    used[r] = used.get(r, 0) + v
                active.append(queue.pop(k))
            else:
                k += 1
        assert active
        nxt = []
        for need, g in active:
            try:
                next(g)
                nxt.append((need, g))
            except StopIteration:
                for r, v in need.items():
                    used[r] -= v
        active = nxt


def alloc_persist(nc, P, es, cst):
    BL = Bufs(nc, P, es, "L", "")
    pw = dict(
        Wb=BL.mk("Wb", [128, 8, IN_COLS], BF16), Wqu=BL.mk("Wqu", [128, 2, 576], BF16),
        Wkk=BL.mk("Wkk", [128, 6, 64], BF16), Wkv=BL.mk("Wkv", [128, 384], BF16),
        gv=BL.mk("gv", [128, NGV], F32), cm=BL.mk("cm", [128, 4, 128], BF16),
        ident=BL.mk("ident", [128, 128], F32), epsb=BL.mk("epsb", [128, 1], F32),
        msk=BL.mk("msk", [128, 2], F32))
    r_w = P.res("w_dram")
    P.dma("sp", pw["cm"][0][:], cst["cm"], reads=[r_w], writes=[pw["cm"][1]])
    P.dma("sp", pw["ident"][0][:], cst["ident"], reads=[r_w], writes=[pw["ident"][1]])
    P.dma("sp", pw["msk"][0][:], cst["msk"], reads=[r_w], writes=[pw["msk"][1]])
    P.op("dve", "memset", [], [pw["epsb"][1]], ap=pw["epsb"][0][:], constant=EPS)
    return pw


def emit_W(nc, P, lid, wts, pw, es):
    r_w = P.res("w_dram")
    Wb, r_Wb = pw["Wb"]
    Wqu, r_Wqu = pw["Wqu"]
    Wkk, r_Wkk = pw["Wkk"]
    Wkv, r_Wkv = pw["Wkv"]
    gv, r_gv = pw["gv"]
    P.dma("sp", gv[:], wts["gv"], reads=[r_w], writes=[r_gv])
    BW = Bufs(nc, P, es, "W", lid)
    BW.mk("wst", [128, IN_COLS], F32, 2)
    BW.mk("wq", [128, 2, 576], F32)
    BW.mk("wk", [128, 768], F32)
    win_v = wts["w_in"].rearrange("(c p) n -> p c n", p=128)
    for c in range(8):
        st, r_st = BW.nxt("wst")
        P.dma("sp", st[:], win_v[:, c, :], reads=[r_w], writes=[r_st])
        if c % 2 == 1:
            P.op("act", "activation", [r_st], [r_Wb], out=Wb[:, c, :], in_=st[:], func=AF.Copy)
        else:
            P.op("dve", "tensor_copy", [r_st], [r_Wb], out=Wb[:, c, :], in_=st[:])
        yield
    wq, r_wq = BW.get("wq")
    P.dma("sp", wq[:], wts["wq_up"].rearrange("(c p) n -> p c n", p=128), reads=[r_w], writes=[r_wq])
    P.op("dve", "tensor_copy", [r_wq], [r_Wqu], out=Wqu[:], in_=wq[:])
    wk, r_wk = BW.get("wk")
    P.dma("sp", wk[:], wts["wkv_up"], reads=[r_w], writes=[r_wk])
    wk3 = wk[:].rearrange("p (h t) -> p h t", t=128)
    P.op("dve", "tensor_copy", [r_wk], [r_Wkk], out=Wkk[:], in_=wk3[:, :, 0:64])
    P.op("dve", "tensor_copy", [r_wk], [r_Wkv], out=Wkv[:].rearrange("p (h d) -> p h d", d=64),
         in_=wk3[:, :, 64:128])


def emit_layer(nc, P, lid, src, dst, wts, cst, scr, pw, do_w, next_w):
    w_in, w_out, wq_up, wkv_up, gvd = wts["w_in"], wts["w_out"], wts["wq_up"], wts["wkv_up"], wts["gv"]
    KT_A = scr["KT_A"]
    KTB_own, KTB_all, KTC_own, KTC_all = scr["KTB_own"], scr["KTB_all"], scr["KTC_own"], scr["KTC_all"]
    VBC_own, VBC_all = scr["VBC_own"], scr["VBC_all"]
    r_KVX = P.res("s_KVX")
    QT_A, QT_B, QT_C = scr["QT_A"], scr["QT_B"], scr["QT_C"]
    V_all, GATE, YT = scr["V"], scr["GATE"], scr["YT"]
    r_KT, r_QT, r_V, r_GATE, r_YT = (P.res("s_KT"), P.res("s_QT"), P.res("s_V"),
                                     P.res("s_GATE"), P.res("s_YT"))
    r_w = P.res("w_dram")

    def ACT(reads, writes, **kw):
        P.op("act", "activation", reads, writes, **kw)

    def MM(reads, writes, **kw):
        P.op("pe", "matmul", reads, writes, **kw)

    with ExitStack() as esL:
        Wb, r_Wb = pw["Wb"]
        Wqu, r_Wqu = pw["Wqu"]
        Wkk, r_Wkk = pw["Wkk"]
        Wkv, r_Wkv = pw["Wkv"]
        gv, r_gv = pw["gv"]
        cm, r_cm = pw["cm"]
        ident, r_id = pw["ident"]
        epsb, r_eps = pw["epsb"]
        msk, r_msk = pw["msk"]
        permB, permC, ones2, onesall = (cm[:, 0, :], cm[:, 1, :], cm[:, 2, :], cm[:, 3, :])

        if do_w:
            with ExitStack() as esW:
                for _ in emit_W(nc, P, lid, wts, pw, esW):
                    pass
                P.barrier()

        with ExitStack() as esP:
            B = Bufs(nc, P, esP, "P", lid)
            B.mk("bank", [128, 512], F32, 8, psum=True)
            B.mk("xt", [128, 4, D], F32, 2)
            if src["gath"] is not None:
                B.mk("xg", [128, 4, D], F32, 1)
            B.mk("junk", [128, D], BF16)
            B.mk("ss", [128, 4], F32, 2)
            B.mk("l4", [128, 4], F32, 2)
            B.mk("r4", [128, 4], F32, 2)
            B.mk("hT", [128, 8, TB], BF16, 2)
            B.mk("tab", [128, 4, TB], F32, 2)
            B.mk("sq", [128, TB], BF16, 5)
            B.mk("ln", [128, TB], F32, 4)
            B.mk("qn", [128, TB], BF16, 4)
            B.mk("t1", [128, TB], F32, 4)
            B.mk("t2", [128, TB], F32, 4)
            B.mk("ob", [128, TB], BF16, 6)
            B.mk("qln", [128, 2, TB], BF16, 2)
            B.mk("kvn", [128, TB], BF16, 2)
            B.mk("ge", [128, TB], F32, 4)
            B.mk("go", [128, TB], F32, 4)
            B.mk("vst", [128, 4, 768], BF16, 1)
            PW = 4

            def proj_group(hT, r_hT, col0, M, out_rows=0, bk=None):
                if bk is None:
                    bk = B.acq("bank")
                bT, bR = bk
                for c in range(8):
                    MM([r_Wb, r_hT], [bR], out=bT[out_rows:out_rows + M, :], lhsT=Wb[:, c, col0:col0 + M],
                       rhs=hT[:, c, :], start=(c == 0), stop=(c == 7))
                return bk

            def square(bk, M):
                bT, bR = bk
                sqp = B.acq("sq")
                sq, r_sq = sqp
                ACT([bR], [r_sq], out=sq[0:M, :], in_=bT[0:M, :], func=AF.Square)
                return sqp

            def rstd_gen(sqs, M, ones_ap, dk, out):
                pn = B.acq("bank")
                pnT, pnR = pn
                for i, (sq, r_sq) in enumerate(sqs):
                    MM([r_sq, r_cm], [pnR], out=pnT[0:M, :], lhsT=ones_ap, rhs=sq[0:M, :],
                       start=(i == 0), stop=(i == len(sqs) - 1))
                for sqp in sqs:
                    B.rel("sq", sqp)
                yield
                lnp = B.acq("ln")
                ln, r_ln = lnp
                ACT([pnR, r_eps], [r_ln], out=ln[0:M, :], in_=pnT[0:M, :], func=AF.Ln,
                    scale=1.0 / dk, bias=epsb[0:M, :])
                B.rel("bank", pn)
                yield
                ACT([r_ln], [r_ln], out=ln[0:M, :], in_=ln[0:M, :], func=AF.Exp, scale=-0.5)
                out.append(lnp)
                yield

            def headnorm_gen(mk_bank, M, dk, gcol, ones_ap, rope, tab, r_tab, stores, r_dst):
                bk = mk_bank()
                bT, bR = bk
                yield
                sqp = square(bk, M)
                yield
                res = []
                yield from rstd_gen([sqp], M, ones_ap, dk, res)
                rs, r_rs = res[0]
                qnp = B.acq("qn")
                qn, r_qn = qnp
                P.op("dve", "scalar_tensor_tensor", [bR, r_rs, r_gv], [r_qn], out=qn[0:M, :], in0=bT[0:M, :],
                     scalar=gv[0:M, gcol:gcol + 1], in1=rs[0:M, :], op0=ALU.mult, op1=ALU.mult)
                B.rel("ln", res[0])
                yield
                if rope is None:
                    B.rel("bank", bk)
                    for (dstap, lo, hi) in stores:
                        P.dma("sp", dstap, qn[lo:hi, :], reads=[r_qn], writes=[r_dst])
                    B.rel("qn", qnp)
                    yield
                    return
                perm = permB[0:M, 0:M] if rope == "B" else permC
                ci, si = (0, 1) if rope == "B" else (2, 3)
                MM([r_qn, r_cm], [bR], out=bT[0:M, :], lhsT=perm, rhs=qn[0:M, :], start=True, stop=True)
                t1p = B.acq("t1")
                t1, r_t1 = t1p
                P.op("dve", "tensor_tensor", [r_qn, r_tab], [r_t1], out=t1[0:M, :], in0=qn[0:M, :],
                     in1=tab[0:M, ci, :], op=ALU.mult)
                B.rel("qn", qnp)
                yield
                t2p = B.acq("t2")
                t2, r_t2 = t2p
                P.op("dve", "tensor_tensor", [bR, r_tab], [r_t2], out=t2[0:M, :], in0=bT[0:M, :],
                     in1=tab[0:M, si, :], op=ALU.mult)
                B.rel("bank", bk)
                yield
                obp = B.acq("ob")
                fin, r_fin = obp
                P.op("dve", "tensor_tensor", [r_t1, r_t2], [r_fin], out=fin[0:M, :], in0=t1[0:M, :],
                     in1=t2[0:M, :], op=ALU.add)
                B.rel("t1", t1p)
                B.rel("t2", t2p)
                yield
                for (dstap, lo, hi) in stores:
                    P.dma("sp", dstap, fin[lo:hi, :], reads=[r_fin], writes=[r_dst])
                B.rel("ob", obp)
                yield

            def gate_gen(hT, r_hT, col0, y0, t0):
                bk = proj_group(hT, r_hT, col0, 128)
                yield
                gep = B.acq("ge")
                ge, r_ge = gep
                ACT([bk[1]], [r_ge], out=ge[:], in_=bk[0][:], func=AF.Exp, scale=-1.0)
                yield
                P.op("dve", "tensor_scalar", [r_ge], [r_ge], out=ge[:], in0=ge[:], scalar1=1.0, scalar2=None, op0=ALU.add)
                yield
                P.op("dve", "reciprocal", [r_ge], [r_ge], out=ge[:], in_=ge[:])
                yield
                gop = B.acq("go")
                go, r_go = gop
                P.op("dve", "tensor_tensor", [bk[1], r_ge], [r_go], out=go[:], in0=bk[0][:], in1=ge[:], op=ALU.mult)
                B.rel("bank", bk)
                B.rel("ge", gep)
                yield
                P.dma("sp", GATE[y0:y0 + 128, t0:t0 + TB], go[:], reads=[r_go], writes=[r_GATE])
                B.rel("go", gop)
                yield

            V_v = V_all.rearrange("(n j p) c -> n p j c", j=4, p=128)

            def load_x(blk):
                xt, r_xt = B.nxt("xt")
                ent = dict(xt=xt, r_xt=r_xt, blend=None)
                if blk < NOWN:
                    P.dma("sp", xt[:], src["own_blk"](blk), reads=[src["r_own"]], writes=[r_xt])
                elif src["gath"] is None:
                    P.dma("sp", xt[:], src["oth_blk"](blk - NOWN), reads=[src["r_oth"]], writes=[r_xt])
                else:
                    xg, r_xg = B.nxt("xg")
                    P.dma("sp", xt[:], src["ga_blk"](blk - NOWN), reads=[src["r_oth"]], writes=[r_xt])
                    P.dma("sp", xg[:], src["gb_blk"](blk - NOWN), reads=[src["r_oth"]], writes=[r_xg])
                    ent["blend"] = (xg, r_xg)
                return ent

            def load_tab(blk):
                tab, r_tab = B.nxt("tab")
                t0 = blk * TB
                for i, nm in enumerate(("cosB", "sinB", "cosC", "sinC")):
                    P.dma("sp", tab[:, i, :], cst[nm][:, t0:t0 + TB], reads=[r_w], writes=[r_tab])
                return tab, r_tab

            def xprep_gen(ent):
                xt, r_xt = ent["xt"], ent["r_xt"]
                if ent["blend"] is not None:
                    xg, r_xg = ent["blend"]
                    xt2 = xt[:].rearrange("p j d -> p (j d)")
                    xg2 = xg[:].rearrange("p j d -> p (j d)")
                    ACT([r_xt, r_msk], [r_xt], out=xt2, in_=xt2, func=AF.Copy, scale=msk[:, 0:1])
                    yield
                    P.op("dve", "scalar_tensor_tensor", [r_xg, r_xt, r_msk], [r_xt], out=xt2, in0=xg2,
                         scalar=msk[:, 1:2], in1=xt2, op0=ALU.mult, op1=ALU.add)
                    yield
                junk, r_junk = B.get("junk")
                ss, r_ss = B.nxt("ss")
                for j in range(4):
                    ACT([r_xt], [r_junk, r_ss], out=junk[:], in_=xt[:, j, :], func=AF.Square, accum_out=ss[:, j:j + 1])
                yield
                l4, r_l4 = B.nxt("l4")
                ACT([r_ss, r_eps], [r_l4], out=l4[:], in_=ss[:], func=AF.Ln, scale=1.0 / D, bias=epsb[:])
                yield
                r4, r_r4 = B.nxt("r4")
                ACT([r_l4], [r_r4], out=r4[:], in_=l4[:], func=AF.Exp, scale=-0.5)
                yield
                for j in range(4):
                    if j % 2 == 0:
                        ACT([r_r4, r_xt], [r_xt], out=xt[:, j, :], in_=xt[:, j, :], func=AF.Copy, scale=r4[:, j:j + 1])
                    else:
                        P.op("dve", "tensor_scalar", [r_r4, r_xt], [r_xt], out=xt[:, j, :], in0=xt[:, j, :],
                             scalar1=r4[:, j:j + 1], scalar2=None, op0=ALU.mult)
                yield
                hT, r_hT = B.nxt("hT")
                ent["hT"] = (hT, r_hT)
                for c in range(8):
                    ptp = B.acq("bank")
                    ptT, ptR = ptp
                    for j in range(4):
                        P.op("pe", "transpose", [r_xt, r_id], [ptR], out=ptT[:, j * 128:(j + 1) * 128],
                             in_=xt[:, j, c * 128:(c + 1) * 128], identity=ident[:])
                    yield
                    if c % 2 == 0:
                        P.op("dve", "tensor_scalar", [ptR, r_gv], [r_hT], out=hT[:, c, :], in0=ptT[:],
                             scalar1=gv[:, c:c + 1], scalar2=None, op0=ALU.mult)
                    else:
                        ACT([ptR, r_gv], [r_hT], out=hT[:, c, :], in_=ptT[:], func=AF.Copy, scale=gv[:, c:c + 1])
                    B.rel("bank", ptp)
                    yield

            ents = {0: load_x(0)}
            if NBLK > 1:
                ents[1] = load_x(1)
            tabs = {0: load_tab(0)}
            run_gens([xprep_gen(ents[0])], 1)
            for blk in range(NBLK):
                own = blk < NOWN
                t0 = blk * TB
                hT, r_hT = ents[blk]["hT"]
                tab, r_tab = tabs[blk]
                if blk + 1 < NBLK:
                    tabs[blk + 1] = load_tab(blk + 1)

                kvn, r_kvn = B.nxt("kvn")
                qln, r_qln = B.nxt("qln")

                def kvlat_gen():
                    bk = proj_group(hT, r_hT, OFF["bkv"], 128)
                    yield
                    sqp = square(bk, 128)
                    yield
                    res = []
                    yield from rstd_gen([sqp], 128, onesall, 128.0, res)
                    rs, r_rs = res[0]
                    P.op("dve", "scalar_tensor_tensor", [bk[1], r_rs, r_gv], [r_kvn], out=kvn[:], in0=bk[0][:],
                         scalar=gv[:, 12:13], in1=rs[:], op0=ALU.mult, op1=ALU.mult)
                    B.rel("ln", res[0])
                    B.rel("bank", bk)
                    yield

                def qlat_gen():
                    bq = [proj_group(hT, r_hT, OFF["bql"] + 128 * c, 128) for c in range(2)]
                    yield
                    sqs = [square(bq[0], 128), square(bq[1], 128)]
                    yield
                    res = []
                    yield from rstd_gen(sqs, 128, onesall, 256.0, res)
                    rs, r_rs = res[0]
                    for c in range(2):
                        P.op("dve", "scalar_tensor_tensor", [bq[c][1], r_rs, r_gv], [r_qln], out=qln[:, c, :],
                             in0=bq[c][0][:], scalar=gv[:, 10 + c:11 + c], in1=rs[:], op0=ALU.mult, op1=ALU.mult)
                    B.rel("ln", res[0])
                    B.rel("bank", bq[0])
                    B.rel("bank", bq[1])
                    yield

                def HN(g):
                    return (dict(bank=2, h=1), g)

                def pg(col0):
                    return lambda: proj_group(hT, r_hT, col0, 128)

                def two(dst, g):
                    return [(dst[(2 * g) * 64:(2 * g + 1) * 64, t0:t0 + TB], 0, 64),
                            (dst[(2 * g + 1) * 64:(2 * g + 2) * 64, t0:t0 + TB], 64, 128)]

                gens = []
                if own:
                    gens = [kvlat_gen(), qlat_gen()]
                run_gens(gens, 2)
                gens = []
                if blk + 1 < NBLK:
                    gens.append((dict(bank=1), xprep_gen(ents[blk + 1])))
                for g in range(2):
                    gens.append(HN(headnorm_gen(pg(OFF["ak"] + 128 * g), 128, 64.0, 9, ones2, None, tab, r_tab, two(KT_A, g), r_KT)))
                if own:
                    gens.append(HN(headnorm_gen(pg(OFF["ck"]), 128, 64.0, 16, ones2, "C", tab, r_tab,
                                                [(KTC_own.ap()[0:64, t0:t0 + TB], 0, 64),
                                                 (KTC_own.ap()[64:128, t0:t0 + TB], 64, 128)], r_KT)))
                if own:
                    for g in range(2):
                        gens.append(HN(headnorm_gen(pg(OFF["aq"] + 128 * g), 128, 64.0, 8, ones2, None, tab, r_tab, two(QT_A, g), r_QT)))
                    for g in range(3):
                        gens.append(HN(headnorm_gen(pg(OFF["cq"] + 128 * g), 128, 64.0, 15, ones2, "C", tab, r_tab, two(QT_C, g), r_QT)))
                    for (gcol, yrow, ng) in ((OFF["ag"], 0, 2), (OFF["bg"], 256, 3), (OFF["cg"], 640, 3)):
                        for g in range(ng):
                            gens.append((dict(bank=1, g=1), gate_gen(hT, r_hT, gcol + 128 * g, yrow + 128 * g, t0)))

                def v_gen():
                    vst, r_vst = B.nxt("vst")
                    for j in range(4):
                        pvp = B.acq("bank")
                        pvT, pvR = pvp
                        grp = ((OFF["av"], 256, 0), (OFF["cv"], 128, 256)) if own else ((OFF["av"], 256, 0),)
                        for (cols, n, o) in grp:
                            for c in range(8):
                                MM([r_hT, r_Wb], [pvR], out=pvT[:, o:o + n], lhsT=hT[:, c, j * 128:(j + 1) * 128],
                                   rhs=Wb[:, c, cols:cols + n], start=(c == 0), stop=(c == 7))
                        yield
                        w = 384 if own else 256
                        ACT([pvR], [r_vst], out=vst[:, j, 0:w], in_=pvT[:, 0:w], func=AF.Copy)
                        B.rel("bank", pvp)
                        if own:
                            pv2p = B.acq("bank")
                            pv2T, pv2R = pv2p
                            MM([r_kvn, r_Wkv], [pv2R], out=pv2T[:, 0:384], lhsT=kvn[:, j * 128:(j + 1) * 128],
                               rhs=Wkv[:], start=True, stop=True)
                            yield
                            P.op("dve", "tensor_copy", [pv2R], [r_vst], out=vst[:, j, 384:768], in_=pv2T[:, 0:384])
                            B.rel("bank", pv2p)
                        yield
                    P.dma("sp", V_v[blk], vst[:, :, 0:256], reads=[r_vst], writes=[r_V])
                    if own:
                        dstv = VBC_own[blk // 4].ap().rearrange("(n j p) c -> n p j c", j=4, p=128)[blk % 4]
                        P.dma("sp", dstv, vst[:, :, 256:768], reads=[r_vst], writes=[r_V])
                    yield
                gens.insert(1, (dict(bank=1), v_gen()))

                def bk_bank(h):
                    def f():
                        bkh = B.acq("bank")
                        MM([r_Wkk, r_kvn], [bkh[1]], out=bkh[0][0:64, :], lhsT=Wkk[:, h, :], rhs=kvn[:], start=True, stop=True)
                        proj_group(hT, r_hT, OFF["bkpe"], 32, out_rows=64, bk=bkh)
                        return bkh
                    return f

                def bq_bank(h):
                    def f():
                        bkh = B.acq("bank")
                        for c in range(2):
                            MM([r_Wqu, r_qln], [bkh[1]], out=bkh[0][0:96, :], lhsT=Wqu[:, c, h * 96:(h + 1) * 96],
                               rhs=qln[:, c, :], start=(c == 0), stop=(c == 1))
                        return bkh
                    return f

                for h in range(6 if own else 0):
                    gens.append(HN(headnorm_gen(bk_bank(h), 96, 96.0, 14, onesall[0:96, 0:96], "B", tab, r_tab,
                                             [(KTB_own[h].ap()[:, t0:t0 + TB], 0, 96)], r_KT)))
                    if own:
                        gens.append(HN(headnorm_gen(bq_bank(h), 96, 96.0, 13, onesall[0:96, 0:96], "B", tab, r_tab,
                                                 [(QT_B[h * 96:(h + 1) * 96, t0:t0 + TB], 0, 96)], r_QT)))
                if own:
                    gates = [g for g in gens if "g" in g[0]]
                    rest = [g for g in gens if "g" not in g[0]]
                    gens = []
                    while rest or gates:
                        gens.extend(rest[:2])
                        rest = rest[2:]
                        gens.extend(gates[:1])
                        gates = gates[1:]
                run_gens(gens, 6, dict(bank=8, h=4, g=4))
                if blk + 2 < NBLK:
                    ents[blk + 2] = load_x(blk + 2)
            P.barrier()
        for h in range(6):
            P.collective("AllGather", ALU.bypass, PAIRS, KTB_own[h].ap().opt(), KTB_all[h].ap().opt(),
                         reads=[r_KT], writes=[r_KVX])
        P.collective("AllGather", ALU.bypass, PAIRS, KTC_own.ap().opt(), KTC_all.ap().opt(),
                     reads=[r_KT], writes=[r_KVX])
        for k in range(2):
            P.collective("AllGather", ALU.bypass, PAIRS, VBC_own[k].ap().opt(), VBC_all[k].ap().opt(),
                         reads=[r_V], writes=[r_KVX])

        with ExitStack() as esA:
            B = Bufs(nc, P, esA, "A", lid)
            B.mk("S", [128, 512 * UC], F32, 2, psum=True)
            B.mk("O", [128, 512], F32, 2, psum=True)
            B.mk("kT", [128, S], BF16, 2)
            B.mk("vA", [128, 64, 128], BF16, 2)
            B.mk("qT", [128, HALF], BF16, 2)
            B.mk("Gt", [128, GW + HW], BF16, 2)
            B.mk("gt", [64, 512], F32, 2)
            B.mk("pT", [128, 512 * UC], BF16, 4)
            B.mk("rd", [128, 512], F32, 2)
            B.mk("rd0", [64, 512], F32, 2)
            B.mk("yf", [64, 512], F32, 2)
            B.mk("yb", [64, 512], BF16, 2)
            for k in range(2):
                vt, vr = B.get("vA", k)
                P.op("pool", "memset", [], [vr], ap=vt[:, :, 64:128], constant=1.0)

            jobs = []
            for h in range(4):
                jobs.append(dict(kv=("A", h), kt=KT_A[h * 64:(h + 1) * 64, :], dk=64, vcol=h * 64,
                                 q=QT_A[h * 64:(h + 1) * 64, :], scale=0.125, yrow=h * 64, isA=True, gi=h))
            for h in range(6):
                jobs.append(dict(kv=("B", h), ktx=KTB_all[h].ap(), ktr=(0, 96), dk=96, vcol=128 + 64 * h,
                                 q=QT_B[h * 96:(h + 1) * 96, :], scale=96.0 ** -0.5, yrow=256 + 64 * h, isA=False))
            for h in range(6):
                kvh = h // 3
                jobs.append(dict(kv=("C", kvh), ktx=KTC_all.ap(), ktr=(kvh * 64, 128 + kvh * 64), dk=64, vcol=64 * kvh,
                                 q=QT_C[h * 64:(h + 1) * 64, :], scale=0.125, yrow=640 + 64 * h, isA=False))

            V_cv = V_all.rearrange("(c p) d -> p c d", p=128)
            state = dict(kvkey=None, kvbuf=None)

            def load_job(job):
                if job["kv"] != state["kvkey"]:
                    kT, r_kT = B.nxt("kT")
                    vA, r_vA = B.nxt("vA")
                    dk = job["dk"]
                    vc = job["vcol"]
                    if job["isA"]:
                        for q4 in range(4):
                            P.dma("sp", kT[0:dk, q4 * 2048:(q4 + 1) * 2048], job["kt"][:, q4 * 2048:(q4 + 1) * 2048],
                                  reads=[r_KT], writes=[r_kT])
                        for q8 in range(8):
                            P.dma("sp", vA[:, q8 * 8:(q8 + 1) * 8, 0:64], V_cv[:, q8 * 8:(q8 + 1) * 8, vc:vc + 64],
                                  reads=[r_V], writes=[r_vA])
                    else:
                        for rk in range(2):
                            r0 = job["ktr"][rk] if rk == 0 else job["ktr"][1]
                            for q2 in range(2):
                                P.dma("sp", kT[0:dk, rk * HALF + q2 * 2048:rk * HALF + (q2 + 1) * 2048],
                                      job["ktx"][r0:r0 + dk, q2 * 2048:(q2 + 1) * 2048], reads=[r_KVX], writes=[r_kT])
                            for k in range(2):
                                vsrc = VBC_all[k].ap().rearrange("(c p) d -> p c d", p=128)
                                c0 = rk * 32 + k * 16
                                P.dma("sp", vA[:, c0:c0 + 16, 0:64], vsrc[:, rk * 16:(rk + 1) * 16, vc:vc + 64],
                                      reads=[r_KVX], writes=[r_vA])
                    state["kvkey"] = job["kv"]
                    state["kvbuf"] = (kT, r_kT, vA, r_vA)
                job["kvbuf"] = state["kvbuf"]
                qT, r_qT = B.nxt("qT")
                P.dma("sp", qT[0:job["dk"], :], job["q"], reads=[r_QT], writes=[r_qT])
                job["qbuf"] = (qT, r_qT)
                if job["isA"]:
                    Gt, r_Gt = B.nxt("Gt")
                    P.dma("sp", Gt[:], cst["gtab"][job["gi"]], reads=[r_w], writes=[r_Gt])
                    job["gbuf"] = (Gt, r_Gt)

            units = []
            for ji, job in enumerate(jobs):
                cnt = 0
                for qb in range(NOWN):
                    q0 = qb * TB
                    if job["isA"]:
                        ch = []
                        for cc in range(max(0, q0 - 1024) // 128, min(HALF, q0 + TB + 1024) // 128):
                            c = (cc * 128 - q0) // 128
                            ch.append((cc, GOFF - 128 * c))
                        for k in range(8):
                            m = 639 + 128 * k
                            s0 = S - 1 - q0 - m
                            if 0 <= s0 < HALF:
                                assert s0 % 128 == 0
                                ch.append(((HALF + s0) // 128, GW + (1535 - m)))
                    else:
                        ch = [(cc, None) for cc in range(S // 128)]
                    for i in range(0, len(ch), UC):
                        units.append(dict(ji=ji, qb=qb, ch=ch[i:i + UC], first=(i == 0),
                                          last=(i + UC >= len(ch)), idx=cnt))
                        cnt += 1

            def stage1(u):
                job = jobs[u["ji"]]
                if u["idx"] == 0 and u["ji"] == 0:
                    load_job(jobs[0])
                if u["idx"] == 3 and u["ji"] + 1 < len(jobs):
                    load_job(jobs[u["ji"] + 1])
                q0 = u["qb"] * TB
                if u["first"]:
                    gt, r_gt = B.nxt("gt")
                    P.dma("sp", gt[:], GATE[job["yrow"]:job["yrow"] + 64, q0:q0 + TB], reads=[r_GATE], writes=[r_gt])
                    job["cur"] = ((gt, r_gt), B.nxt("O"))
                u["gt"], u["O"] = job["cur"]
                kT, r_kT, vA, r_vA = job["kvbuf"]
                qT, r_qT = job["qbuf"]
                dk = job["dk"]
                sT, r_S = B.nxt("S")
                u["S"] = (sT, r_S)
                for k, (cc, _) in enumerate(u["ch"]):
                    MM([r_kT, r_qT], [r_S], out=sT[:, k * 512:(k + 1) * 512], lhsT=kT[0:dk, cc * 128:(cc + 1) * 128],
                       rhs=qT[0:dk, q0:q0 + TB], start=True, stop=True)

            def stage2(u):
                job = jobs[u["ji"]]
                sT, r_S = u["S"]
                pT, r_pT = B.nxt("pT")
                r_pTc = P.res(r_pT.name + "_c")
                u["pT"] = (pT, r_pT, r_pTc)
                w = 512 * len(u["ch"])
                ACT([r_S], [r_pT, r_pTc], out=pT[:, 0:w], in_=sT[:, 0:w], func=AF.Exp, scale=job["scale"])
                if job["isA"]:
                    Gt, r_Gt = job["gbuf"]
                    for k, (cc, u0) in enumerate(u["ch"]):
                        eng, rr = ("dve", r_pTc) if k == 2 else ("dve", r_pT)
                        P.op(eng, "tensor_tensor", [rr, r_Gt], [rr], out=pT[:, k * 512:(k + 1) * 512],
                             in0=pT[:, k * 512:(k + 1) * 512], in1=Gt[:, u0:u0 + 512], op=ALU.mult)

            def stage3(u):
                job = jobs[u["ji"]]
                kT, r_kT, vA, r_vA = job["kvbuf"]
                pT, r_pT, r_pTc = u["pT"]
                oT, r_O = u["O"]
                nch = len(u["ch"])
                for k, (cc, _) in enumerate(u["ch"]):
                    MM([r_vA, (r_pTc if k == 2 else r_pT)], [r_O], out=oT[:], lhsT=vA[:, cc, :], rhs=pT[:, k * 512:(k + 1) * 512],
                       start=(u["first"] and k == 0), stop=(u["last"] and k == nch - 1))
                if u["last"]:
                    gt, r_gt = u["gt"]
                    rd, r_rd = B.nxt("rd")
                    P.op("dve", "reciprocal", [r_O], [r_rd], out=rd[64:128, :], in_=oT[64:128, :])
                    rd0, r_rd0 = B.nxt("rd0")
                    P.op("dve", "tensor_copy", [r_rd], [r_rd0], out=rd0[:], in_=rd[64:128, :])
                    yf, r_yf = B.nxt("yf")
                    P.op("dve", "tensor_tensor", [r_O, r_rd0], [r_yf], out=yf[:], in0=oT[0:64, :], in1=rd0[:], op=ALU.mult)
                    yb, r_yb = B.nxt("yb")
                    P.op("dve", "tensor_tensor", [r_yf, r_gt], [r_yb], out=yb[:], in0=yf[:], in1=gt[:], op=ALU.mult)
                    q0 = u["qb"] * TB
                    P.dma("sp", YT[job["yrow"]:job["yrow"] + 64, q0:q0 + TB], yb[:], reads=[r_yb], writes=[r_YT])

            n = len(units)
            for i in range(n + 3):
                if i < n:
                    stage1(units[i])
                if 0 <= i - 1 < n:
                    stage2(units[i - 1])
                if 0 <= i - 3 < n:
                    stage3(units[i - 3])
            P.barrier()

        with ExitStack() as esO:
            B = Bufs(nc, P, esO, "O", lid)
            B.mk("bank", [128, 512], F32, 8, psum=True)
            Wo, r_Wo = B.mk("Wo", [128, 8, D], BF16)
            B.mk("wost", [128, D], F32, 2)
            B.mk("yT", [128, 8, TB], BF16, 2)
            B.mk("xo", [128, 4, D], F32, 2)
            wo_v = w_out.rearrange("(c p) n -> p c n", p=128)
            for c in range(8):
                st, r_st = B.nxt("wost")
                P.dma("sp", st[:], wo_v[:, c, :], reads=[r_w], writes=[r_st])
                if c % 2 == 0:
                    P.op("dve", "tensor_copy", [r_st], [r_Wo], out=Wo[:, c, :], in_=st[:])
                else:
                    ACT([r_st], [r_Wo], out=Wo[:, c, :], in_=st[:], func=AF.Copy)
            wgen = next_w(esO) if next_w is not None else iter(())
            YT_v = YT.rearrange("(c p) t -> p c t", p=128)
            for blk in range(NOWN):
                q0 = blk * TB
                yT, r_yT = B.nxt("yT")
                P.dma("sp", yT[:], YT_v[:, :, q0:q0 + TB], reads=[r_YT], writes=[r_yT])
                xo, r_xo = B.nxt("xo")
                P.dma("sp", xo[:], src["own_blk"](blk), reads=[src["r_own"]], writes=[r_xo])
                for j in range(4):
                    for hh in range(2):
                        bT, bR = B.nxt("bank")
                        for c in range(8):
                            MM([r_yT, r_Wo], [bR], out=bT[:], lhsT=yT[:, c, j * 128:(j + 1) * 128],
                               rhs=Wo[:, c, hh * 512:(hh + 1) * 512], start=(c == 0), stop=(c == 7))
                        P.op("dve", "tensor_tensor", [bR, r_xo], [r_xo], out=xo[:, j, hh * 512:(hh + 1) * 512],
                             in0=bT[:], in1=xo[:, j, hh * 512:(hh + 1) * 512], op=ALU.add)
                P.dma("sp", dst["blk"](blk), xo[:], reads=[r_xo], writes=[dst["res"]])
                if dst["after"] is not None:
                    dst["after"](blk)
                next(wgen, None)
            for _ in wgen:
                pass
            P.barrier()


PAIRS = [[0, 1], [2, 3], [4, 5], [6, 7]]


def build_program(nl=DEPTH, dbg=False):
    nc = bass.Bass("TRN2", target_bir_lowering=False)
    x_own = nc.dram_tensor("x_own", [HALF, D], F32, kind="ExternalInput").ap()
    x_oth = nc.dram_tensor("x_oth", [HALF, D], F32, kind="ExternalInput").ap()
    xo = nc.dram_tensor("xo", [HALF, D], F32, kind="ExternalOutput").ap()
    wts = []
    for l in range(nl):
        wts.append(dict(
            w_in=nc.dram_tensor(f"w_in{l}", [D, IN_COLS], F32, kind="ExternalInput").ap(),
            w_out=nc.dram_tensor(f"w_out{l}", [D, D], F32, kind="ExternalInput").ap(),
            wq_up=nc.dram_tensor(f"wq_up{l}", [256, 576], F32, kind="ExternalInput").ap(),
            wkv_up=nc.dram_tensor(f"wkv_up{l}", [128, 768], F32, kind="ExternalInput").ap(),
            gv=nc.dram_tensor(f"gv{l}", [128, NGV], F32, kind="ExternalInput").ap(),
        ))
    cst = dict(
        cm=nc.dram_tensor("cm", [128, 4, 128], BF16, kind="ExternalInput").ap(),
        ident=nc.dram_tensor("ident", [128, 128], F32, kind="ExternalInput").ap(),
        gtab=nc.dram_tensor("gtab", [4, 128, GW + HW], BF16, kind="ExternalInput").ap(),
        msk=nc.dram_tensor("msk", [128, 2], F32, kind="ExternalInput").ap(),
    )
    for nm in ("cosB", "sinB", "cosC", "sinC"):
        cst[nm] = nc.dram_tensor(nm, [128, S], F32, kind="ExternalInput").ap()
    kind = "ExternalOutput" if dbg else "Internal"
    scr = dict(
        KT_A=nc.dram_tensor("KT_A", [256, S], BF16, kind=kind).ap(),
        KTB_own=[nc.dram_tensor(f"KTB_own{h}", [96, HALF], BF16) for h in range(6)],
        KTB_all=[nc.dram_tensor(f"KTB_all{h}", [192, HALF], BF16) for h in range(6)],
        KTC_own=nc.dram_tensor("KTC_own", [128, HALF], BF16),
        KTC_all=nc.dram_tensor("KTC_all", [256, HALF], BF16),
        VBC_own=[nc.dram_tensor(f"VBC_own{k}", [HALF // 2, 512], BF16) for k in range(2)],
        VBC_all=[nc.dram_tensor(f"VBC_all{k}", [HALF, 512], BF16) for k in range(2)],
        QT_A=nc.dram_tensor("QT_A", [256, HALF], BF16, kind=kind).ap(),
        QT_B=nc.dram_tensor("QT_B", [576, HALF], BF16, kind=kind).ap(),
        QT_C=nc.dram_tensor("QT_C", [384, HALF], BF16, kind=kind).ap(),
        V=nc.dram_tensor("V_A", [S, 256], BF16, kind=kind).ap(),
        GATE=nc.dram_tensor("GATE", [D, HALF], F32, kind=kind).ap(),
        YT=nc.dram_tensor("YT", [D, HALF], BF16, kind=kind).ap(),
    )
    ownc = [[nc.dram_tensor(f"ownc{i}_{j}", [TB, D], F32) for j in range(NOWN)] for i in range(2)]
    gathc = [[nc.dram_tensor(f"gathc{i}_{j}", [2 * TB, D], F32) for j in range(NOWN)] for i in range(2)]

    def blkview(ap):
        return ap.rearrange("(j p) d -> p j d", p=128)

    x_own_v = x_own.rearrange("(n j p) d -> n p j d", j=4, p=128)
    x_oth_v = x_oth.rearrange("(n j p) d -> n p j d", j=4, p=128)
    xo_v = xo.rearrange("(n j p) d -> n p j d", j=4, p=128)
    with ExitStack() as es:
        P = Prog(nc, es)
        pw = alloc_persist(nc, P, es, cst)
        r_in = P.res("x_in")
        r_ownc = [P.res(f"ownc{i}") for i in range(2)]
        r_gathc = [P.res(f"gathc{i}") for i in range(2)]
        for l in range(nl):
            if l > 0:
                P.new_epoch()
            if l == 0:
                src = dict(own_blk=lambda b: x_own_v[b], r_own=r_in, oth_blk=lambda j: x_oth_v[j], r_oth=r_in, gath=None)
            else:
                k = (l - 1) % 2
                src = dict(own_blk=lambda b, k=k: blkview(ownc[k][b].ap()), r_own=r_ownc[k], gath=True,
                           ga_blk=lambda j, k=k: blkview(gathc[k][j].ap()[0:TB, :]),
                           gb_blk=lambda j, k=k: blkview(gathc[k][j].ap()[TB:2 * TB, :]), r_oth=r_gathc[k])
            last = (l == nl - 1)
            if last:
                dst = dict(blk=lambda b: xo_v[b], res=P.res("x_out"), after=None)
            else:
                k = l % 2

                def after(b, k=k):
                    P.collective("AllGather", ALU.bypass, PAIRS, ownc[k][b].ap().opt(), gathc[k][b].ap().opt(),
                                 reads=[r_ownc[k]], writes=[r_gathc[k]])
                dst = dict(blk=lambda b, k=k: blkview(ownc[k][b].ap()), res=r_ownc[k], after=after)
            next_w = None
            if not last:
                next_w = (lambda es_, l=l: emit_W(nc, P, l + 1, wts[l + 1], pw, es_))
            emit_layer(nc, P, l, src, dst, wts[l], cst, scr, pw, (l == 0), next_w)
        P.emit_all()
        nsem = P.nsem
    return nc, nsem


def _rope_tables():
    f32 = np.float32
    freqs = np.power(f32(10000.0), (f32(-2.0) * np.arange(16, dtype=f32) / f32(32.0))).astype(f32)
    t = np.arange(S)
    row = (t // 64).astype(f32)
    col = (t % 64).astype(f32)
    tt = t.astype(f32)
    cosB = np.zeros((128, S), f32)
    sinB = np.zeros((128, S), f32)
    cosB[0:64] = 1.0
    for p in range(64, 96):
        sub = p - 64
        i = sub % 16
        ang = (tt * freqs[i]).astype(f32)
        cosB[p] = np.cos(ang)
        sinB[p] = (-np.sin(ang)) if sub < 16 else np.sin(ang)
    cosC = np.zeros((128, S), f32)
    sinC = np.zeros((128, S), f32)
    for p in range(128):
        dd = p % 64
        pos = row if dd < 32 else col
        sub = dd % 32
        i = sub % 16
        ang = (pos * freqs[i]).astype(f32)
        cosC[p] = np.cos(ang)
        sinC[p] = (-np.sin(ang)) if sub < 16 else np.sin(ang)
    return cosB, sinB, cosC, sinC


def _const_mats():
    permB = np.zeros((128, 128), np.float32)
    for p in range(64, 96):
        sub = p - 64
        partner = p + 16 if sub < 16 else p - 16
        permB[partner, p] = 1.0
    permC = np.zeros((128, 128), np.float32)
    for p in range(128):
        sub = (p % 64) % 32
        partner = p + 16 if sub < 16 else p - 16
        permC[partner, p] = 1.0
    ones2 = np.zeros((128, 128), np.float32)
    ones2[0:64, 0:64] = 1.0
    ones2[64:128, 64:128] = 1.0
    onesall = np.ones((128, 128), np.float32)
    cm = np.stack([permB, permC, ones2, onesall], axis=1)
    return np.ascontiguousarray(cm).astype(ml_dtypes.bfloat16)


def _gtab():
    lim = 8192
    delta = np.arange(-lim, lim + 1)
    mult = np.zeros(delta.shape, np.float64)
    for d in (1, 4, 16):
        mult += ((delta % d) == 0) & (np.abs(delta) // d <= 64)
    out = np.zeros((4, 128, GW + HW), np.float32)
    kk = np.arange(128)[:, None]
    u = np.arange(GW)[None, :]
    dT = kk - u + GOFF
    uh = np.arange(HW)[None, :]
    dH = 1535 - kk - uh
    for h in range(4):
        slope = 2.0 ** (-8.0 * (h + 1) / 4.0)
        for (dl, c0, c1) in ((dT, 0, GW), (dH, GW, GW + HW)):
            m = mult[dl + lim]
            out[h][:, c0:c1] = (m * np.exp(-slope * np.abs(dl))).astype(np.float32)
    return out.astype(ml_dtypes.bfloat16)


def _gains(l, p):
    gvec = np.zeros((128, NGV), np.float32)
    gvec[:, 0:8] = p["norm_g"][l].reshape(8, 128).T
    gvec[:, 8] = np.tile(p["a_q_norm_g"][l], 2)
    gvec[:, 9] = np.tile(p["a_k_norm_g"][l], 2)
    gvec[:, 10:12] = p["b_q_lat_norm_g"][l].reshape(2, 128).T
    gvec[:, 12] = p["b_kv_lat_norm_g"][l]
    gvec[0:96, 13] = p["b_q_norm_g"][l]
    gvec[0:96, 14] = p["b_k_norm_g"][l]
    gvec[:, 15] = np.tile(p["c_q_norm_g"][l], 2)
    gvec[:, 16] = np.tile(p["c_k_norm_g"][l], 2)
    return gvec


def _core_consts():
    cosB, sinB, cosC, sinC = _rope_tables()
    out = {}
    asc = np.arange(HALF)
    desc = S - 1 - np.arange(HALF)
    for hf in range(2):
        tok = np.concatenate([asc, desc]) if hf == 0 else np.concatenate([desc, asc])
        msk = np.zeros((128, 2), np.float32)
        msk[:, 1 - hf] = 1.0
        out[hf] = dict(cosB=np.ascontiguousarray(cosB[:, tok]), sinB=np.ascontiguousarray(sinB[:, tok]),
                       cosC=np.ascontiguousarray(cosC[:, tok]), sinC=np.ascontiguousarray(sinC[:, tok]), msk=msk)
    return out


_CACHE = {}


def make_in_maps(p, nl=DEPTH, cores=range(8)):
    if "cc" not in _CACHE:
        _CACHE["cc"] = _core_consts()
        _CACHE["cm"] = _const_mats()
        _CACHE["ident"] = np.eye(128, dtype=np.float32)
        _CACHE["gtab"] = _gtab()
    x = np.ascontiguousarray(p["x"], dtype=np.float32)
    in_maps = []
    for c in cores:
        b, hf = c // 2, c % 2
        lo = np.ascontiguousarray(x[b, 0:HALF])
        hi = np.ascontiguousarray(x[b, HALF:S][::-1])
        m = dict(x_own=(lo if hf == 0 else hi), x_oth=(hi if hf == 0 else lo),
                 cm=_CACHE["cm"], ident=_CACHE["ident"], gtab=_CACHE["gtab"])
        m.update(_CACHE["cc"][hf])
        for l in range(nl):
            m[f"w_in{l}"] = np.ascontiguousarray(p["w_in"][l], dtype=np.float32)
            m[f"w_out{l}"] = np.ascontiguousarray(p["w_out"][l], dtype=np.float32)
            m[f"wq_up{l}"] = np.ascontiguousarray(p["w_b_q_up"][l], dtype=np.float32)
            m[f"wkv_up{l}"] = np.ascontiguousarray(p["w_b_kv_up"][l], dtype=np.float32)
            m[f"gv{l}"] = _gains(l, p)
        in_maps.append(m)
    return in_maps


def kernel(**inputs):
    p = {k: np.asarray(v) for k, v in inputs.items()}
    if "nc" not in _CACHE:
        _CACHE["nc"] = build_program(DEPTH)[0]
    in_maps = make_in_maps(p, DEPTH)
    res = run_bass_kernel_spmd(_CACHE["nc"], in_maps, core_ids=list(range(8)))
    out = np.empty((4, S, D), np.float32)
    for c in range(8):
        b, hf = c // 2, c % 2
        o = res.results[c]["xo"]
        if hf == 0:
            out[b, 0:HALF] = o
        else:
            out[b, HALF:S] = o[::-1]
    return out
```
